# Optimizing a Trainium2 kernel written in Bass

```python
import math
import jax
import jax.numpy as jnp
from jax import lax
import numpy as np

D_MODEL = 1024
BATCH = 8
SEQ = 2048
DEPTH = 1

PLE_DIM = 256
NSA_HEADS = 8
NSA_GROUPS = 2
NSA_HPG = NSA_HEADS // NSA_GROUPS
NSA_HEAD_DIM = 64
NSA_WIDTH = NSA_HEADS * NSA_HEAD_DIM
NSA_KV = NSA_GROUPS * NSA_HEAD_DIM
CMP_BLOCK = 32
CMP_STRIDE = 16
CMP_HIDDEN = 256
SEL_BLOCK = 64
SEL_TOPK = 8
WINDOW = 512
Q_BLOCK = 128
DN_HEADS = 4
DN_HEAD_DIM = 128
DN_WIDTH = DN_HEADS * DN_HEAD_DIM
DN_CONV = 4
DN_CHUNK = 64
NUM_BUCKETS = 32
REL_MAX_DIST = 1024
DEEPNORM_ALPHA = (2 * DEPTH) ** 0.25
DEEPNORM_BETA = (8 * DEPTH) ** -0.25
NEG = -1e30
FORCE = 1e6
IN_SPLITS = (NSA_WIDTH, 6 * NSA_KV, 3 * NSA_HEADS, NSA_WIDTH,
             3 * DN_WIDTH, DN_HEADS, DN_HEADS, DN_WIDTH, 2 * D_MODEL)
D_IN = sum(IN_SPLITS)

kernel_name = 'hybrid_nsa_gdn_deepnorm_layer'


def rel_bucket(dist):
    n = jnp.maximum(dist, 0)
    max_exact = NUM_BUCKETS // 2
    nf = jnp.maximum(n, 1).astype(jnp.float32)
    large = max_exact + (jnp.log(nf / max_exact) / math.log(REL_MAX_DIST / max_exact)
                         * (NUM_BUCKETS - max_exact)).astype(jnp.int32)
    large = jnp.minimum(large, NUM_BUCKETS - 1)
    return jnp.where(n < max_exact, n, large)


def masked_softmax(s, valid):
    s = jnp.where(valid, s.astype(jnp.float32), NEG)
    return jnp.where(valid, jax.nn.softmax(s, axis=-1), 0.0)


def layer_norm(h, g, b, eps=1e-5):
    h = h.astype(jnp.float32)
    mu = jnp.mean(h, -1, keepdims=True)
    var = jnp.mean(jnp.square(h - mu), -1, keepdims=True)
    return (h - mu) * lax.rsqrt(var + eps) * g.astype(jnp.float32) + b.astype(jnp.float32)


def l2norm(t):
    return t * lax.rsqrt(jnp.sum(jnp.square(t), -1, keepdims=True) + 1e-6)


def compress_blocks(t, pos, w1, w2):
    B, S, G, dh = t.shape
    r = CMP_BLOCK // CMP_STRIDE
    c = t.reshape(B, S // CMP_STRIDE, CMP_STRIDE, G, dh)
    n_cmp = S // CMP_STRIDE - r + 1
    blocks = jnp.concatenate([c[:, j:j + n_cmp] for j in range(r)], axis=2)
    blocks = blocks + pos[None, None, :, None, :]
    flat = blocks.transpose(0, 1, 3, 2, 4).reshape(B, n_cmp, G, CMP_BLOCK * dh)
    return jax.nn.gelu(flat @ w1) @ w2


def nsa_mixer(q, kv, gates, pos_k, pos_v, w1_k, w2_k, w1_v, w2_v, rel_bias):
    B, S, _ = q.shape
    G, Hg, dh = NSA_GROUPS, NSA_HPG, NSA_HEAD_DIM
    q = q.astype(jnp.float32).reshape(B, S, G, Hg, dh) * dh ** -0.5
    k_c, v_c, k_s, v_s, k_w, v_w = [t.reshape(B, S, G, dh) for t in jnp.split(kv, 6, axis=-1)]
    tab = rel_bias.astype(jnp.float32)
    tab_g = tab.reshape(NUM_BUCKETS, G, Hg)
    t_pos = jnp.arange(S)

    kc = compress_blocks(k_c, pos_k, w1_k, w2_k)
    vc = compress_blocks(v_c, pos_v, w1_v, w2_v)
    n_cmp = kc.shape[1]
    cmp_end = jnp.arange(n_cmp) * CMP_STRIDE + CMP_BLOCK - 1
    dist_c = t_pos[:, None] - cmp_end[None, :]
    bias_c = tab[rel_bucket(dist_c)].reshape(S, n_cmp, G, Hg).transpose(2, 3, 0, 1)
    p_c = masked_softmax(jnp.einsum('bsghd,bngd->bghsn', q, kc) + bias_c, dist_c >= 0)
    o_cmp = jnp.einsum('bghsn,bngd->bsghd', p_c, vc)

    n_sel = S // SEL_BLOCK
    ci = jnp.arange(n_cmp)[:, None]
    sj = jnp.arange(n_sel)[None, :]
    overlap = ((ci * CMP_STRIDE < (sj + 1) * SEL_BLOCK) &
               (ci * CMP_STRIDE + CMP_BLOCK > sj * SEL_BLOCK)).astype(jnp.float32)
    imp = jnp.einsum('bghsn,nj->bgsj', p_c, overlap)
    cur = (t_pos // SEL_BLOCK)[:, None]
    blk = jnp.arange(n_sel)[None, :]
    forced = (blk == 0) | (blk == cur) | (blk == cur - 1)
    imp = jnp.where(forced, FORCE, jnp.where(blk > cur, -FORCE, imp))
    n_top = min(SEL_TOPK, n_sel)
    _, sel_idx = lax.top_k(imp, n_top)

    kb_s = k_s.reshape(B, n_sel, SEL_BLOCK, G, dh).transpose(0, 3, 1, 2, 4)
    vb_s = v_s.reshape(B, n_sel, SEL_BLOCK, G, dh).transpose(0, 3, 1, 2, 4)
    pad = ((0, 0), (WINDOW, 0), (0, 0), (0, 0))
    kw_p = jnp.pad(k_w, pad)
    vw_p = jnp.pad(v_w, pad)
    bi = jnp.arange(B)[:, None, None, None]
    gi = jnp.arange(G)[None, :, None, None]
    gi5 = jnp.arange(G)[None, :, None, None, None]
    sel_off = jnp.arange(SEL_BLOCK)
    win_off = jnp.arange(WINDOW + Q_BLOCK)

    def query_block(i):
        q0 = i * Q_BLOCK
        tq = q0 + jnp.arange(Q_BLOCK)
        qb = lax.dynamic_slice_in_dim(q, q0, Q_BLOCK, axis=1)
        idx = lax.dynamic_slice_in_dim(sel_idx, q0, Q_BLOCK, axis=2)
        ks = kb_s[bi, gi, idx]
        vs = vb_s[bi, gi, idx]
        kpos = idx[..., None] * SEL_BLOCK + sel_off
        dist = tq[None, None, :, None, None] - kpos
        bias_s = jnp.moveaxis(tab_g[rel_bucket(dist), gi5], -1, 2)
        s_s = jnp.einsum('bqghd,bgqnld->bghqnl', qb, ks) + bias_s
        s_s = s_s.reshape(B, G, Hg, Q_BLOCK, n_top * SEL_BLOCK)
        valid_s = (dist >= 0).reshape(B, G, 1, Q_BLOCK, n_top * SEL_BLOCK)
        p_s = masked_softmax(s_s, valid_s).reshape(B, G, Hg, Q_BLOCK, n_top, SEL_BLOCK)
        o_s = jnp.einsum('bghqnl,bgqnld->bqghd', p_s, vs)
        kw = lax.dynamic_slice_in_dim(kw_p, q0, WINDOW + Q_BLOCK, axis=1)
        vw = lax.dynamic_slice_in_dim(vw_p, q0, WINDOW + Q_BLOCK, axis=1)
        kposw = q0 - WINDOW + win_off
        distw = tq[:, None] - kposw[None, :]
        validw = (distw >= 0) & (distw < WINDOW) & (kposw[None, :] >= 0)
        bias_w = tab[rel_bucket(distw)].reshape(Q_BLOCK, WINDOW + Q_BLOCK, G, Hg).transpose(2, 3, 0, 1)
        p_w = masked_softmax(jnp.einsum('bqghd,bkgd->bghqk', qb, kw) + bias_w, validw)
        o_w = jnp.einsum('bghqk,bkgd->bqghd', p_w, vw)
        return o_s, o_w

    o_sel, o_win = lax.map(query_block, jnp.arange(S // Q_BLOCK))
    o_sel = jnp.moveaxis(o_sel, 0, 1).reshape(B, S, G, Hg, dh)
    o_win = jnp.moveaxis(o_win, 0, 1).reshape(B, S, G, Hg, dh)

    g = jax.nn.sigmoid(gates.astype(jnp.float32)).reshape(B, S, 3, G, Hg, 1)
    o = g[:, :, 0] * o_cmp + g[:, :, 1] * o_sel + g[:, :, 2] * o_win
    return o.reshape(B, S, NSA_WIDTH)


def chunk_gated_delta_rule(q, k, v, g, beta):
    B, S, H, dk = q.shape
    dv = v.shape[-1]
    C = DN_CHUNK
    N = S // C

    def chunks(t):
        return jnp.moveaxis(t.reshape((B, N, C, H) + t.shape[3:]), 3, 1)

    qc = chunks(q * dk ** -0.5)
    kc = chunks(k)
    vc = chunks(v)
    bc = chunks(beta)
    gc = jnp.cumsum(chunks(g), axis=-1)
    pos = jnp.arange(C)
    incl = pos[:, None] >= pos[None, :]
    strict = pos[:, None] > pos[None, :]
    diff = gc[..., :, None] - gc[..., None, :]
    decay = jnp.where(incl, jnp.exp(jnp.where(incl, diff, 0.0)), 0.0)
    kb = kc * bc[..., None]
    lower = jnp.where(strict, jnp.einsum('bhncd,bhnjd->bhncj', kb, kc) * decay, 0.0)
    rhs = jnp.concatenate([vc * bc[..., None], kb * jnp.exp(gc)[..., None]], axis=-1)
    sol = lax.linalg.triangular_solve(lower, rhs, left_side=True, lower=True, unit_diagonal=True)
    u, w = sol[..., :dv], sol[..., dv:]
    a_qk = jnp.einsum('bhncd,bhnjd->bhncj', qc, kc) * decay

    def step(state, xs):
        q_i, k_i, u_i, w_i, a_i, g_i = xs
        v_new = u_i - jnp.einsum('bhck,bhkv->bhcv', w_i, state)
        o_i = (jnp.einsum('bhck,bhkv->bhcv', q_i * jnp.exp(g_i)[..., None], state)
               + jnp.einsum('bhcj,bhjv->bhcv', a_i, v_new))
        g_last = g_i[..., -1]
        k_dec = k_i * jnp.exp(g_last[..., None] - g_i)[..., None]
        state = state * jnp.exp(g_last)[..., None, None] + jnp.einsum('bhck,bhcv->bhkv', k_dec, v_new)
        return state, o_i

    xs = tuple(jnp.moveaxis(t, 2, 0) for t in (qc, kc, u, w, a_qk, gc))
    state0 = jnp.zeros((B, H, dk, dv), jnp.float32)
    _, o = lax.scan(step, state0, xs)
    return o.transpose(1, 0, 3, 2, 4).reshape(B, S, H, dv)


def deltanet_mixer(qkv, beta_in, a_in, z, conv_w, a_log, dt_bias, norm_w):
    B, S, C = qkv.shape
    H, d = DN_HEADS, DN_HEAD_DIM
    qkv = lax.conv_general_dilated(qkv, conv_w.astype(qkv.dtype)[:, None, :], window_strides=(1,),
                                   padding=((DN_CONV - 1, 0),),
                                   dimension_numbers=('NWC', 'WIO', 'NWC'), feature_group_count=C)
    qkv = jax.nn.silu(qkv.astype(jnp.float32))
    q, k, v = [t.reshape(B, S, H, d) for t in jnp.split(qkv, 3, axis=-1)]
    q, k = l2norm(q), l2norm(k)
    beta = jax.nn.sigmoid(beta_in.astype(jnp.float32))
    g = -jnp.exp(a_log.astype(jnp.float32)) * jax.nn.softplus(a_in.astype(jnp.float32) + dt_bias.astype(jnp.float32))
    o = chunk_gated_delta_rule(q, k, v, g, beta)
    o = o * lax.rsqrt(jnp.mean(jnp.square(o), -1, keepdims=True) + 1e-6) * norm_w.astype(jnp.float32)
    o = o * jax.nn.silu(z.astype(jnp.float32)).reshape(B, S, H, d)
    return o.reshape(B, S, DN_WIDTH)


def setup_inputs(seed: int = 0) -> dict:
    key = jax.random.key(seed)
    ks = jax.random.split(key, 24)
    L = DEPTH

    def nrm(k, shape, scale):
        return jax.random.normal(k, shape, jnp.float32) * scale

    dt = jnp.exp(jax.random.uniform(ks[12], (L, DN_HEADS), jnp.float32, math.log(1e-3), math.log(1e-1)))
    return {
        'x': nrm(ks[0], (BATCH, SEQ, D_MODEL), 1.0),
        'p': nrm(ks[1], (L, BATCH, SEQ, PLE_DIM), 1.0),
        'w_in': nrm(ks[2], (L, D_MODEL, D_IN), D_MODEL ** -0.5),
        'cmp_pos_k': nrm(ks[3], (L, CMP_BLOCK, NSA_HEAD_DIM), 0.1),
        'cmp_pos_v': nrm(ks[4], (L, CMP_BLOCK, NSA_HEAD_DIM), 0.1),
        'cmp_w1_k': nrm(ks[5], (L, CMP_BLOCK * NSA_HEAD_DIM, CMP_HIDDEN), (CMP_BLOCK * NSA_HEAD_DIM) ** -0.5),
        'cmp_w2_k': nrm(ks[6], (L, CMP_HIDDEN, NSA_HEAD_DIM), CMP_HIDDEN ** -0.5),
        'cmp_w1_v': nrm(ks[7], (L, CMP_BLOCK * NSA_HEAD_DIM, CMP_HIDDEN), (CMP_BLOCK * NSA_HEAD_DIM) ** -0.5),
        'cmp_w2_v': nrm(ks[8], (L, CMP_HIDDEN, NSA_HEAD_DIM), CMP_HIDDEN ** -0.5),
        'rel_bias': nrm(ks[9], (NUM_BUCKETS, NSA_HEADS), 0.2),
        'dn_conv_w': nrm(ks[10], (L, DN_CONV, 3 * DN_WIDTH), DN_CONV ** -0.5),
        'dn_a_log': jnp.log(jax.random.uniform(ks[11], (L, DN_HEADS), jnp.float32, 1.0, 16.0)),
        'dn_dt_bias': dt + jnp.log(-jnp.expm1(-dt)),
        'dn_norm_w': 1.0 + nrm(ks[13], (L, DN_HEAD_DIM), 0.05),
        'w_branch_a': nrm(ks[14], (L, NSA_WIDTH, D_MODEL), NSA_WIDTH ** -0.5),
        'w_branch_b': nrm(ks[15], (L, DN_WIDTH, D_MODEL), DN_WIDTH ** -0.5),
        'w_out': nrm(ks[16], (L, D_MODEL, D_MODEL), DEEPNORM_BETA * D_MODEL ** -0.5),
        'w_ple': nrm(ks[17], (L, PLE_DIM, D_MODEL), PLE_DIM ** -0.5),
        'w_ple_gate': nrm(ks[18], (L, D_MODEL, D_MODEL), D_MODEL ** -0.5),
        'ln_g': 1.0 + nrm(ks[19], (L, D_MODEL), 0.05),
        'ln_b': nrm(ks[20], (L, D_MODEL), 0.02),
    }


def reference(x, p, w_in, cmp_pos_k, cmp_pos_v, cmp_w1_k, cmp_w2_k, cmp_w1_v, cmp_w2_v, rel_bias,
              dn_conv_w, dn_a_log, dn_dt_bias, dn_norm_w, w_branch_a, w_branch_b, w_out,
              w_ple, w_ple_gate, ln_g, ln_b):
    out_dtype = x.dtype
    cuts = []
    acc = 0
    for s in IN_SPLITS[:-1]:
        acc += s
        cuts.append(acc)
    for i in range(DEPTH):
        proj = x @ w_in[i]
        q_a, kv_a, g_a, z_a, qkv_b, beta_b, a_b, z_b, g_merge = jnp.split(proj, cuts, axis=-1)
        o_a = nsa_mixer(q_a, kv_a, g_a, cmp_pos_k[i], cmp_pos_v[i], cmp_w1_k[i], cmp_w2_k[i],
                        cmp_w1_v[i], cmp_w2_v[i], rel_bias)
        o_a = o_a * jax.nn.silu(z_a.astype(jnp.float32))
        o_b = deltanet_mixer(qkv_b, beta_b, a_b, z_b, dn_conv_w[i], dn_a_log[i], dn_dt_bias[i], dn_norm_w[i])
        y_a = o_a @ w_branch_a[i]
        y_b = o_b @ w_branch_b[i]
        gm_a, gm_b = jnp.split(jax.nn.sigmoid(g_merge.astype(jnp.float32)), 2, axis=-1)
        mixed = (gm_a * y_a + gm_b * y_b) @ w_out[i]
        h = DEEPNORM_ALPHA * x.astype(jnp.float32) + mixed
        h = h + jax.nn.sigmoid(h @ w_ple_gate[i]) * (p[i] @ w_ple[i])
        x = layer_norm(h, ln_g[i], ln_b[i]).astype(out_dtype)
    return x
```

```python
import math
import itertools
from contextlib import ExitStack

import numpy as np
import concourse.bass as bass
import concourse.mybir as mybir
from concourse.bass_utils import run_bass_kernel_spmd

F32 = mybir.dt.float32
BF16 = mybir.dt.bfloat16
AF = mybir.ActivationFunctionType
ALU = mybir.AluOpType
AX = mybir.AxisListType

ENGS = ("pe", "act", "dve", "pool", "sp")
NCORES = 8
SEQ = 2048
DM = 1024
NT = SEQ // 128
NEG = -30000.0
NH = 2


class Buf:
    def __init__(self, t, name):
        self.t = t
        self.name = name
        self.st = {}
        self.whole = [None, {}]

    def _get(self, key):
        if key not in self.st:
            self.st[key] = [self.whole[0], dict(self.whole[1])]
        return self.st[key]

    def states(self, key):
        if key is None:
            return [self.whole] + list(self.st.values())
        return [self._get(key)]


class Sched:
    def __init__(self, nc, sems, dma_sems):
        self.nc = nc
        self.sem = sems
        self.dma_sems = dma_sems
        self.dma_cnt = [0] * len(dma_sems)
        self.dma_rr = 0
        self.cnt = {e: 0 for e in ENGS}
        self.known = {e: {} for e in ENGS}
        self.barrier = {}
        self.phase_id = 0
        self.q = {e: [] for e in ENGS}

    def op(self, eng, fn, reads=(), writes=()):
        idx = len(self.q[eng])
        deps = set()
        for (b, k) in reads:
            for s in b.states(k):
                if s[0] is not None:
                    deps.add((s[0][0], s[0][1], False))
        for (b, k) in writes:
            for s in b.states(k):
                if s[0] is not None:
                    deps.add((s[0][0], s[0][1], False))
                for e2, ref in s[1].items():
                    deps.add((e2 if isinstance(e2, str) else e2[0], ref, True))
        self.q[eng].append({"fn": fn, "deps": deps, "inc": False})
        ref = (self.phase_id, idx)
        for (b, k) in reads:
            for s in b.states(k):
                s[1][eng if eng != "sp" else ("sp", idx)] = ref
        for (b, k) in writes:
            for s in b.states(k):
                s[0] = (eng, ref)
                s[1].clear()

    def emit(self):
        nc = self.nc
        ph = self.phase_id
        need = {e: [] for e in ENGS}
        for e in ENGS:
            for rec in self.q[e]:
                ws = []
                for (e2, ref, war) in rec["deps"]:
                    if ref[0] != ph:
                        continue
                    if e2 == e and e == "pe":
                        continue
                    ws.append((e2, ref[1]))
                    self.q[e2][ref[1]]["inc"] = True
                need[e].append(ws)
        for e in ENGS:
            if e != "sp" and self.q[e]:
                self.q[e][-1]["inc"] = True
        val = {e: [] for e in ENGS}
        for e in ENGS:
            if e == "sp":
                for rec in self.q[e]:
                    s = self.dma_rr % len(self.dma_sems)
                    self.dma_rr += 1
                    rec["dsem"] = s
                    rec["dprev"] = self.dma_cnt[s]
                    self.dma_cnt[s] += 16
                    val[e].append((("d", s), self.dma_cnt[s]))
            else:
                c = self.cnt[e]
                for rec in self.q[e]:
                    if rec["inc"]:
                        c += 1
                    val[e].append(((e,), c))
                self.cnt[e] = c

        def semobj(key):
            return self.dma_sems[key[1]] if key[0] == "d" else self.sem[key[0]]

        start_bar = dict(self.barrier)

        if getattr(self, "check", False):
            semv = dict(self.chk_sem) if hasattr(self, "chk_sem") else {}
            pos = {e: 0 for e in ENGS}
            prog = True
            while prog:
                prog = False
                for e in ENGS:
                    while pos[e] < len(self.q[e]):
                        idx = pos[e]
                        rec = self.q[e][idx]
                        ok = True
                        for (e2, i2) in need[e][idx]:
                            key, v = val[e2][i2]
                            if semv.get(key, 0) < v:
                                ok = False
                                break
                        if e == "sp" and ok:
                            key = ("d", rec["dsem"])
                            if semv.get(key, 0) < rec["dprev"]:
                                ok = False
                        if not ok:
                            break
                        if e == "sp":
                            key = ("d", rec["dsem"])
                            semv[key] = semv.get(key, 0) + 16
                        elif rec["inc"]:
                            semv[(e,)] = semv.get((e,), 0) + 1
                        pos[e] += 1
                        prog = True
            stuck = {e: (pos[e], len(self.q[e])) for e in ENGS if pos[e] < len(self.q[e])}
            if stuck:
                print("DEADLOCK in phase", ph, stuck)
                for e in stuck:
                    idx = pos[e]
                    print("  ", e, idx, [(e2, i2, val[e2][i2], semv.get(val[e2][i2][0], 0)) for (e2, i2) in need[e][idx]])
            self.chk_sem = semv

        def run_engine(e, engobj):
            known = self.known[e]
            for key, v in start_bar.items():
                if key == (e,):
                    continue
                if v > 0 and known.get(key, 0) < v:
                    engobj.wait_ge(semobj(key), v)
                    known[key] = v
            for idx, rec in enumerate(self.q[e]):
                waits = {}
                for (e2, i2) in need[e][idx]:
                    key, v = val[e2][i2]
                    if waits.get(key, 0) < v:
                        waits[key] = v
                if e == "sp":
                    key = ("d", rec["dsem"])
                    if rec["dprev"] > 0 and waits.get(key, 0) < rec["dprev"]:
                        waits[key] = rec["dprev"]
                for key, v in waits.items():
                    if known.get(key, 0) >= v:
                        continue
                    engobj.wait_ge(semobj(key), v)
                    known[key] = v
                ins = rec["fn"](engobj)
                if e == "sp":
                    ins.then_inc(self.dma_sems[rec["dsem"]], 16)
                elif rec["inc"]:
                    ins.then_inc(self.sem[e], 1)

        with nc.Block() as block:
            @block.tensor
            def _(eng):
                run_engine("pe", eng)

            @block.scalar
            def _(eng):
                run_engine("act", eng)

            @block.vector
            def _(eng):
                run_engine("dve", eng)

            @block.gpsimd
            def _(eng):
                run_engine("pool", eng)

            @block.sync
            def _(eng):
                run_engine("sp", eng)
                for s, c in enumerate(self.dma_cnt):
                    key = ("d", s)
                    if c > 0 and self.known["sp"].get(key, 0) < c:
                        eng.wait_ge(self.dma_sems[s], c)
                        self.known["sp"][key] = c
        self.barrier = {(e,): self.cnt[e] for e in ENGS if e != "sp"}
        for s_, c in enumerate(self.dma_cnt):
            self.barrier[("d", s_)] = c
        self.phase_id += 1
        self.q = {e: [] for e in ENGS}


C_QA = 0
C_KC, C_VC, C_KS, C_VS, C_KW, C_VW = 512, 640, 768, 896, 1024, 1152
C_GA = 1280
C_ZA = 1304
C_QKVB = 1816
C_BETA = 3352
C_AB = 3356
C_ZB = 3360
C_GM = 3872
D_IN = 5920


CONST_SHAPES = {
    "w1k": [128, 32, 256], "w1v": [128, 32, 256], "w2k": [128, 2, 128], "w2v": [128, 2, 64],
    "posk": [128, 32], "posv": [128, 32], "ovl": [127, 32],
    "bsel": [128, 8, 1920], "bwin": [128, 8, 1408], "bcmp": [128, 8, 512], "shc": [128, 4, 127],
    "ident": [128, 128], "e30": [32, 16, 128], "keepm": [128, 16, 32], "addm": [128, 16, 32],
    "convw": [128, 12, 4], "alog": [128, 4], "dtb": [128, 4], "normw": [128, 128],
    "ubd": [128, 128], "slbd": [128, 128], "ones": [128, 128],
    "w_ba": [512, 1024], "w_bb": [512, 1024], "w_out": [1024, 1024], "w_ple": [256, 1024], "w_pg": [1024, 1024],
    "lng": [128, 1024], "lnb": [128, 1024],
}


class Ctx:
    pass


MAXSTAGE = [10 ** 9]


def run_interleaved(items, W):
    pending = list(items)
    active = []
    for s_ in range(W):
        if pending:
            active.append([s_, pending.pop(0)(s_)])
    while active:
        for ent in list(active):
            try:
                ent.append(0) if len(ent) < 3 else None
                ent[2] += 1
                if ent[2] > MAXSTAGE[0]:
                    raise StopIteration
                next(ent[1])
            except StopIteration:
                if len(ent) >= 3:
                    ent[2] = 0
                if pending:
                    ent[1] = pending.pop(0)(ent[0])
                else:
                    active.remove(ent)


def build_program(debug=()):
    nc = bass.Bass("TRN2", target_bir_lowering=False)
    K = Ctx()
    K.nc = nc
    K.debug = set(debug)
    K.dbg_out = {}
    din = {}

    def dram_in(name, shape):
        din[name] = nc.dram_tensor(name, list(shape), F32, kind="ExternalInput")
        return din[name]

    K.xT = dram_in("xT", [DM, SEQ])
    K.pT = dram_in("pT", [256, SEQ])
    K.x = dram_in("x", [SEQ, DM])
    K.w_in = dram_in("w_in", [DM, D_IN])
    K.out = nc.dram_tensor("out", [SEQ, DM], F32, kind="ExternalOutput")
    for nm, shp in CONST_SHAPES.items():
        setattr(K, nm, dram_in(nm, shp))

    def dbg(name, shape):
        t = nc.dram_tensor("dbg_" + name, list(shape), F32, kind="ExternalOutput")
        K.dbg_out[name] = t
        return t

    K.dbg = dbg

    with ExitStack() as es:
        sems = {e: es.enter_context(nc.semaphore("s_" + e)) for e in ENGS if e != "sp"}
        dsems = [es.enter_context(nc.semaphore(f"dq{i}")) for i in range(8)]
        S = Sched(nc, sems, dsems)
        K.S = S

        def sb(stack, name, shape, dt):
            return Buf(stack.enter_context(nc.sbuf_tensor(name, list(shape), dt)), name)

        K.sb = sb
        K.PS = [Buf(es.enter_context(nc.psum_tensor(f"ps{i}", [128, 512], F32)), f"ps{i}") for i in range(8)]
        K.cast_rr = itertools.cycle(["dve", "act"])

        K.oaT = sb(es, "oaT", [128, 4, SEQ], BF16)
        K.obT = sb(es, "obT", [128, 4, SEQ], BF16)
        if "skip_nsa" not in K.debug:
            phase_nsa(K, es)
        phase_dn(K)
        phase_tail(K)
    return nc, K


def cast_op(K, dst_buf, dst_ap, dkey, src_buf, src_ap, skey, eng=None, scale=None):
    S = K.S
    eng = eng or next(K.cast_rr)
    if eng == "act":
        if scale is None:
            fn = lambda e: e.copy(dst_ap, src_ap)
        else:
            fn = lambda e: e.mul(dst_ap, src_ap, scale)
    else:
        if scale is None:
            fn = lambda e: e.tensor_copy(dst_ap, src_ap)
        else:
            fn = lambda e: e.tensor_scalar_mul(dst_ap, src_ap, scale)
    S.op(eng, fn, reads=[(src_buf, skey)], writes=[(dst_buf, dkey)])


def load_xT(K, xT, xst):
    S = K.S
    src = K.xT.ap().rearrange("(k p) t -> p k t", p=128)
    for hf in range(2):
        for k in range(8):
            st = xst[k % len(xst)]
            sl = slice(hf * 1024, (hf + 1) * 1024)
            S.op("sp", lambda e, st=st, k=k, sl=sl: e.dma_start(out=st.t[:, :], in_=src[:, k, sl]), writes=[(st, None)])
            cast_op(K, xT, xT.t[:, k, sl], (k, hf), st, st.t[:, :], None)


def load_w(K, wdst, dcol, wsrc_ap, c0, n, wst, wkey):
    S = K.S
    kc = wsrc_ap.shape[1]
    step = wst[0].t.shape[2]
    for o in range(0, n, step):
        m = min(step, n - o)
        st = wst[K.wst_rr % len(wst)]
        K.wst_rr += 1
        S.op("sp", lambda e, st=st, o=o, m=m: e.dma_start(out=st.t[:, 0:kc, 0:m], in_=wsrc_ap[:, :, c0 + o:c0 + o + m]),
             writes=[(st, None)])
        cast_op(K, wdst, wdst.t[:, 0:kc, dcol + o:dcol + o + m], (wkey, o), st, st.t[:, 0:kc, 0:m], None)


def phase_nsa(K, es_outer):
    nc, S, sb = K.nc, K.S, K.sb
    with ExitStack() as es:
        K.QT = sb(es, "QT", [128, 4, SEQ], BF16)
        K.KsT2 = sb(es, "KsT2", [128, 2, SEQ], BF16)
        K.KwT2 = sb(es, "KwT2", [128, 2, SEQ], BF16)
        K.KcT = sb(es, "KcT", [128, SEQ], BF16)
        K.VcT = sb(es, "VcT", [128, SEQ], BF16)
        K.VS = sb(es, "VS", [128, NT, 2, 65], BF16)
        K.VW = sb(es, "VW", [128, NT, 2, 65], BF16)
        K.GA = sb(es, "GA", [128, NT, 24], F32)
        K.ZA = sb(es, "ZA", [128, NT, 512], BF16)
        K.KCT2 = sb(es, "KCT2", [128, 2, 127], BF16)
        K.VCX = sb(es, "VCX", [128, 2, 97], BF16)
        phase_nsa_proj(K)
        phase_cmp(K)
        phase_attn(K)


def phase_nsa_proj(K):
    nc, S, sb, PS = K.nc, K.S, K.sb, K.PS
    w_in = K.w_in.ap().rearrange("(k p) c -> p k c", p=128)
    with ExitStack() as es:
        xT = sb(es, "xT_bf", [128, 8, SEQ], BF16)
        xst = [sb(es, f"xst{i}", [128, 1024], F32) for i in range(2)]
        wst = [sb(es, f"wst{i}", [128, 8, 256], F32) for i in range(2)]
        K.wst_rr = 0
        wq = sb(es, "wq", [128, 8, 512], BF16)
        wkd = sb(es, "wkd", [128, 8, 6 * 128], BF16)
        wtok = sb(es, "wtok", [128, 8, 280], BF16)
        wz = sb(es, "wz", [128, 8, 512], BF16)
        S.op("pool", lambda e: e.memset(K.VS.t[:, :, :, 64:65], 1.0), writes=[(K.VS, "ones")])
        S.op("pool", lambda e: e.memset(K.VW.t[:, :, :, 64:65], 1.0), writes=[(K.VW, "ones")])
        load_w(K, wq, 0, w_in, C_QA, 512, wst, "q")
        load_xT(K, xT, xst)
        gi = 0
        for base in (C_KS, C_KW):
            for g in range(2):
                for dup in range(2):
                    load_w(K, wkd, gi * 128 + dup * 64, w_in, base + g * 64, 64, wst, ("kd", gi, dup))
                gi += 1
        load_w(K, wkd, 4 * 128, w_in, C_KC, 128, wst, "kc")
        load_w(K, wkd, 5 * 128, w_in, C_VC, 128, wst, "vc")
        load_w(K, wtok, 0, w_in, C_VS, 128, wst, "vs")
        load_w(K, wtok, 128, w_in, C_VW, 128, wst, "vw")
        load_w(K, wtok, 256, w_in, C_GA, 24, wst, "ga")
        load_w(K, wz, 0, w_in, C_ZA, 512, wst, "za")

        psi = itertools.count()
        fm = []
        for j in range(4):
            fm.append((wq, j * 128, ("q", j)))
        for gi in range(6):
            fm.append((wkd, gi * 128, ("k", gi)))
        for (wb, c0, cons) in fm:
            for t in range(4):
                ps = PS[next(psi) % 8]
                tsl = slice(t * 512, (t + 1) * 512)
                for k in range(8):
                    S.op("pe", lambda e, ps=ps, wb=wb, c0=c0, k=k, tsl=tsl: e.matmul(
                        ps.t[:, :], wb.t[:, k, c0:c0 + 128], xT.t[:, k, tsl], start=(k == 0), stop=(k == 7)),
                        reads=[(wb, None), (xT, (k, tsl.start // 1024))], writes=[(ps, None)])
                if cons[0] == "q":
                    dst, dap, scale = K.QT, K.QT.t[:, cons[1], tsl], 0.125
                else:
                    gi = cons[1]
                    if gi < 2:
                        dst, dap = K.KsT2, K.KsT2.t[:, gi, tsl]
                    elif gi < 4:
                        dst, dap = K.KwT2, K.KwT2.t[:, gi - 2, tsl]
                    elif gi == 4:
                        dst, dap = K.KcT, K.KcT.t[:, tsl]
                    else:
                        dst, dap = K.VcT, K.VcT.t[:, tsl]
                    scale = None
                eng = "act" if (t % 2 == 0) else "dve"
                cast_op(K, dst, dap, (cons, t), ps, ps.t[:, :], None, eng=eng, scale=scale)
        for i in range(NT):
            isl = slice(i * 128, (i + 1) * 128)
            ps = PS[next(psi) % 8]
            for k in range(8):
                S.op("pe", lambda e, ps=ps, k=k, isl=isl: e.matmul(
                    ps.t[:, 0:280], xT.t[:, k, isl], wtok.t[:, k, 0:280], start=(k == 0), stop=(k == 7)),
                    reads=[(wtok, None), (xT, (k, isl.start // 1024))], writes=[(ps, None)])
            S.op("dve", lambda e, ps=ps, i=i: e.tensor_copy(
                K.VS.t[:, i, :, 0:64], ps.t[:, 0:128].rearrange("p (g d) -> p g d", g=2)),
                reads=[(ps, None)], writes=[(K.VS, i)])
            S.op("dve", lambda e, ps=ps, i=i: e.tensor_copy(
                K.VW.t[:, i, :, 0:64], ps.t[:, 128:256].rearrange("p (g d) -> p g d", g=2)),
                reads=[(ps, None)], writes=[(K.VW, i)])
            S.op("dve", lambda e, ps=ps, i=i: e.tensor_copy(K.GA.t[:, i, :], ps.t[:, 256:280]),
                 reads=[(ps, None)], writes=[(K.GA, i)])
            ps2 = PS[next(psi) % 8]
            for k in range(8):
                S.op("pe", lambda e, ps2=ps2, k=k, isl=isl: e.matmul(
                    ps2.t[:, :], xT.t[:, k, isl], wz.t[:, k, :], start=(k == 0), stop=(k == 7)),
                    reads=[(wz, None), (xT, (k, isl.start // 1024))], writes=[(ps2, None)])
            S.op("act", lambda e, ps2=ps2, i=i: e.activation(K.ZA.t[:, i, :], ps2.t[:, :], AF.Silu),
                 reads=[(ps2, None)], writes=[(K.ZA, i)])
        S.op("act", lambda e: e.activation(K.GA.t[:, :, :], K.GA.t[:, :, :], AF.Sigmoid), reads=[(K.GA, None)], writes=[(K.GA, None)])
        if "p1" in K.debug:
            with ExitStack() as es2:
                tmp = sb(es2, "dbgtmp", [128, SEQ], F32)
                for j in range(4):
                    dump(K, tmp, K.QT, K.QT.t[:, j, :], f"QT{j}")
                for g in range(2):
                    dump(K, tmp, K.KsT2, K.KsT2.t[:, g, :], f"KsT{g}")
                    dump(K, tmp, K.KwT2, K.KwT2.t[:, g, :], f"KwT{g}")
                dump(K, tmp, K.KcT, K.KcT.t[:, :], "KcT")
                dump(K, tmp, K.VcT, K.VcT.t[:, :], "VcT")
                dump(K, tmp, K.VS, K.VS.t[:, :, :, :].rearrange("p a b c -> p (a b c)"), "VS")
                dump(K, tmp, K.VW, K.VW.t[:, :, :, :].rearrange("p a b c -> p (a b c)"), "VW")
                dump(K, tmp, K.GA, K.GA.t[:, :, :].rearrange("p a b -> p (a b)"), "GA")
                dump(K, tmp, K.ZA, K.ZA.t[:, :, :].rearrange("p a b -> p (a b)"), "ZA")
                S.emit()
        else:
            S.emit()


def dump(K, tmp, buf, ap, name):
    S = K.S
    P, n = ap.shape[0], ap.shape[1]
    d = K.dbg(name, [P, n])
    W = tmp.t.shape[1]
    for o in range(0, n, W):
        m = min(W, n - o)
        S.op("dve", lambda e, o=o, m=m: e.tensor_copy(tmp.t[0:P, 0:m], ap[:, o:o + m]), reads=[(buf, None)], writes=[(tmp, None)])
        S.op("sp", lambda e, o=o, m=m: e.dma_start(out=d.ap()[:, o:o + m], in_=tmp.t[0:P, 0:m]), reads=[(tmp, None)])


def host_inputs(inputs, b):
    x = np.ascontiguousarray(inputs["x"][b])
    m = {
        "xT": np.ascontiguousarray(x.T),
        "x": x,
        "pT": np.ascontiguousarray(inputs["p"][0, b].T),
        "w_in": np.ascontiguousarray(inputs["w_in"][0]),
    }
    if "_consts" not in inputs:
        inputs["_consts"] = host_consts(inputs)
    m.update(inputs["_consts"])
    return m


_CACHE = {}


def kernel(**inputs):
    inputs = {k: np.asarray(v) for k, v in inputs.items()}
    if "nc" not in _CACHE:
        _CACHE["nc"] = build_program()[0]
    nc = _CACHE["nc"]
    in_maps = [host_inputs(inputs, b) for b in range(NCORES)]
    res = run_bass_kernel_spmd(nc, in_maps, core_ids=list(range(NCORES)))
    out = np.stack([np.asarray(r["out"]) for r in res.results], axis=0)
    return out.astype(np.float32)


def _bucket(dist):
    n = np.maximum(dist, 0)
    nf = np.maximum(n, 1).astype(np.float32)
    large = 16 + (np.log(nf / np.float32(16)) / np.float32(math.log(64)) * np.float32(16)).astype(np.int32)
    large = np.minimum(large, 31)
    return np.where(n < 16, n, large)


def host_consts(inputs):
    c = {}
    w1k = inputs["cmp_w1_k"][0].reshape(32, 64, 256).transpose(1, 0, 2)
    w1v = inputs["cmp_w1_v"][0].reshape(32, 64, 256).transpose(1, 0, 2)
    c["w1k"] = np.concatenate([w1k, w1k], 0)
    c["w1v"] = np.concatenate([w1v, w1v], 0)
    w2k = inputs["cmp_w2_k"][0]
    c["w2k"] = np.concatenate([w2k, w2k], 1).reshape(2, 128, 128).transpose(1, 0, 2)
    c["w2v"] = inputs["cmp_w2_v"][0].reshape(2, 128, 64).transpose(1, 0, 2)
    pk = inputs["cmp_pos_k"][0].T
    pv = inputs["cmp_pos_v"][0].T
    c["posk"] = np.concatenate([pk, pk], 0)
    c["posv"] = np.concatenate([pv, pv], 0)
    ci = np.arange(127)[:, None]
    sj = np.arange(32)[None, :]
    c["ovl"] = ((ci * 16 < (sj + 1) * 64) & (ci * 16 + 32 > sj * 64)).astype(np.float32)
    tab = inputs["rel_bias"].astype(np.float32)
    kp = np.arange(128)[:, None]

    def toep(width, win):
        n = np.arange(width)[None, :] - 384 - kp
        g = tab[_bucket(n)]
        bad = (n < 0) | ((n >= 512) if win else False)
        g = np.where(bad[:, :, None], np.float32(NEG), g)
        return np.ascontiguousarray(g.transpose(0, 2, 1))

    c["bsel"] = toep(1920, False)
    c["bwin"] = toep(1408, True)
    r = np.arange(127)[:, None]
    xx = np.arange(512)[None, :]
    n = xx - 16 * (r - 96) - 31
    g = np.where((n < 0)[:, :, None], np.float32(NEG), tab[_bucket(n)])
    g = np.concatenate([g.transpose(0, 2, 1), np.full((1, 8, 512), NEG, np.float32)], 0)
    c["bcmp"] = np.ascontiguousarray(g)
    shc = np.zeros((128, 4, 127), np.float32)
    for cc in range(4):
        for nn in range(127):
            p = nn + 96 - 32 * cc
            if nn < min(127, 32 * (cc + 1) - 1):
                shc[p, cc, nn] = 1.0
            else:
                shc[127, cc, nn] = 1.0
    c["shc"] = shc
    c["ident"] = np.eye(128, dtype=np.float32)
    e30 = np.zeros((32, 16, 128), np.float32)
    for kt in range(16):
        for p in range(128):
            e30[2 * kt + p // 64, kt, p] = 30000.0
    c["e30"] = e30
    q = np.arange(SEQ)
    cur = (q // 64)[:, None]
    blk = np.arange(32)[None, :]
    forced = (blk == 0) | (blk == cur) | (blk == cur - 1)
    fut = blk > cur
    keep = (~(forced | fut)).astype(np.float32)
    addm = np.where(forced, 1e6, np.where(fut, -1e6, 0.0)).astype(np.float32)
    c["keepm"] = np.ascontiguousarray(keep.reshape(16, 128, 32).transpose(1, 0, 2))
    c["addm"] = np.ascontiguousarray(addm.reshape(16, 128, 32).transpose(1, 0, 2))
    c["convw"] = inputs["dn_conv_w"][0].T.reshape(12, 128, 4).transpose(1, 0, 2)
    c["alog"] = np.broadcast_to(inputs["dn_a_log"][0][None, :], (128, 4))
    c["dtb"] = np.broadcast_to(inputs["dn_dt_bias"][0][None, :], (128, 4))
    c["normw"] = np.broadcast_to(inputs["dn_norm_w"][0][None, :], (128, 128))
    t = np.arange(128)
    same = (t[:, None] // 64) == (t[None, :] // 64)
    c["ubd"] = ((t[:, None] <= t[None, :]) & same).astype(np.float32)
    c["slbd"] = ((t[:, None] > t[None, :]) & same).astype(np.float32)
    c["ones"] = np.ones((128, 128), np.float32)
    c["w_ba"] = inputs["w_branch_a"][0]
    c["w_bb"] = inputs["w_branch_b"][0]
    c["w_out"] = inputs["w_out"][0]
    c["w_ple"] = inputs["w_ple"][0]
    c["w_pg"] = inputs["w_ple_gate"][0]
    c["lng"] = np.broadcast_to(inputs["ln_g"][0][None, :], (128, 1024))
    c["lnb"] = np.broadcast_to(inputs["ln_b"][0][None, :], (128, 1024))
    return {k: np.ascontiguousarray(v, dtype=np.float32) for k, v in c.items()}


def load_cast(K, dst, dst_ap, dkey, src_ap, stg, P, n, eng=None):
    S = K.S
    W = stg[0].t.shape[1]
    for o in range(0, n, W):
        m = min(W, n - o)
        st = stg[K.stg_rr % len(stg)]
        K.stg_rr += 1
        S.op("sp", lambda e, st=st, o=o, m=m: e.dma_start(out=st.t[0:P, 0:m], in_=src_ap[:, o:o + m]), writes=[(st, None)])
        cast_op(K, dst, dst_ap[:, o:o + m], (dkey, o), st, st.t[0:P, 0:m], None, eng=eng)


def phase_cmp(K):
    nc, S, sb, PS = K.nc, K.S, K.sb, K.PS
    with ExitStack() as es:
        stg = [sb(es, f"cstg{i}", [128, 2048], F32) for i in range(2)]
        K.stg_rr = 0
        w1 = [sb(es, "w1k_s", [128, 32 * 256], BF16), sb(es, "w1v_s", [128, 32 * 256], BF16)]
        w2k = sb(es, "w2k_s", [128, 256], BF16)
        w2v = sb(es, "w2v_s", [128, 128], BF16)
        pos = [sb(es, "posk_s", [128, 32], BF16), sb(es, "posv_s", [128, 32], BF16)]
        ovl = sb(es, "ovl_s", [128, 32], BF16)
        pb = sb(es, "pb", [128, 4], F32)
        gel = sb(es, "gel", [128, 8, 127], BF16)
        load_cast(K, w1[0], w1[0].t[:, :], "w", K.w1k.ap().rearrange("p i h -> p (i h)"), stg, 128, 8192)
        load_cast(K, w1[1], w1[1].t[:, :], "w", K.w1v.ap().rearrange("p i h -> p (i h)"), stg, 128, 8192)
        load_cast(K, w2k, w2k.t[:, :], "w", K.w2k.ap().rearrange("p a b -> p (a b)"), stg, 128, 256)
        load_cast(K, w2v, w2v.t[:, :], "w", K.w2v.ap().rearrange("p a b -> p (a b)"), stg, 128, 128)
        load_cast(K, pos[0], pos[0].t[:, :], "w", K.posk.ap(), stg, 128, 32)
        load_cast(K, pos[1], pos[1].t[:, :], "w", K.posv.ap(), stg, 128, 32)
        load_cast(K, ovl, ovl.t[0:127, :], "w", K.ovl.ap(), stg, 127, 32)
        psi = itertools.count()
        srcs = [K.KcT, K.VcT]
        for kv in range(2):
            for half in range(2):
                ps = PS[next(psi) % 8]
                hs = slice(half * 128, (half + 1) * 128)
                for i in range(32):
                    S.op("pe", lambda e, ps=ps, kv=kv, i=i, half=half: e.matmul(
                        ps.t[:, 0:1], w1[kv].t[0:64, i * 256 + half * 128:i * 256 + half * 128 + 128], pos[kv].t[0:64, i:i + 1],
                        start=(i == 0), stop=(i == 31)), reads=[(w1[kv], ("w", (i * 256) // 2048 * 2048)), (pos[kv], None)], writes=[(ps, None)])
                col = kv * 2 + half
                S.op("dve", lambda e, ps=ps, col=col: e.tensor_copy(pb.t[:, col:col + 1], ps.t[:, 0:1]),
                     reads=[(ps, None)], writes=[(pb, col)])
                for g in range(2):
                    ps = PS[next(psi) % 8]
                    gs = slice(g * 64, (g + 1) * 64)
                    for i in range(32):
                        S.op("pe", lambda e, ps=ps, kv=kv, i=i, half=half, gs=gs: e.matmul(
                            ps.t[:, 0:127], w1[kv].t[gs, i * 256 + half * 128:i * 256 + half * 128 + 128],
                            srcs[kv].t[gs, i:i + 2017:16], start=(i == 0), stop=(i == 31)),
                            reads=[(w1[kv], ("w", (i * 256) // 2048 * 2048)), (srcs[kv], None)], writes=[(ps, None)])
                    gi = (kv * 2 + half) * 2 + g
                    S.op("act", lambda e, ps=ps, gi=gi, col=col: e.activation(
                        gel.t[:, gi, :], ps.t[:, 0:127], AF.Gelu, bias=pb.t[:, col:col + 1]),
                        reads=[(ps, None), (pb, col)], writes=[(gel, gi)])
        for g in range(2):
            ps = PS[next(psi) % 8]
            for half in range(2):
                gi = (0 * 2 + half) * 2 + g
                S.op("pe", lambda e, ps=ps, half=half, gi=gi: e.matmul(
                    ps.t[:, 0:127], w2k.t[:, half * 128:(half + 1) * 128], gel.t[:, gi, :], start=(half == 0), stop=(half == 1)),
                    reads=[(w2k, None), (gel, gi)], writes=[(ps, None)])
            S.op("dve", lambda e, ps=ps, g=g: e.tensor_copy(K.KCT2.t[:, g, :], ps.t[:, 0:127]),
                 reads=[(ps, None)], writes=[(K.KCT2, g)])
            ps = PS[next(psi) % 8]
            for half in range(2):
                gi = (1 * 2 + half) * 2 + g
                S.op("pe", lambda e, ps=ps, half=half, gi=gi: e.matmul(
                    ps.t[0:127, 0:64], gel.t[:, gi, :], w2v.t[:, half * 64:(half + 1) * 64], start=(half == 0), stop=(half == 1)),
                    reads=[(w2v, None), (gel, gi)], writes=[(ps, None)])
            S.op("dve", lambda e, ps=ps, g=g: e.tensor_copy(K.VCX.t[0:127, g, 33:97], ps.t[0:127, 0:64]),
                 reads=[(ps, None)], writes=[(K.VCX, ("v", g))])
            S.op("pool", lambda e, g=g: e.tensor_copy(K.VCX.t[0:127, g, 0:32], ovl.t[0:127, :]),
                 reads=[(ovl, None)], writes=[(K.VCX, ("o", g))])
            S.op("pool", lambda e, g=g: e.memset(K.VCX.t[0:127, g, 32:33], 1.0), writes=[(K.VCX, ("1", g))])
        if "p2" in K.debug:
            with ExitStack() as es2:
                tmp = sb(es2, "dbgtmp2", [128, 512], F32)
                dump(K, tmp, K.KCT2, K.KCT2.t[:, :, :].rearrange("p a b -> p (a b)"), "KCT2")
                dump(K, tmp, K.VCX, K.VCX.t[0:127, :, :].rearrange("p a b -> p (a b)"), "VCX")
                S.emit()
        else:
            S.emit()


def phase_attn(K):
    nc, S, sb, PS = K.nc, K.S, K.sb, K.PS
    with ExitStack() as es:
        stg = [sb(es, f"astg{i}", [128, 2048], F32) for i in range(2)]
        K.stg_rr = 0
        ident = sb(es, "ident_s", [128, 128], BF16)
        BSel = sb(es, "BSel", [128, 8 * 1920], BF16)
        BWin = sb(es, "BWin", [128, 8 * 1408], BF16)
        BC = sb(es, "BC", [128, 8 * 512], BF16)
        SHC = sb(es, "SHC", [128, 4 * 127], BF16)
        E30 = sb(es, "E30", [32, 16 * 128], BF16)
        KEEP = sb(es, "KEEP", [128, 16 * 32], F32)
        ADDM = sb(es, "ADDM", [128, 16 * 32], F32)
        load_cast(K, ident, ident.t[:, :], "c", K.ident.ap(), stg, 128, 128)
        load_cast(K, BSel, BSel.t[:, :], "c", K.bsel.ap().rearrange("p a b -> p (a b)"), stg, 128, 8 * 1920)
        load_cast(K, BWin, BWin.t[:, :], "c", K.bwin.ap().rearrange("p a b -> p (a b)"), stg, 128, 8 * 1408)
        load_cast(K, BC, BC.t[:, :], "c", K.bcmp.ap().rearrange("p a b -> p (a b)"), stg, 128, 8 * 512)
        load_cast(K, SHC, SHC.t[:, :], "c", K.shc.ap().rearrange("p a b -> p (a b)"), stg, 128, 4 * 127)
        load_cast(K, E30, E30.t[:, :], "c", K.e30.ap().rearrange("p a b -> p (a b)"), stg, 32, 16 * 128)
        for (tb_, n_) in ((BSel, 8 * 1920), (BWin, 8 * 1408)):
            for o in range(0, n_, 2048):
                m_ = min(2048, n_ - o)
                S.op("act", lambda e, tb_=tb_, o=o, m_=m_: e.activation(tb_.t[:, o:o + m_], tb_.t[:, o:o + m_], AF.Exp),
                     reads=[(tb_, ("c", o))], writes=[(tb_, ("c", o))])
        S.op("sp", lambda e: e.dma_start(out=KEEP.t[:, :], in_=K.keepm.ap().rearrange("p a b -> p (a b)")), writes=[(KEEP, None)])
        S.op("sp", lambda e: e.dma_start(out=ADDM.t[:, :], in_=K.addm.ap().rearrange("p a b -> p (a b)")), writes=[(ADDM, None)])

        MTm1 = sb(es, "MTm1", [32, 2, 512], BF16)
        impacc = sb(es, "impacc", [128, 2, 4, 32], F32)
        imptmp = sb(es, "imptmp", [128, 4, 32], F32)
        imp2 = sb(es, "imp2", [128, 4, 32], F32)
        top8 = sb(es, "top8", [128, 8], F32)
        msk = sb(es, "msk", [128, 32], BF16)
        PT = [sb(es, f"PT{i}", [128, 512], BF16) for i in range(4)]
        oacc = sb(es, "oacc", [128, 4, 512], F32)
        otmp = sb(es, "otmp", [128, 4, 64], F32)
        rec = [sb(es, f"rec{i}", [128, 4], F32) for i in range(2)]
        coef = [sb(es, f"coef{i}", [128, 4], F32) for i in range(2)]
        oab = sb(es, "oab", [128, 512], BF16)
        PSs = [PS[0], PS[1], PS[2], PS[3]]
        PSo = [PS[4], PS[5], PS[6]]
        PSm = PS[7]
        si = itertools.count()
        oi = itertools.count()
        ri = itertools.count()
        ACT_EXP = AF.Exp

        def finish_branch(ps_o, W, dcol, imp_col, br, h, c, first):
            r = rec[next(ri) % 2]
            cf = coef[(next(ri)) % 2]
            pv = ps_o.t[:, 0:4 * W].rearrange("p (i w) -> p i w", w=W)
            S.op("dve", lambda e: e.tensor_scalar(r.t[:, :], pv[:, :, dcol:dcol + 1].rearrange("p i o -> p (i o)"), 1e-30, None, op0=ALU.max),
                 reads=[(ps_o, None)], writes=[(r, None)])
            S.op("dve", lambda e: e.reciprocal(r.t[:, :], r.t[:, :]), reads=[(r, None)], writes=[(r, None)])
            if imp_col is not None:
                g, hh = h // 4, h % 4
                rb = r.t[:, :].unsqueeze(2).to_broadcast([128, 4, 32])
                if hh == 0:
                    S.op("dve", lambda e: e.tensor_tensor(impacc.t[:, g, :, :], pv[:, :, 0:32], rb, ALU.mult),
                         reads=[(ps_o, None), (r, None)], writes=[(impacc, g)])
                else:
                    S.op("dve", lambda e: e.tensor_tensor(imptmp.t[:, :, :], pv[:, :, 0:32], rb, ALU.mult),
                         reads=[(ps_o, None), (r, None)], writes=[(imptmp, None)])
                    S.op("dve", lambda e: e.tensor_tensor(impacc.t[:, g, :, :], impacc.t[:, g, :, :], imptmp.t[:, :, :], ALU.add),
                         reads=[(impacc, g), (imptmp, None)], writes=[(impacc, g)])
            S.op("dve", lambda e: e.tensor_tensor(cf.t[:, :], r.t[:, :], K.GA.t[:, 4 * c:4 * c + 4, br * 8 + h], ALU.mult),
                 reads=[(r, None), (K.GA, None)], writes=[(cf, None)])
            cb = cf.t[:, :].unsqueeze(2).to_broadcast([128, 4, 64])
            ocol = 33 if imp_col is not None else 0
            dst = oacc.t[:, :, h * 64:(h + 1) * 64]
            if first:
                S.op("dve", lambda e: e.tensor_tensor(dst, pv[:, :, ocol:ocol + 64], cb, ALU.mult),
                     reads=[(ps_o, None), (cf, None)], writes=[(oacc, h)])
            else:
                S.op("dve", lambda e: e.tensor_tensor(otmp.t[:, :, :], pv[:, :, ocol:ocol + 64], cb, ALU.mult),
                     reads=[(ps_o, None), (cf, None)], writes=[(otmp, None)])
                S.op("dve", lambda e: e.tensor_tensor(dst, dst, otmp.t[:, :, :], ALU.add),
                     reads=[(oacc, h), (otmp, None)], writes=[(oacc, h)])

        DEPTH = 3

        def pipeline(jobs):
            n = len(jobs)
            for idx in range(n + DEPTH):
                if idx < n:
                    jobs[idx][0]()
                if idx - DEPTH >= 0:
                    jobs[idx - DEPTH][1]()

        for c in range(4):
            csl = slice(c * 512, (c + 1) * 512)
            NC = 127
            jobs = []
            for h in range(8):
                def mk(h=h, c=c, csl=csl, NC=NC):
                    g, hp = h // 4, (h % 2) * 64
                    st = {}

                    def scores():
                        k_ = next(si)
                        ps_s = PSs[k_ % 4]
                        pt = PT[k_ % 4]
                        st["pt"] = pt
                        S.op("pe", lambda e: e.matmul(ps_s.t[0:NC, :], K.KCT2.t[hp:hp + 64, g, 0:NC], K.QT.t[hp:hp + 64, h // 2, csl], start=True, stop=False),
                             reads=[(K.KCT2, None), (K.QT, None)], writes=[(ps_s, None)])
                        S.op("pe", lambda e: e.matmul(ps_s.t[0:NC, :], SHC.t[:, c * 127:c * 127 + NC], BC.t[:, h * 512:(h + 1) * 512], start=False, stop=True),
                             reads=[(SHC, None), (BC, None)], writes=[(ps_s, None)])
                        S.op("act", lambda e: e.activation(pt.t[0:NC, :], ps_s.t[0:NC, :], ACT_EXP), reads=[(ps_s, None)], writes=[(pt, None)])

                    def pv():
                        pt = st["pt"]
                        ps_o = PSo[next(oi) % 3]
                        for i in range(4):
                            S.op("pe", lambda e, i=i: e.matmul(ps_o.t[:, i * 97:(i + 1) * 97], pt.t[0:NC, i * 128:(i + 1) * 128], K.VCX.t[0:NC, g, :],
                                                              start=True, stop=True),
                                 reads=[(pt, None), (K.VCX, None)], writes=[(ps_o, None)])
                        finish_branch(ps_o, 97, 32, 0, 0, h, c, True)
                    return (scores, pv)
                jobs.append(mk())
            pipeline(jobs)
            if "p3x" in K.debug and c == 0:
                K.dtmp = sb(es, "dbgtmp4", [128, 1024], F32)
                dump(K, K.dtmp, oacc, oacc.t[:, :, :].rearrange("p a b -> p (a b)"), "oacc_cmp")
                dump(K, K.dtmp, impacc, impacc.t[:, :, :, :].rearrange("p a b c -> p (a b c)"), "impacc")
            for g in range(2):
                S.op("dve", lambda e, g=g, c=c: e.tensor_tensor(imp2.t[:, :, :], impacc.t[:, g, :, :],
                                                               KEEP.t[:, 4 * c * 32:(4 * c + 4) * 32].rearrange("p (i j) -> p i j", j=32), ALU.mult),
                     reads=[(impacc, g), (KEEP, None)], writes=[(imp2, None)])
                S.op("dve", lambda e, c=c: e.tensor_tensor(imp2.t[:, :, :], imp2.t[:, :, :],
                                                          ADDM.t[:, 4 * c * 32:(4 * c + 4) * 32].rearrange("p (i j) -> p i j", j=32), ALU.add),
                     reads=[(imp2, None), (ADDM, None)], writes=[(imp2, None)])
                for i in range(4):
                    S.op("dve", lambda e, i=i: e.max(top8.t[:, :], imp2.t[:, i, :]), reads=[(imp2, None)], writes=[(top8, None)])
                    S.op("dve", lambda e, i=i: e.tensor_scalar(msk.t[:, :], imp2.t[:, i, :], top8.t[:, 7:8], -1.0, op0=ALU.is_ge, op1=ALU.add),
                         reads=[(imp2, None), (top8, None)], writes=[(msk, None)])
                    psb = PSm.t.bitcast(BF16)
                    S.op("pe", lambda e, psb=psb: e.transpose(psb[0:32, 0:128], msk.t[:, :], ident.t[:, :]),
                         reads=[(msk, None), (ident, None)], writes=[(PSm, None)])
                    S.op("act", lambda e, psb=psb, g=g, i=i: e.copy(MTm1.t[0:32, g, i * 128:(i + 1) * 128], psb[0:32, 0:128]),
                         reads=[(PSm, None)], writes=[(MTm1, (g, i))])
            jobs = []
            for br in (2, 1):
                for h in range(8):
                    g, hp = h // 4, (h % 2) * 64
                    if br == 1:
                        kts = list(range(0, 4 * c + 4))
                    else:
                        kts = list(range(max(0, 4 * c - 4), 4 * c + 4))
                    shared = {"first": True}
                    for kt in kts:
                        def mk(br=br, h=h, g=g, hp=hp, kt=kt, c=c, csl=csl, shared=shared, kts=kts):
                            st = {}

                            def scores():
                                k_ = next(si)
                                ps_s = PSs[k_ % 4]
                                pt = PT[k_ % 4]
                                st["pt"] = pt
                                ksl = slice(kt * 128, (kt + 1) * 128)
                                KT = K.KsT2 if br == 1 else K.KwT2
                                S.op("pe", lambda e: e.matmul(ps_s.t[:, :], KT.t[hp:hp + 64, g, ksl], K.QT.t[hp:hp + 64, h // 2, csl], start=True, stop=(br == 2)),
                                     reads=[(KT, None), (K.QT, None)], writes=[(ps_s, None)])
                                d0 = 512 * c - 128 * kt
                                if br == 1:
                                    off = h * 1920 + min(d0, 1024) + 384
                                    EB = BSel
                                    S.op("pe", lambda e: e.matmul(ps_s.t[:, :], E30.t[0:32, kt * 128:(kt + 1) * 128], MTm1.t[0:32, g, :], start=False, stop=True),
                                         reads=[(E30, None), (MTm1, None)], writes=[(ps_s, None)])
                                else:
                                    off = h * 1408 + d0 + 384
                                    EB = BWin
                                S.op("act", lambda e: e.activation(pt.t[:, :], ps_s.t[:, :], ACT_EXP), reads=[(ps_s, None)], writes=[(pt, None)])
                                S.op("dve", lambda e: e.tensor_tensor(pt.t[:, :], pt.t[:, :], EB.t[:, off:off + 512], ALU.mult),
                                     reads=[(pt, None), (EB, None)], writes=[(pt, None)])

                            def pv():
                                pt = st["pt"]
                                if shared["first"]:
                                    shared["ps_o"] = PSo[next(oi) % 3]
                                ps_o = shared["ps_o"]
                                VV = K.VS if br == 1 else K.VW
                                for i in range(4):
                                    qi = 4 * c + i
                                    lo = 0 if br == 1 else max(0, qi - 4)
                                    if kt < lo or kt > qi:
                                        continue
                                    S.op("pe", lambda e, i=i, qi=qi, fst=shared["first"]: e.matmul(
                                        ps_o.t[:, i * 65:(i + 1) * 65], pt.t[:, i * 128:(i + 1) * 128], VV.t[:, kt, g, :],
                                        start=fst, stop=(kt == qi), skip_group_check=True),
                                        reads=[(pt, None), (VV, None)], writes=[(ps_o, None)])
                                    shared["first"] = False
                                if kt == kts[-1]:
                                    finish_branch(ps_o, 65, 64, None, br, h, c, False)
                            return (scores, pv)
                        jobs.append(mk())
            pipeline(jobs)
            for i in range(4):
                qi = 4 * c + i
                S.op("dve", lambda e, i=i, qi=qi: e.tensor_tensor(oab.t[:, :], oacc.t[:, i, :], K.ZA.t[:, qi, :], ALU.mult),
                     reads=[(oacc, None), (K.ZA, qi)], writes=[(oab, None)])
                psb = PSm.t.bitcast(BF16)
                for j in range(4):
                    S.op("pe", lambda e, psb=psb, j=j: e.transpose(psb[:, j * 128:(j + 1) * 128], oab.t[:, j * 128:(j + 1) * 128], ident.t[:, :]),
                         reads=[(oab, None), (ident, None)], writes=[(PSm, None)])
                S.op("act", lambda e, psb=psb, qi=qi: e.copy(K.oaT.t[:, :, qi * 128:(qi + 1) * 128],
                                                             psb[:, 0:512].rearrange("p (j t) -> p j t", t=128)),
                     reads=[(PSm, None)], writes=[(K.oaT, qi)])
        if "p3" in K.debug:
            with ExitStack() as es2:
                tmp = sb(es2, "dbgtmp3", [128, 1024], F32)
                for j in range(4):
                    dump(K, tmp, K.oaT, K.oaT.t[:, j, :], f"oaT{j}")
                S.emit()
        else:
            S.emit()


def phase_dn(K):
    nc, S, sb, PS = K.nc, K.S, K.sb, K.PS
    w_in = K.w_in.ap().rearrange("(k p) c -> p k c", p=128)
    for hp in range(2):
      K.h0 = NH * hp
      K.sfx = f"_{hp}"
      sfx = K.sfx
      with ExitStack() as es:
        qT = sb(es, "dqT" + sfx, [128, NH, SEQ], BF16)
        UO = sb(es, "dUO" + sfx, [128, NT, NH, 128], F32)
        WT = sb(es, "dWT" + sfx, [128, NH, SEQ], BF16)
        AQ = sb(es, "dAQ" + sfx, [128, NH, NT, 128], BF16)
        KD = sb(es, "dKD" + sfx, [128, NT, NH, 128], BF16)
        ZB = sb(es, "dZB" + sfx, [128, NT, NH * 128], BF16)
        EGC = sb(es, "dEGC" + sfx, [128, NT, NH], F32)
        EGT = sb(es, "dEGT" + sfx, [128, NT, 2 * NH], F32)
        ident = sb(es, "dident" + sfx, [128, 128], BF16)
        cst = sb(es, "dcst" + sfx, [128, 128], F32)
        S.op("sp", lambda e, cst=cst: e.dma_start(out=cst.t[:, :], in_=K.ident.ap()), writes=[(cst, None)])
        S.op("dve", lambda e, cst=cst, ident=ident: e.tensor_copy(ident.t[:, :], cst.t[:, :]), reads=[(cst, None)], writes=[(ident, None)])
        with ExitStack() as es1:
            kT = sb(es1, "dkT" + sfx, [128, NH, SEQ], BF16)
            Vt = sb(es1, "dVt" + sfx, [128, NT, NH, 128], BF16)
            Kt = sb(es1, "dKt" + sfx, [128, NT, NH, 128], BF16)
            G = sb(es1, "dG" + sfx, [128, NT, NH], F32)
            NB = sb(es1, "dNB" + sfx, [128, NT, NH], F32)
            Bt = sb(es1, "dB" + sfx, [128, NT, NH], F32)
            dn_proj(K, es1, w_in, qT, kT, Vt, Kt, G, NB, Bt, ZB, ident)
            if "stop_proj" not in K.debug:
                dn_prep(K, qT, kT, Vt, Kt, G, NB, Bt, UO, WT, AQ, KD, EGC, EGT, ident)
        if "stop_proj" not in K.debug and "stop_prep" not in K.debug:
            dn_scan(K, qT, UO, WT, AQ, KD, ZB, EGC, EGT, ident)


def dn_proj(K, es1, w_in, qT, kT, Vt, Kt, G, NB, Bt, ZB, ident):
    nc, S, sb, PS = K.nc, K.S, K.sb, K.PS
    with ExitStack() as es:
        xT = sb(es, "xT_bf2" + K.sfx, [128, 8, SEQ], BF16)
        xst = [sb(es, f"dxst{i}" + K.sfx, [128, 1024], F32) for i in range(2)]
        wst = [sb(es, f"dwst{i}" + K.sfx, [128, 8, 128], F32) for i in range(2)]
        K.wst_rr = 0
        wb = [sb(es, f"dwb{i}" + K.sfx, [128, 8, NH * 128], BF16) for i in range(3)]
        wsm = sb(es, "dwsm" + K.sfx, [128, 8, 8], BF16)
        h0 = K.h0
        cw = sb(es, "dcw" + K.sfx, [128, 12, 4], F32)
        alog = sb(es, "dalog" + K.sfx, [128, 4], F32)
        dtb = sb(es, "ddtb" + K.sfx, [128, 4], F32)
        ones = sb(es, "dones" + K.sfx, [128, 128], BF16)
        onesf = sb(es, "donesf" + K.sfx, [128, 128], F32)
        WP = 3
        stage_s = [sb(es, f"dstage{i}" + K.sfx, [128, 516], BF16) for i in range(WP)]
        identf = sb(es, "didentf" + K.sfx, [128, 128], F32)
        DW = sb(es, "dDW" + K.sfx, [128, 3 * NH * 4, 128], BF16)
        act_s = [sb(es, f"dact{i}" + K.sfx, [128, 512], F32) for i in range(WP)]
        sq_s = [sb(es, f"dsq{i}" + K.sfx, [128, 512], BF16) for i in range(WP)]
        rstd_s = [sb(es, f"drstd{i}" + K.sfx, [128, 512], F32) for i in range(WP)]
        vT_s = [sb(es, f"dvT{i}" + K.sfx, [128, 512], BF16) for i in range(WP)]
        sp_ = sb(es, "dsp" + K.sfx, [128, NT, NH], F32)
        RAW = sb(es, "draw" + K.sfx, [128, NT, 8], F32)
        S.op("sp", lambda e: e.dma_start(out=cw.t[:, :, :], in_=K.convw.ap()), writes=[(cw, None)])
        S.op("sp", lambda e: e.dma_start(out=alog.t[:, :], in_=K.alog.ap()), writes=[(alog, None)])
        S.op("sp", lambda e: e.dma_start(out=dtb.t[:, :], in_=K.dtb.ap()), writes=[(dtb, None)])
        S.op("sp", lambda e: e.dma_start(out=onesf.t[:, :], in_=K.ones.ap()), writes=[(onesf, None)])
        S.op("dve", lambda e: e.tensor_copy(ones.t[:, :], onesf.t[:, :]), reads=[(onesf, None)], writes=[(ones, None)])
        S.op("act", lambda e: e.activation(alog.t[:, :], alog.t[:, :], AF.Exp), reads=[(alog, None)], writes=[(alog, None)])
        S.op("sp", lambda e: e.dma_start(out=identf.t[:, :], in_=K.ident.ap()), writes=[(identf, None)])
        for cg_ in range(3):
            for hh_ in range(NH):
                for i_ in range(4):
                    j_ = cg_ * 4 + h0 + hh_
                    d_ = (cg_ * NH + hh_) * 4 + i_
                    S.op("dve", lambda e, j_=j_, i_=i_, d_=d_: e.tensor_scalar(DW.t[:, d_, :], identf.t[:, :], cw.t[:, j_, i_:i_ + 1], None, op0=ALU.mult),
                         reads=[(identf, None), (cw, None)], writes=[(DW, d_)])
        psi = itertools.count()
        load_w(K, wb[0], 0, w_in, C_QKVB + 0 * 512 + h0 * 128, NH * 128, wst, ("w", 0))
        load_xT(K, xT, xst)
        for cg in range(1, 3):
            load_w(K, wb[cg], 0, w_in, C_QKVB + cg * 512 + h0 * 128, NH * 128, wst, ("w", cg))
        prev_stage = [None]

        def conv_item(cg, hh, t):
            def gen(slot):
                w = wb[cg]
                j = cg * 4 + h0 + hh
                tsl = slice(t * 512, (t + 1) * 512)
                stage, act_, sq, rstd, vT = stage_s[slot], act_s[slot], sq_s[slot], rstd_s[slot], vT_s[slot]
                ps = PS[2 * slot]
                acc = ps
                for k in range(8):
                    S.op("pe", lambda e, k=k: e.matmul(ps.t[:, :], w.t[:, k, hh * 128:(hh + 1) * 128], xT.t[:, k, tsl], start=(k == 0), stop=(k == 7)),
                         reads=[(w, None), (xT, (k, tsl.start // 1024))], writes=[(ps, None)])
                yield
                if t == 0:
                    S.op("dve", lambda e: e.memset(stage.t[:, 0:3], 0.0), writes=[(stage, "c")])
                else:
                    pst_ = prev_stage[0]
                    S.op("dve", lambda e: e.tensor_copy(stage.t[:, 0:3], pst_.t[:, 512:515]), reads=[(pst_, None)], writes=[(stage, "c")])
                prev_stage[0] = stage
                S.op("act", lambda e: e.copy(stage.t[:, 3:515], ps.t[:, :]), reads=[(ps, None)], writes=[(stage, "m")])
                yield
                for i in range(4):
                    d_ = (cg * NH + hh) * 4 + i
                    S.op("pe", lambda e, i=i, d_=d_: e.matmul(ps.t[:, :], DW.t[:, d_, :], stage.t[:, i:i + 512], start=(i == 0), stop=(i == 3)),
                         reads=[(DW, d_), (stage, None)], writes=[(ps, None)])
                yield
                if cg == 2:
                    S.op("act", lambda e: e.activation(vT.t[:, :], acc.t[:, :], AF.Silu), reads=[(acc, None)], writes=[(vT, None)])
                    yield
                    psb = PS[2 * slot + 1]
                    pb = psb.t.bitcast(BF16)
                    for u in range(4):
                        S.op("pe", lambda e, u=u: e.transpose(pb[:, u * 128:(u + 1) * 128], vT.t[:, u * 128:(u + 1) * 128], ident.t[:, :]),
                             reads=[(vT, None), (ident, None)], writes=[(psb, None)])
                    yield
                    S.op("dve", lambda e: e.tensor_copy(Vt.t[:, 4 * t:4 * t + 4, hh, :], pb[:, 0:512].rearrange("p (u d) -> p u d", d=128)),
                         reads=[(psb, None)], writes=[(Vt, (t, hh))])
                else:
                    S.op("act", lambda e: e.activation(act_.t[:, :], acc.t[:, :], AF.Silu), reads=[(acc, None)], writes=[(act_, None)])
                    yield
                    S.op("dve", lambda e: e.tensor_tensor(sq.t[:, :], act_.t[:, :], act_.t[:, :], ALU.mult), reads=[(act_, None)], writes=[(sq, None)])
                    yield
                    ps2 = PS[2 * slot + 1]
                    S.op("pe", lambda e: e.matmul(ps2.t[:, :], ones.t[:, :], sq.t[:, :], start=True, stop=True),
                         reads=[(ones, None), (sq, None)], writes=[(ps2, None)])
                    yield
                    S.op("act", lambda e: e.activation(rstd.t[:, :], ps2.t[:, :], AF.Sqrt, bias=1e-6), reads=[(ps2, None)], writes=[(rstd, None)])
                    yield
                    S.op("dve", lambda e: e.reciprocal(rstd.t[:, :], rstd.t[:, :]), reads=[(rstd, None)], writes=[(rstd, None)])
                    dst = qT if cg == 0 else kT
                    sc = 128 ** -0.5 if cg == 0 else 1.0
                    S.op("dve", lambda e: e.scalar_tensor_tensor(dst.t[:, hh, tsl], act_.t[:, :], sc, rstd.t[:, :], op0=ALU.mult, op1=ALU.mult),
                         reads=[(act_, None), (rstd, None)], writes=[(dst, (hh, t))])
                    if cg == 1:
                        yield
                        psb = PS[2 * slot]
                        pb = psb.t.bitcast(BF16)
                        for u in range(4):
                            S.op("pe", lambda e, u=u: e.transpose(pb[:, u * 128:(u + 1) * 128], kT.t[:, hh, t * 512 + u * 128:t * 512 + (u + 1) * 128], ident.t[:, :]),
                                 reads=[(kT, (hh, t)), (ident, None)], writes=[(psb, None)])
                        yield
                        S.op("act", lambda e: e.copy(Kt.t[:, 4 * t:4 * t + 4, hh, :], pb[:, 0:512].rearrange("p (u d) -> p u d", d=128)),
                             reads=[(psb, None)], writes=[(Kt, (t, hh))])
            return gen

        run_interleaved([conv_item(cg, hh, t) for cg in range(3) for hh in range(NH) for t in range(4)], WP)
        psi = itertools.count()
        load_w(K, wsm, 0, w_in, C_BETA, 8, wst, "sm")
        wz = wb[1]
        load_w(K, wz, 0, w_in, C_ZB + h0 * 128, NH * 128, wst, "zb")
        for i in range(NT):
            isl = slice(i * 128, (i + 1) * 128)
            ps = PS[next(psi) % 4]
            for k in range(8):
                S.op("pe", lambda e, ps=ps, k=k, isl=isl: e.matmul(ps.t[:, 0:8], xT.t[:, k, isl], wsm.t[:, k, :], start=(k == 0), stop=(k == 7)),
                     reads=[(wsm, None), (xT, (k, isl.start // 1024))], writes=[(ps, None)])
            S.op("dve", lambda e, ps=ps, i=i: e.tensor_copy(RAW.t[:, i, :], ps.t[:, 0:8]), reads=[(ps, None)], writes=[(RAW, i)])
            ps2 = PS[next(psi) % 4]
            for k in range(8):
                S.op("pe", lambda e, ps2=ps2, k=k, isl=isl: e.matmul(ps2.t[:, 0:NH * 128], xT.t[:, k, isl], wz.t[:, k, 0:NH * 128], start=(k == 0), stop=(k == 7)),
                     reads=[(wz, None), (xT, (k, isl.start // 1024))], writes=[(ps2, None)])
            S.op("act", lambda e, ps2=ps2, i=i: e.activation(ZB.t[:, i, :], ps2.t[:, 0:NH * 128], AF.Silu), reads=[(ps2, None)], writes=[(ZB, i)])
        S.op("act", lambda e: e.activation(Bt.t[:, :, :], RAW.t[:, :, h0:h0 + NH], AF.Sigmoid), reads=[(RAW, None)], writes=[(Bt, None)])
        S.op("dve", lambda e: e.tensor_tensor(sp_.t[:, :, :], RAW.t[:, :, 4 + h0:4 + h0 + NH],
                                             dtb.t[:, h0:h0 + NH].unsqueeze(1).to_broadcast([128, NT, NH]), ALU.add),
             reads=[(RAW, None), (dtb, None)], writes=[(sp_, None)])
        S.op("act", lambda e: e.activation(sp_.t[:, :, :], sp_.t[:, :, :], AF.Exp), reads=[(sp_, None)], writes=[(sp_, None)])
        S.op("act", lambda e: e.activation(sp_.t[:, :, :], sp_.t[:, :, :], AF.Ln, bias=1.0), reads=[(sp_, None)], writes=[(sp_, None)])
        S.op("dve", lambda e: e.scalar_tensor_tensor(G.t[:, :, :], sp_.t[:, :, :], -1.0,
                                                    alog.t[:, h0:h0 + NH].unsqueeze(1).to_broadcast([128, NT, NH]), op0=ALU.mult, op1=ALU.mult),
             reads=[(sp_, None), (alog, None)], writes=[(G, None)])
        S.op("dve", lambda e: e.tensor_scalar_mul(NB.t[:, :, :], Bt.t[:, :, :], -1.0), reads=[(Bt, None)], writes=[(NB, None)])
        if "p4a" in K.debug:
            tmp = sb(es, "dbgtmp5" + K.sfx, [128, 2048], F32)
            for h in range(NH):
                dump(K, tmp, qT, qT.t[:, h, :], f"dqT{h0 + h}")
                dump(K, tmp, kT, kT.t[:, h, :], f"dkT{h0 + h}")
            dump(K, tmp, Vt, Vt.t[:, :, :, :].rearrange("p a b c -> p (a b c)"), "dVt" + K.sfx)
            dump(K, tmp, Kt, Kt.t[:, :, :, :].rearrange("p a b c -> p (a b c)"), "dKt" + K.sfx)
            dump(K, tmp, G, G.t[:, :, :].rearrange("p a b -> p (a b)"), "dG" + K.sfx)
            dump(K, tmp, Bt, Bt.t[:, :, :].rearrange("p a b -> p (a b)"), "dB" + K.sfx)
        S.emit()


def dn_prep(K, qT, kT, Vt, Kt, G, NB, Bt, UO, WT, AQ, KD, EGC, EGT, ident):
    nc, S, sb, PS = K.nc, K.S, K.sb, K.PS
    with ExitStack() as es:
        UBD = sb(es, "pUBD" + K.sfx, [128, 128], F32)
        SLBD = sb(es, "pSLBD" + K.sfx, [128, 128], F32)
        ONES = sb(es, "pONES" + K.sfx, [128, 128], F32)
        S.op("sp", lambda e: e.dma_start(out=UBD.t[:, :], in_=K.ubd.ap()), writes=[(UBD, None)])
        S.op("sp", lambda e: e.dma_start(out=SLBD.t[:, :], in_=K.slbd.ap()), writes=[(SLBD, None)])
        S.op("sp", lambda e: e.dma_start(out=ONES.t[:, :], in_=K.ones.ap()), writes=[(ONES, None)])
        UBDb = sb(es, "pUBDb" + K.sfx, [128, 128], BF16)
        SLBDb = sb(es, "pSLBDb" + K.sfx, [128, 128], BF16)
        ONESb = sb(es, "pONESb" + K.sfx, [128, 128], BF16)
        for (a_, b_) in ((UBD, UBDb), (SLBD, SLBDb), (ONES, ONESb)):
            S.op("dve", lambda e, a_=a_, b_=b_: e.tensor_copy(b_.t[:, :], a_.t[:, :]), reads=[(a_, None)], writes=[(b_, None)])
        NEGSL = sb(es, "pNEGSL" + K.sfx, [128, 128], BF16)
        NEGU = sb(es, "pNEGU" + K.sfx, [128, 128], BF16)
        for (a_, b_) in ((SLBD, NEGSL), (UBD, NEGU)):
            S.op("dve", lambda e, a_=a_, b_=b_: e.tensor_scalar(b_.t[:, :], a_.t[:, :], -1.0, -NEG, op0=ALU.add, op1=ALU.mult),
                 reads=[(a_, None)], writes=[(b_, None)])
        Gh = sb(es, "pGh" + K.sfx, [128, NT, NH], BF16)
        Gl = sb(es, "pGl" + K.sfx, [128, NT, NH], BF16)
        Gf = sb(es, "pGf" + K.sfx, [128, NT, NH], F32)
        S.op("dve", lambda e: e.tensor_copy(Gh.t[:, :, :], G.t[:, :, :]), reads=[(G, None)], writes=[(Gh, None)])
        S.op("dve", lambda e: e.tensor_copy(Gf.t[:, :, :], Gh.t[:, :, :]), reads=[(Gh, None)], writes=[(Gf, None)])
        S.op("dve", lambda e: e.tensor_tensor(Gf.t[:, :, :], G.t[:, :, :], Gf.t[:, :, :], ALU.subtract), reads=[(G, None), (Gf, None)], writes=[(Gf, None)])
        S.op("dve", lambda e: e.tensor_copy(Gl.t[:, :, :], Gf.t[:, :, :]), reads=[(Gf, None)], writes=[(Gl, None)])
        GSl2 = [sb(es, f"pGSl{i}" + K.sfx, [128, 2 * NH], BF16) for i in range(NT)]
        NBUF = 6
        gUl = [sb(es, f"pgUl{i}" + K.sfx, [128, 128], BF16) for i in range(NBUF)]
        gU = [sb(es, f"pgU{i}" + K.sfx, [128, 128], BF16) for i in range(NBUF)]
        Dm = [sb(es, f"pDm{i}" + K.sfx, [128, 128], F32) for i in range(NBUF)]
        DTm = [sb(es, f"pDTm{i}" + K.sfx, [128, 128], F32) for i in range(NBUF)]
        Nb = [[sb(es, f"pN{i}_{l}" + K.sfx, [128, 128], BF16) for l in range(2)] for i in range(NBUF)]
        NTb = [[sb(es, f"pNT{i}_{l}" + K.sfx, [128, 128], BF16) for l in range(2)] for i in range(NBUF)]
        X = [sb(es, f"pX{i}" + K.sfx, [128, 256], F32) for i in range(NBUF)]
        Xb = [sb(es, f"pXb{i}" + K.sfx, [128, 256], BF16) for i in range(NBUF)]
        GS2 = [sb(es, f"pGS{i}" + K.sfx, [128, 2 * NH], BF16) for i in range(NT)]
        GR2 = [sb(es, f"pGR{i}" + K.sfx, [128, NH], F32) for i in range(NT)]
        BK2 = [sb(es, f"pBK{i}" + K.sfx, [128, NH], F32) for i in range(NT)]
        wb_ = [sb(es, f"pwb{i}" + K.sfx, [128, 128], BF16) for i in range(NBUF)]
        psi = itertools.count()
        gens = []
        for m in range(NT):
            msl = slice(m * 128, (m + 1) * 128)
            GS, GSl, GR, BK = GS2[m], GSl2[m], GR2[m], BK2[m]
            ps = PS[next(psi) % 8]
            S.op("pe", lambda e, ps=ps, m=m: e.matmul(ps.t[:, 0:NH], UBDb.t[:, :], Gh.t[:, m, :], start=True, stop=False),
                 reads=[(UBDb, None), (Gh, None)], writes=[(ps, None)])
            S.op("pe", lambda e, ps=ps, m=m: e.matmul(ps.t[:, 0:NH], UBDb.t[:, :], Gl.t[:, m, :], start=False, stop=True),
                 reads=[(UBDb, None), (Gl, None)], writes=[(ps, None)])
            S.op("act", lambda e, ps=ps, m=m: e.activation(EGC.t[:, m, :], ps.t[:, 0:NH], AF.Exp), reads=[(ps, None)], writes=[(EGC, m)])
            ps = PS[next(psi) % 8]
            S.op("pe", lambda e, ps=ps, m=m: e.matmul(ps.t[:, 0:NH], SLBDb.t[:, :], Gh.t[:, m, :], start=True, stop=False),
                 reads=[(SLBDb, None), (Gh, None)], writes=[(ps, None)])
            S.op("pe", lambda e, ps=ps, m=m: e.matmul(ps.t[:, 0:NH], SLBDb.t[:, :], Gl.t[:, m, :], start=False, stop=True),
                 reads=[(SLBDb, None), (Gl, None)], writes=[(ps, None)])
            S.op("act", lambda e, ps=ps, GR=GR: e.activation(GR.t[:, :], ps.t[:, 0:NH], AF.Exp), reads=[(ps, None)], writes=[(GR, None)])
            ps = PS[next(psi) % 8]
            for (gs_, gsrc, st_) in ((GS, Gh, True), (GSl, Gl, False)):
                S.op("pool", lambda e, gs_=gs_: e.memset(gs_.t[:, :], 0.0), writes=[(gs_, None)])
                S.op("pool", lambda e, m=m, gs_=gs_, gsrc=gsrc: e.tensor_copy(gs_.t[0:64, 0:NH], gsrc.t[0:64, m, :]), reads=[(gsrc, None)], writes=[(gs_, None)])
                S.op("pool", lambda e, m=m, gs_=gs_, gsrc=gsrc: e.tensor_copy(gs_.t[64:128, NH:2 * NH], gsrc.t[64:128, m, :]), reads=[(gsrc, None)], writes=[(gs_, None)])
                S.op("pe", lambda e, ps=ps, gs_=gs_, st_=st_: e.matmul(ps.t[:, 0:2 * NH], ONESb.t[:, :], gs_.t[:, :], start=st_, stop=(not st_)),
                     reads=[(ONESb, None), (gs_, None)], writes=[(ps, None)])
            S.op("act", lambda e, ps=ps, m=m: e.activation(EGT.t[:, m, :], ps.t[:, 0:2 * NH], AF.Exp), reads=[(ps, None)], writes=[(EGT, m)])
            S.op("dve", lambda e, m=m, BK=BK: e.tensor_tensor(BK.t[:, :], Bt.t[:, m, :], EGC.t[:, m, :], ALU.mult),
                 reads=[(Bt, m), (EGC, m)], writes=[(BK, None)])
            for h in range(NH):
              def mk(m=m, h=h, msl=msl, GR=GR, BK=BK):
               def gen(b):
                    S.op("dve", lambda e, b=b, m=m, h=h: e.tensor_scalar(gU[b].t[:, :], UBD.t[:, :], Gh.t[:, m, h:h + 1], None, op0=ALU.mult),
                         reads=[(UBD, None), (Gh, None)], writes=[(gU[b], None)])
                    S.op("dve", lambda e, b=b, m=m, h=h: e.tensor_scalar(gUl[b].t[:, :], UBD.t[:, :], Gl.t[:, m, h:h + 1], None, op0=ALU.mult),
                         reads=[(UBD, None), (Gl, None)], writes=[(gUl[b], None)])
                    yield
                    psd = PS[next(psi) % 8]
                    S.op("pe", lambda e, psd=psd, b=b: e.matmul(psd.t[:, 0:128], gU[b].t[:, :], SLBDb.t[:, :], start=True, stop=False),
                         reads=[(gU[b], None), (SLBDb, None)], writes=[(psd, None)])
                    S.op("pe", lambda e, psd=psd, b=b: e.matmul(psd.t[:, 0:128], gUl[b].t[:, :], SLBDb.t[:, :], start=False, stop=False),
                         reads=[(gUl[b], None), (SLBDb, None)], writes=[(psd, None)])
                    S.op("pe", lambda e, psd=psd: e.matmul(psd.t[:, 0:128], ident.t[:, :], NEGSL.t[:, :], start=False, stop=True),
                         reads=[(ident, None), (NEGSL, None)], writes=[(psd, None)])
                    S.op("pe", lambda e, psd=psd, b=b: e.matmul(psd.t[:, 128:256], SLBDb.t[:, :], gU[b].t[:, :], start=False, stop=False, skip_group_check=True),
                         reads=[(gU[b], None), (SLBDb, None)], writes=[(psd, None)])
                    S.op("pe", lambda e, psd=psd, b=b: e.matmul(psd.t[:, 128:256], SLBDb.t[:, :], gUl[b].t[:, :], start=False, stop=False, skip_group_check=True),
                         reads=[(gUl[b], None), (SLBDb, None)], writes=[(psd, None)])
                    S.op("pe", lambda e, psd=psd: e.matmul(psd.t[:, 128:256], ident.t[:, :], NEGU.t[:, :], start=False, stop=True, skip_group_check=True),
                         reads=[(ident, None), (NEGU, None)], writes=[(psd, None)])
                    yield
                    S.op("act", lambda e, psd=psd, b=b: e.activation(Dm[b].t[:, :], psd.t[:, 0:128], AF.Exp), reads=[(psd, None)], writes=[(Dm[b], None)])
                    S.op("act", lambda e, psd=psd, b=b: e.activation(DTm[b].t[:, :], psd.t[:, 128:256], AF.Exp), reads=[(psd, None)], writes=[(DTm[b], None)])
                    yield
                    psk = PS[next(psi) % 8]
                    S.op("pe", lambda e, psk=psk, h=h, msl=msl: e.matmul(psk.t[:, 0:128], kT.t[:, h, msl], kT.t[:, h, msl], start=True, stop=True),
                         reads=[(kT, None)], writes=[(psk, None)])
                    S.op("pe", lambda e, psk=psk, h=h, msl=msl: e.matmul(psk.t[:, 128:256], kT.t[:, h, msl], qT.t[:, h, msl], start=True, stop=True),
                         reads=[(kT, None), (qT, None)], writes=[(psk, None)])
                    yield
                    S.op("dve", lambda e, psk=psk, b=b, m=m, h=h: e.scalar_tensor_tensor(
                        Nb[b][0].t[:, :], psk.t[:, 0:128], NB.t[:, m, h:h + 1], Dm[b].t[:, :], op0=ALU.mult, op1=ALU.mult),
                        reads=[(psk, None), (NB, m), (Dm[b], None)], writes=[(Nb[b][0], None)])
                    S.op("dve", lambda e, psk=psk, b=b, m=m, h=h: e.tensor_tensor(AQ.t[:, h, m, :], psk.t[:, 128:256], DTm[b].t[:, :], ALU.mult),
                         reads=[(psk, None), (DTm[b], None)], writes=[(AQ, (h, m))])
                    yield
                    pst = PS[next(psi) % 8]
                    ptb = pst.t.bitcast(BF16)
                    S.op("pe", lambda e, ptb=ptb, b=b: e.transpose(ptb[:, 0:128], Nb[b][0].t[:, :], ident.t[:, :]),
                         reads=[(Nb[b][0], None), (ident, None)], writes=[(pst, None)])
                    yield
                    S.op("act", lambda e, ptb=ptb, b=b: e.copy(NTb[b][0].t[:, :], ptb[:, 0:128]), reads=[(pst, None)], writes=[(NTb[b][0], None)])
                    yield
                    S.op("dve", lambda e, b=b, m=m, h=h: e.tensor_scalar(X[b].t[:, 0:128], Vt.t[:, m, h, :], Bt.t[:, m, h:h + 1], None, op0=ALU.mult),
                         reads=[(Vt, None), (Bt, m)], writes=[(X[b], None)])
                    S.op("dve", lambda e, b=b, m=m, h=h: e.tensor_scalar(X[b].t[:, 128:256], Kt.t[:, m, h, :], BK.t[:, h:h + 1], None, op0=ALU.mult),
                         reads=[(Kt, None), (BK, None)], writes=[(X[b], None)])
                    S.op("act", lambda e, b=b: e.copy(Xb[b].t[:, :], X[b].t[:, :]), reads=[(X[b], None)], writes=[(Xb[b], None)])
                    S.op("pool", lambda e, b=b, m=m, h=h: e.tensor_scalar(KD.t[:, m, h, :], Kt.t[:, m, h, :], GR.t[:, h:h + 1], None, op0=ALU.mult),
                         reads=[(Kt, None), (GR, None)], writes=[(KD, (m, h))])
                    for l in range(6):
                        yield
                        cur, nxt = l % 2, (l + 1) % 2
                        psx = PS[next(psi) % 8]
                        S.op("pe", lambda e, psx=psx, b=b, cur=cur: e.matmul(psx.t[:, 0:256], NTb[b][cur].t[:, :], Xb[b].t[:, :], start=True, stop=True),
                             reads=[(NTb[b][cur], None), (Xb[b], None)], writes=[(psx, None)])
                        yield
                        S.op("dve", lambda e, psx=psx, b=b: e.tensor_tensor(X[b].t[:, :], X[b].t[:, :], psx.t[:, 0:256], ALU.add),
                             reads=[(psx, None), (X[b], None)], writes=[(X[b], None)])
                        yield
                        if l < 5:
                            S.op("act", lambda e, b=b: e.copy(Xb[b].t[:, :], X[b].t[:, :]), reads=[(X[b], None)], writes=[(Xb[b], None)])
                            psq = PS[next(psi) % 8]
                            S.op("pe", lambda e, psq=psq, b=b, cur=cur: e.matmul(psq.t[:, 0:128], NTb[b][cur].t[:, :], Nb[b][cur].t[:, :], start=True, stop=True),
                                 reads=[(NTb[b][cur], None), (Nb[b][cur], None)], writes=[(psq, None)])
                            S.op("pe", lambda e, psq=psq, b=b, cur=cur: e.matmul(psq.t[:, 128:256], Nb[b][cur].t[:, :], NTb[b][cur].t[:, :], start=True, stop=True),
                                 reads=[(NTb[b][cur], None), (Nb[b][cur], None)], writes=[(psq, None)])
                            yield
                            S.op("act", lambda e, psq=psq, b=b, nxt=nxt: e.copy(Nb[b][nxt].t[:, :], psq.t[:, 0:128]),
                                 reads=[(psq, None)], writes=[(Nb[b][nxt], None)])
                            S.op("dve", lambda e, psq=psq, b=b, nxt=nxt: e.tensor_copy(NTb[b][nxt].t[:, :], psq.t[:, 128:256]),
                                 reads=[(psq, None)], writes=[(NTb[b][nxt], None)])
                    yield
                    S.op("pool", lambda e, b=b, m=m, h=h: e.tensor_copy(UO.t[:, m, h, :], X[b].t[:, 0:128]), reads=[(X[b], None)], writes=[(UO, (m, h))])
                    S.op("act", lambda e, b=b: e.copy(wb_[b].t[:, :], X[b].t[:, 128:256]), reads=[(X[b], None)], writes=[(wb_[b], None)])
                    yield
                    pst = PS[next(psi) % 8]
                    ptb = pst.t.bitcast(BF16)
                    S.op("pe", lambda e, ptb=ptb, b=b: e.transpose(ptb[:, 0:128], wb_[b].t[:, :], ident.t[:, :]),
                         reads=[(wb_[b], None), (ident, None)], writes=[(pst, None)])
                    yield
                    S.op("act", lambda e, ptb=ptb, h=h, msl=msl: e.copy(WT.t[:, h, msl], ptb[:, 0:128]), reads=[(pst, None)], writes=[(WT, (h, msl.start))])
               return gen
              gens.append(mk())
        run_interleaved(gens, NBUF)
        S.emit()


def dn_scan(K, qT, UO, WT, AQ, KD, ZB, EGC, EGT, ident):
    nc, S, sb, PS = K.nc, K.S, K.sb, K.PS
    with ExitStack() as es:
        St = sb(es, "sSt" + K.sfx, [128, NH, 128], F32)
        Sb = sb(es, "sSb" + K.sfx, [128, NH, 128], BF16)
        h0 = K.h0
        VN = [sb(es, f"sVN{h}" + K.sfx, [128, 128], BF16) for h in range(NH)]
        tq = [sb(es, f"stq{h}" + K.sfx, [128, 128], F32) for h in range(NH)]
        NW = sb(es, "sNW" + K.sfx, [128, 128], F32)
        sq = sb(es, "ssq" + K.sfx, [128, NH * 128], F32)
        ss = sb(es, "sss" + K.sfx, [128, NH], F32)
        on = sb(es, "son" + K.sfx, [128, NH * 128], F32)
        ob = sb(es, "sob" + K.sfx, [128, NH * 128], BF16)
        S.op("sp", lambda e: e.dma_start(out=NW.t[:, :], in_=K.normw.ap()), writes=[(NW, None)])
        S.op("pool", lambda e: e.memset(St.t[:, :, :], 0.0), writes=[(St, None)])
        S.op("pool", lambda e: e.memset(Sb.t[:, :, :], 0.0), writes=[(Sb, None)])
        for n in range(2 * NT):
            m, half = n // 2, n % 2
            rs = slice(half * 64, half * 64 + 64)
            msl = slice(m * 128, (m + 1) * 128)

            def head_gen(h, m=m, half=half, rs=rs, msl=msl):
                def gen(slot):
                    p1, p2 = PS[2 * h], PS[2 * h + 1]
                    S.op("pe", lambda e: e.matmul(p1.t[:, 0:128], WT.t[:, h, msl], Sb.t[:, h, :], start=True, stop=True),
                         reads=[(WT, None), (Sb, h)], writes=[(p1, "a")])
                    S.op("pe", lambda e: e.matmul(p1.t[:, 128:256], qT.t[:, h, msl], Sb.t[:, h, :], start=True, stop=True),
                         reads=[(qT, None), (Sb, h)], writes=[(p1, "b")])
                    yield
                    S.op("dve", lambda e: e.tensor_tensor(VN[h].t[rs, :], UO.t[rs, m, h, :], p1.t[rs, 0:128], ALU.subtract),
                         reads=[(UO, (m, h)), (p1, "a")], writes=[(VN[h], None)])
                    S.op("dve", lambda e: e.tensor_scalar(tq[h].t[rs, :], p1.t[rs, 128:256], EGC.t[rs, m, h:h + 1], None, op0=ALU.mult),
                         reads=[(p1, "b"), (EGC, None)], writes=[(tq[h], None)])
                    yield
                    S.op("pe", lambda e: e.matmul(p2.t[:, 128:256], KD.t[rs, m, h, :], VN[h].t[rs, :], start=True, stop=True),
                         reads=[(KD, None), (VN[h], None)], writes=[(p2, "b")])
                    S.op("pe", lambda e: e.matmul(p2.t[:, 0:128], AQ.t[rs, h, m, :], VN[h].t[rs, :], start=True, stop=True),
                         reads=[(AQ, None), (VN[h], None)], writes=[(p2, "a")])
                    yield
                    ecol = EGT.t[:, m, half * NH + h:half * NH + h + 1]
                    S.op("dve", lambda e: e.scalar_tensor_tensor(Sb.t[:, h, :], St.t[:, h, :], ecol, p2.t[:, 128:256], op0=ALU.mult, op1=ALU.add),
                         reads=[(St, h), (EGT, None), (p2, "b")], writes=[(Sb, h)])
                    S.op("dve", lambda e: e.scalar_tensor_tensor(St.t[:, h, :], St.t[:, h, :], ecol, p2.t[:, 128:256], op0=ALU.mult, op1=ALU.add),
                         reads=[(St, h), (EGT, None), (p2, "b")], writes=[(St, h)])
                    S.op("dve", lambda e: e.tensor_tensor(UO.t[rs, m, h, :], tq[h].t[rs, :], p2.t[rs, 0:128], ALU.add),
                         reads=[(tq[h], None), (p2, "a")], writes=[(UO, (m, h))])
                return gen

            run_interleaved([head_gen(h) for h in range(NH)], NH)
            if half == 1 and "nonorm" not in K.debug:
                uo = UO.t[:, m, :, :]
                S.op("pool", lambda e, uo=uo: e.tensor_tensor(sq.t[:, :].rearrange("p (h d) -> p h d", d=128), uo, uo, ALU.mult),
                     reads=[(UO, None)], writes=[(sq, None)])
                S.op("dve", lambda e: e.tensor_reduce(ss.t[:, :], sq.t[:, :].rearrange("p (h d) -> p h d", d=128), axis=AX.X, op=ALU.add),
                     reads=[(sq, None)], writes=[(ss, None)])
                S.op("act", lambda e: e.activation(ss.t[:, :], ss.t[:, :], AF.Sqrt, bias=1e-6, scale=1.0 / 128), reads=[(ss, None)], writes=[(ss, None)])
                S.op("dve", lambda e: e.reciprocal(ss.t[:, :], ss.t[:, :]), reads=[(ss, None)], writes=[(ss, None)])
                S.op("dve", lambda e, uo=uo: e.tensor_tensor(on.t[:, :].rearrange("p (h d) -> p h d", d=128), uo,
                                                            ss.t[:, :].unsqueeze(2).to_broadcast([128, NH, 128]), ALU.mult),
                     reads=[(UO, None), (ss, None)], writes=[(on, None)])
                S.op("pool", lambda e: e.tensor_tensor(on.t[:, :].rearrange("p (h d) -> p h d", d=128), on.t[:, :].rearrange("p (h d) -> p h d", d=128),
                                                      NW.t[:, :].unsqueeze(1).to_broadcast([128, NH, 128]), ALU.mult),
                     reads=[(on, None), (NW, None)], writes=[(on, None)])
                S.op("dve", lambda e, m=m: e.tensor_tensor(ob.t[:, :], on.t[:, :], ZB.t[:, m, :], ALU.mult),
                     reads=[(on, None), (ZB, m)], writes=[(ob, None)])
                pst = PS[7]
                ptb = pst.t.bitcast(BF16)
                for j in range(NH):
                    S.op("pe", lambda e, ptb=ptb, j=j: e.transpose(ptb[:, 512 + j * 128:512 + (j + 1) * 128], ob.t[:, j * 128:(j + 1) * 128], ident.t[:, :]),
                         reads=[(ob, None), (ident, None)], writes=[(pst, "t")])
                S.op("act", lambda e, ptb=ptb, m=m: e.copy(K.obT.t[:, h0:h0 + NH, m * 128:(m + 1) * 128], ptb[:, 512:512 + NH * 128].rearrange("p (j t) -> p j t", t=128)),
                     reads=[(pst, "t")], writes=[(K.obT, (m, h0))])
        if "p4" in K.debug:
            tmp = sb(es, "dbgtmp6" + K.sfx, [128, 2048], F32)
            for j in range(NH):
                dump(K, tmp, K.obT, K.obT.t[:, h0 + j, :], f"obT{h0 + j}")
        S.emit()


ALPHA = float((2 * 1) ** 0.25)


def phase_tail(K):
    nc, S, sb, PS = K.nc, K.S, K.sb, K.PS
    w_in = K.w_in.ap().rearrange("(k p) c -> p k c", p=128)
    with ExitStack() as es0:
        mT = sb(es0, "mT", [128, 8, SEQ], BF16)
        wst = [sb(es0, f"twst{i}", [128, 8, 256], F32) for i in range(2)]
        K.wst_rr = 0
        with ExitStack() as es:
            xT = sb(es, "xT_bf3", [128, 8, SEQ], BF16)
            xst = [sb(es, f"txst{i}", [128, 1024], F32) for i in range(2)]
            Wa = sb(es, "tWa", [128, 4, 1024], BF16)
            Wb = sb(es, "tWb", [128, 4, 1024], BF16)
            wgm = [sb(es, f"twgm{i}", [128, 8, 256], BF16) for i in range(2)]
            gma = sb(es, "tgma", [128, 512], F32)
            gmb = sb(es, "tgmb", [128, 512], F32)
            t1 = sb(es, "tt1", [128, 512], F32)
            t2 = sb(es, "tt2", [128, 512], F32)
            load_w(K, Wa, 0, K.w_ba.ap().rearrange("(k p) c -> p k c", p=128), 0, 1024, wst, "wa")
            load_w(K, Wb, 0, K.w_bb.ap().rearrange("(k p) c -> p k c", p=128), 0, 1024, wst, "wb")
            load_xT(K, xT, xst)
            for f in range(8):
                wg = wgm[f % 2]
                fs = slice(f * 128, (f + 1) * 128)
                load_w(K, wg, 0, w_in, C_GM + f * 128, 128, wst, ("ga", f))
                load_w(K, wg, 128, w_in, C_GM + 1024 + f * 128, 128, wst, ("gb", f))
                for t in range(4):
                    tsl = slice(t * 512, (t + 1) * 512)
                    pa, pb, pga, pgb = PS[0 + 4 * (t % 2)], PS[1 + 4 * (t % 2)], PS[2 + 4 * (t % 2)], PS[3 + 4 * (t % 2)]
                    for k in range(4):
                        S.op("pe", lambda e, pa=pa, k=k, fs=fs, tsl=tsl: e.matmul(pa.t[:, :], Wa.t[:, k, fs], K.oaT.t[:, k, tsl], start=(k == 0), stop=(k == 3)),
                             reads=[(Wa, None), (K.oaT, None)], writes=[(pa, None)])
                    for k in range(4):
                        S.op("pe", lambda e, pb=pb, k=k, fs=fs, tsl=tsl: e.matmul(pb.t[:, :], Wb.t[:, k, fs], K.obT.t[:, k, tsl], start=(k == 0), stop=(k == 3)),
                             reads=[(Wb, None), (K.obT, None)], writes=[(pb, None)])
                    for k in range(8):
                        S.op("pe", lambda e, pga=pga, k=k, wg=wg, tsl=tsl: e.matmul(pga.t[:, :], wg.t[:, k, 0:128], xT.t[:, k, tsl], start=(k == 0), stop=(k == 7)),
                             reads=[(wg, None), (xT, (k, tsl.start // 1024))], writes=[(pga, None)])
                    for k in range(8):
                        S.op("pe", lambda e, pgb=pgb, k=k, wg=wg, tsl=tsl: e.matmul(pgb.t[:, :], wg.t[:, k, 128:256], xT.t[:, k, tsl], start=(k == 0), stop=(k == 7)),
                             reads=[(wg, None), (xT, (k, tsl.start // 1024))], writes=[(pgb, None)])
                    S.op("act", lambda e, pga=pga: e.activation(gma.t[:, :], pga.t[:, :], AF.Sigmoid), reads=[(pga, None)], writes=[(gma, None)])
                    S.op("act", lambda e, pgb=pgb: e.activation(gmb.t[:, :], pgb.t[:, :], AF.Sigmoid), reads=[(pgb, None)], writes=[(gmb, None)])
                    S.op("dve", lambda e, pa=pa: e.tensor_tensor(t1.t[:, :], gma.t[:, :], pa.t[:, :], ALU.mult), reads=[(gma, None), (pa, None)], writes=[(t1, None)])
                    S.op("dve", lambda e, pb=pb: e.tensor_tensor(t2.t[:, :], gmb.t[:, :], pb.t[:, :], ALU.mult), reads=[(gmb, None), (pb, None)], writes=[(t2, None)])
                    S.op("pool", lambda e, f=f, tsl=tsl: e.tensor_tensor(mT.t[:, f, tsl], t1.t[:, :], t2.t[:, :], ALU.add),
                         reads=[(t1, None), (t2, None)], writes=[(mT, (f, t))])
            S.emit()
        with ExitStack() as es:
            Wo = sb(es, "tWo", [128, 8, 1024], BF16)
            Wg = sb(es, "tWg", [128, 8, 1024], BF16)
            Wp = sb(es, "tWp", [128, 2, 1024], BF16)
            pT = sb(es, "tpT", [128, 2, SEQ], BF16)
            LNG = sb(es, "tLNG", [128, 1024], F32)
            LNB = sb(es, "tLNB", [128, 1024], F32)
            ident = sb(es, "tident", [128, 128], BF16)
            cst = sb(es, "tcst", [128, 128], F32)
            xst = [sb(es, f"tpst{i}", [128, 1024], F32) for i in range(2)]
            S.op("sp", lambda e: e.dma_start(out=cst.t[:, :], in_=K.ident.ap()), writes=[(cst, None)])
            S.op("dve", lambda e: e.tensor_copy(ident.t[:, :], cst.t[:, :]), reads=[(cst, None)], writes=[(ident, None)])
            S.op("sp", lambda e: e.dma_start(out=LNG.t[:, :], in_=K.lng.ap()), writes=[(LNG, None)])
            S.op("sp", lambda e: e.dma_start(out=LNB.t[:, :], in_=K.lnb.ap()), writes=[(LNB, None)])
            load_w(K, Wo, 0, K.w_out.ap().rearrange("(k p) c -> p k c", p=128), 0, 1024, wst, "wo")
            load_w(K, Wg, 0, K.w_pg.ap().rearrange("(k p) c -> p k c", p=128), 0, 1024, wst, "wg")
            load_w(K, Wp, 0, K.w_ple.ap().rearrange("(k p) c -> p k c", p=128), 0, 1024, wst, "wp")
            psrc = K.pT.ap().rearrange("(k p) t -> p k t", p=128)
            for k in range(2):
                for hf in range(2):
                    st = xst[(2 * k + hf) % 2]
                    sl = slice(hf * 1024, (hf + 1) * 1024)
                    S.op("sp", lambda e, st=st, k=k, sl=sl: e.dma_start(out=st.t[:, :], in_=psrc[:, k, sl]), writes=[(st, None)])
                    cast_op(K, pT, pT.t[:, k, sl], (k, hf), st, st.t[:, :], None)
            W2 = 2
            xt = [sb(es, f"txt{i}", [128, 1024], F32) for i in range(W2)]
            hh_s = [sb(es, f"th{i}", [128, 1024], F32) for i in range(W2)]
            hb_s = [sb(es, f"thb{i}", [128, 1024], BF16) for i in range(W2)]
            hT_s = [sb(es, f"thT{i}", [128, 8, 128], BF16) for i in range(W2)]
            gate_s = [sb(es, f"tgate{i}", [128, 1024], F32) for i in range(W2)]
            tt_s = [sb(es, f"ttt{i}", [128, 1024], F32) for i in range(W2)]
            h2_s = [sb(es, f"th2{i}", [128, 1024], F32) for i in range(W2)]
            yy = [sb(es, f"tyy{i}", [128, 1024], F32) for i in range(W2)]
            st1_s = [sb(es, f"tst1{i}", [128, 4], F32) for i in range(W2)]
            xsrc = K.x.ap()

            def tile_gen(i):
                def gen(slot):
                    isl = slice(i * 128, (i + 1) * 128)
                    xb, yb, hh_, hb, hT, gate, tt, h2, st1 = (xt[slot], yy[slot], hh_s[slot], hb_s[slot], hT_s[slot], gate_s[slot],
                                                               tt_s[slot], h2_s[slot], st1_s[slot])
                    cen, sqv = h2, tt
                    bk = PS[4 * slot:4 * slot + 4]
                    S.op("sp", lambda e: e.dma_start(out=xb.t[:, :], in_=xsrc[isl, :]), writes=[(xb, None)])
                    for nh in range(2):
                        ns = slice(nh * 512, (nh + 1) * 512)
                        ps = bk[nh]
                        for k in range(8):
                            S.op("pe", lambda e, ps=ps, k=k, ns=ns: e.matmul(ps.t[:, :], mT.t[:, k, isl], Wo.t[:, k, ns], start=(k == 0), stop=(k == 7)),
                                 reads=[(mT, None), (Wo, None)], writes=[(ps, None)])
                    yield
                    for nh in range(2):
                        ns = slice(nh * 512, (nh + 1) * 512)
                        ps = bk[nh]
                        S.op("dve", lambda e, ps=ps, ns=ns: e.scalar_tensor_tensor(hh_.t[:, ns], xb.t[:, ns], ALPHA, ps.t[:, :], op0=ALU.mult, op1=ALU.add),
                             reads=[(xb, None), (ps, None)], writes=[(hh_, nh)])
                    yield
                    S.op("act", lambda e: e.copy(hb.t[:, :], hh_.t[:, :]), reads=[(hh_, None)], writes=[(hb, None)])
                    yield
                    pst = bk[2]
                    ptb = pst.t.bitcast(BF16)
                    for k in range(8):
                        S.op("pe", lambda e, k=k: e.transpose(ptb[:, k * 128:(k + 1) * 128], hb.t[:, k * 128:(k + 1) * 128], ident.t[:, :]),
                             reads=[(hb, None), (ident, None)], writes=[(pst, None)])
                    yield
                    S.op("act", lambda e: e.copy(hT.t[:, :, :], ptb[:, 0:1024].rearrange("p (k t) -> p k t", t=128)),
                         reads=[(pst, None)], writes=[(hT, None)])
                    yield
                    for nh in range(2):
                        ns = slice(nh * 512, (nh + 1) * 512)
                        pg, pp = bk[nh], bk[2 + nh]
                        for k in range(8):
                            S.op("pe", lambda e, pg=pg, k=k, ns=ns: e.matmul(pg.t[:, :], hT.t[:, k, :], Wg.t[:, k, ns], start=(k == 0), stop=(k == 7)),
                                 reads=[(hT, None), (Wg, None)], writes=[(pg, None)])
                        for k in range(2):
                            S.op("pe", lambda e, pp=pp, k=k, ns=ns: e.matmul(pp.t[:, :], pT.t[:, k, isl], Wp.t[:, k, ns], start=(k == 0), stop=(k == 1)),
                                 reads=[(pT, None), (Wp, None)], writes=[(pp, None)])
                    yield
                    for nh in range(2):
                        ns = slice(nh * 512, (nh + 1) * 512)
                        S.op("act", lambda e, pg=bk[nh], ns=ns: e.activation(gate.t[:, ns], pg.t[:, :], AF.Sigmoid), reads=[(bk[nh], None)], writes=[(gate, nh)])
                    yield
                    for nh in range(2):
                        ns = slice(nh * 512, (nh + 1) * 512)
                        S.op("dve", lambda e, pp=bk[2 + nh], ns=ns: e.tensor_tensor(tt.t[:, ns], gate.t[:, ns], pp.t[:, :], ALU.mult),
                             reads=[(gate, nh), (bk[2 + nh], None)], writes=[(tt, nh)])
                    yield
                    S.op("dve", lambda e: e.tensor_tensor(h2.t[:, :], hh_.t[:, :], tt.t[:, :], ALU.add), reads=[(hh_, None), (tt, None)], writes=[(h2, None)])
                    yield
                    S.op("dve", lambda e: e.tensor_reduce(st1.t[:, 0:1], h2.t[:, :], axis=AX.X, op=ALU.add), reads=[(h2, None)], writes=[(st1, 0)])
                    yield
                    S.op("dve", lambda e: e.tensor_scalar_mul(st1.t[:, 1:2], st1.t[:, 0:1], -1.0 / 1024), reads=[(st1, 0)], writes=[(st1, 1)])
                    yield
                    S.op("act", lambda e: e.activation(cen.t[:, :], h2.t[:, :], AF.Identity, bias=st1.t[:, 1:2]), reads=[(h2, None), (st1, 1)], writes=[(cen, None)])
                    yield
                    S.op("dve", lambda e: e.tensor_tensor(sqv.t[:, :], cen.t[:, :], cen.t[:, :], ALU.mult), reads=[(cen, None)], writes=[(sqv, None)])
                    yield
                    S.op("dve", lambda e: e.tensor_reduce(st1.t[:, 2:3], sqv.t[:, :], axis=AX.X, op=ALU.add), reads=[(sqv, None)], writes=[(st1, 2)])
                    yield
                    S.op("act", lambda e: e.activation(st1.t[:, 3:4], st1.t[:, 2:3], AF.Sqrt, bias=1e-5, scale=1.0 / 1024), reads=[(st1, 2)], writes=[(st1, 3)])
                    yield
                    S.op("dve", lambda e: e.reciprocal(st1.t[:, 3:4], st1.t[:, 3:4]), reads=[(st1, 3)], writes=[(st1, 3)])
                    yield
                    S.op("dve", lambda e: e.scalar_tensor_tensor(yb.t[:, :], cen.t[:, :], st1.t[:, 3:4], LNG.t[:, :], op0=ALU.mult, op1=ALU.mult),
                         reads=[(cen, None), (st1, 3), (LNG, None)], writes=[(yb, None)])
                    yield
                    S.op("dve", lambda e: e.tensor_tensor(yb.t[:, :], yb.t[:, :], LNB.t[:, :], ALU.add), reads=[(yb, None), (LNB, None)], writes=[(yb, None)])
                    yield
                    S.op("sp", lambda e: e.dma_start(out=K.out.ap()[isl, :], in_=yb.t[:, :]), reads=[(yb, None)])
                return gen

            run_interleaved([tile_gen(i) for i in range(NT)], W2)
            S.emit()
```

```python
import math
import itertools
from contextlib import ExitStack

import numpy as np
import concourse.bass as bass
import concourse.mybir as mybir
from concourse.bass_utils import run_bass_kernel_spmd

F32 = mybir.dt.float32
BF16 = mybir.dt.bfloat16
AF = mybir.ActivationFunctionType
ALU = mybir.AluOpType
AX = mybir.AxisListType

ENGS = ("pe", "act", "dve", "pool", "sp")
NCORES = 8
SEQ = 2048
DM = 1024
NT = SEQ // 128
NEG = -30000.0
NH = 2


class Buf:
    def __init__(self, t, name):
        self.t = t
        self.name = name
        self.st = {}
        self.whole = [None, {}]

    def _get(self, key):
        if key not in self.st:
            self.st[key] = [self.whole[0], dict(self.whole[1])]
        return self.st[key]

    def states(self, key):
        if key is None:
            return [self.whole] + list(self.st.values())
        return [self._get(key)]


class Sched:
    def __init__(self, nc, sems, dma_sems):
        self.nc = nc
        self.sem = sems
        self.dma_sems = dma_sems
        self.dma_cnt = [0] * len(dma_sems)
        self.dma_rr = 0
        self.cnt = {e: 0 for e in ENGS}
        self.known = {e: {} for e in ENGS}
        self.barrier = {}
        self.phase_id = 0
        self.q = {e: [] for e in ENGS}

    def op(self, eng, fn, reads=(), writes=()):
        idx = len(self.q[eng])
        deps = set()
        for (b, k) in reads:
            for s in b.states(k):
                if s[0] is not None:
                    deps.add((s[0][0], s[0][1], False))
        for (b, k) in writes:
            for s in b.states(k):
                if s[0] is not None:
                    deps.add((s[0][0], s[0][1], False))
                for e2, ref in s[1].items():
                    deps.add((e2 if isinstance(e2, str) else e2[0], ref, True))
        self.q[eng].append({"fn": fn, "deps": deps, "inc": False})
        ref = (self.phase_id, idx)
        for (b, k) in reads:
            for s in b.states(k):
                s[1][eng if eng != "sp" else ("sp", idx)] = ref
        for (b, k) in writes:
            for s in b.states(k):
                s[0] = (eng, ref)
                s[1].clear()

    def emit(self):
        nc = self.nc
        ph = self.phase_id
        need = {e: [] for e in ENGS}
        for e in ENGS:
            for rec in self.q[e]:
                ws = []
                for (e2, ref, war) in rec["deps"]:
                    if ref[0] != ph:
                        continue
                    if e2 == e and e == "pe":
                        continue
                    ws.append((e2, ref[1]))
                    self.q[e2][ref[1]]["inc"] = True
                need[e].append(ws)
        for e in ENGS:
            if e != "sp" and self.q[e]:
                self.q[e][-1]["inc"] = True
        val = {e: [] for e in ENGS}
        for e in ENGS:
            if e == "sp":
                for rec in self.q[e]:
                    s = self.dma_rr % len(self.dma_sems)
                    self.dma_rr += 1
                    rec["dsem"] = s
                    rec["dprev"] = self.dma_cnt[s]
                    self.dma_cnt[s] += 16
                    val[e].append((("d", s), self.dma_cnt[s]))
            else:
                c = self.cnt[e]
                for rec in self.q[e]:
                    if rec["inc"]:
                        c += 1
                    val[e].append(((e,), c))
                self.cnt[e] = c

        def semobj(key):
            return self.dma_sems[key[1]] if key[0] == "d" else self.sem[key[0]]

        start_bar = dict(self.barrier)

        if getattr(self, "check", False):
            semv = dict(self.chk_sem) if hasattr(self, "chk_sem") else {}
            pos = {e: 0 for e in ENGS}
            prog = True
            while prog:
                prog = False
                for e in ENGS:
                    while pos[e] < len(self.q[e]):
                        idx = pos[e]
                        rec = self.q[e][idx]
                        ok = True
                        for (e2, i2) in need[e][idx]:
                            key, v = val[e2][i2]
                            if semv.get(key, 0) < v:
                                ok = False
                                break
                        if e == "sp" and ok:
                            key = ("d", rec["dsem"])
                            if semv.get(key, 0) < rec["dprev"]:
                                ok = False
                        if not ok:
                            break
                        if e == "sp":
                            key = ("d", rec["dsem"])
                            semv[key] = semv.get(key, 0) + 16
                        elif rec["inc"]:
                            semv[(e,)] = semv.get((e,), 0) + 1
                        pos[e] += 1
                        prog = True
            stuck = {e: (pos[e], len(self.q[e])) for e in ENGS if pos[e] < len(self.q[e])}
            if stuck:
                print("DEADLOCK in phase", ph, stuck)
                for e in stuck:
                    idx = pos[e]
                    print("  ", e, idx, [(e2, i2, val[e2][i2], semv.get(val[e2][i2][0], 0)) for (e2, i2) in need[e][idx]])
            self.chk_sem = semv

        def run_engine(e, engobj):
            known = self.known[e]
            for key, v in start_bar.items():
                if key == (e,):
                    continue
                if v > 0 and known.get(key, 0) < v:
                    engobj.wait_ge(semobj(key), v)
                    known[key] = v
            for idx, rec in enumerate(self.q[e]):
                waits = {}
                for (e2, i2) in need[e][idx]:
                    key, v = val[e2][i2]
                    if waits.get(key, 0) < v:
                        waits[key] = v
                if e == "sp":
                    key = ("d", rec["dsem"])
                    if rec["dprev"] > 0 and waits.get(key, 0) < rec["dprev"]:
                        waits[key] = rec["dprev"]
                for key, v in waits.items():
                    if known.get(key, 0) >= v:
                        continue
                    engobj.wait_ge(semobj(key), v)
                    known[key] = v
                ins = rec["fn"](engobj)
                if e == "sp":
                    ins.then_inc(self.dma_sems[rec["dsem"]], 16)
                elif rec["inc"]:
                    ins.then_inc(self.sem[e], 1)

        with nc.Block() as block:
            @block.tensor
            def _(eng):
                run_engine("pe", eng)

            @block.scalar
            def _(eng):
                run_engine("act", eng)

            @block.vector
            def _(eng):
                run_engine("dve", eng)

            @block.gpsimd
            def _(eng):
                run_engine("pool", eng)

            @block.sync
            def _(eng):
                run_engine("sp", eng)
                for s, c in enumerate(self.dma_cnt):
                    key = ("d", s)
                    if c > 0 and self.known["sp"].get(key, 0) < c:
                        eng.wait_ge(self.dma_sems[s], c)
                        self.known["sp"][key] = c
        self.barrier = {(e,): self.cnt[e] for e in ENGS if e != "sp"}
        for s_, c in enumerate(self.dma_cnt):
            self.barrier[("d", s_)] = c
        self.phase_id += 1
        self.q = {e: [] for e in ENGS}


C_QA = 0
C_KC, C_VC, C_KS, C_VS, C_KW, C_VW = 512, 640, 768, 896, 1024, 1152
C_GA = 1280
C_ZA = 1304
C_QKVB = 1816
C_BETA = 3352
C_AB = 3356
C_ZB = 3360
C_GM = 3872
D_IN = 5920


CONST_SHAPES = {
    "w1k": [128, 32, 256], "w1v": [128, 32, 256], "w2k": [128, 2, 128], "w2v": [128, 2, 64],
    "posk": [128, 32], "posv": [128, 32], "ovl": [127, 32],
    "bsel": [128, 8, 1920], "bwin": [128, 8, 1408], "bcmp": [128, 8, 512], "shc": [128, 4, 127],
    "ident": [128, 128], "e30": [32, 16, 128], "keepm": [128, 16, 32], "addm": [128, 16, 32],
    "convw": [128, 12, 4], "alog": [128, 4], "dtb": [128, 4], "normw": [128, 128],
    "ubd": [128, 128], "slbd": [128, 128], "ones": [128, 128],
    "w_ba": [512, 1024], "w_bb": [512, 1024], "w_out": [1024, 1024], "w_ple": [256, 1024], "w_pg": [1024, 1024],
    "lng": [128, 1024], "lnb": [128, 1024],
}


class Ctx:
    pass


MAXSTAGE = [10 ** 9]


def run_interleaved(items, W):
    pending = list(items)
    active = []
    for s_ in range(W):
        if pending:
            active.append([s_, pending.pop(0)(s_)])
    while active:
        for ent in list(active):
            try:
                ent.append(0) if len(ent) < 3 else None
                ent[2] += 1
                if ent[2] > MAXSTAGE[0]:
                    raise StopIteration
                next(ent[1])
            except StopIteration:
                if len(ent) >= 3:
                    ent[2] = 0
                if pending:
                    ent[1] = pending.pop(0)(ent[0])
                else:
                    active.remove(ent)


def build_program(debug=()):
    nc = bass.Bass("TRN2", target_bir_lowering=False)
    K = Ctx()
    K.nc = nc
    K.debug = set(debug)
    K.dbg_out = {}
    din = {}

    def dram_in(name, shape):
        din[name] = nc.dram_tensor(name, list(shape), F32, kind="ExternalInput")
        return din[name]

    K.xT = dram_in("xT", [DM, SEQ])
    K.pT = dram_in("pT", [256, SEQ])
    K.x = dram_in("x", [SEQ, DM])
    K.w_in = dram_in("w_in", [DM, D_IN])
    K.out = nc.dram_tensor("out", [SEQ, DM], F32, kind="ExternalOutput")
    for nm, shp in CONST_SHAPES.items():
        setattr(K, nm, dram_in(nm, shp))

    def dbg(name, shape):
        t = nc.dram_tensor("dbg_" + name, list(shape), F32, kind="ExternalOutput")
        K.dbg_out[name] = t
        return t

    K.dbg = dbg

    with ExitStack() as es:
        sems = {e: es.enter_context(nc.semaphore("s_" + e)) for e in ENGS if e != "sp"}
        dsems = [es.enter_context(nc.semaphore(f"dq{i}")) for i in range(8)]
        S = Sched(nc, sems, dsems)
        K.S = S

        def sb(stack, name, shape, dt):
            return Buf(stack.enter_context(nc.sbuf_tensor(name, list(shape), dt)), name)

        K.sb = sb
        K.PS = [Buf(es.enter_context(nc.psum_tensor(f"ps{i}", [128, 512], F32)), f"ps{i}") for i in range(8)]
        K.cast_rr = itertools.cycle(["dve", "act"])

        K.oaT = sb(es, "oaT", [128, 4, SEQ], BF16)
        K.obT = sb(es, "obT", [128, 4, SEQ], BF16)
        if "skip_nsa" not in K.debug:
            phase_nsa(K, es)
        phase_dn(K)
        phase_tail(K)
    return nc, K


def cast_op(K, dst_buf, dst_ap, dkey, src_buf, src_ap, skey, eng=None, scale=None):
    S = K.S
    eng = eng or next(K.cast_rr)
    if eng == "act":
        if scale is None:
            fn = lambda e: e.copy(dst_ap, src_ap)
        else:
            fn = lambda e: e.mul(dst_ap, src_ap, scale)
    else:
        if scale is None:
            fn = lambda e: e.tensor_copy(dst_ap, src_ap)
        else:
            fn = lambda e: e.tensor_scalar_mul(dst_ap, src_ap, scale)
    S.op(eng, fn, reads=[(src_buf, skey)], writes=[(dst_buf, dkey)])


def load_xT(K, xT, xst):
    S = K.S
    src = K.xT.ap().rearrange("(k p) t -> p k t", p=128)
    for hf in range(2):
        for k in range(8):
            st = xst[k % len(xst)]
            sl = slice(hf * 1024, (hf + 1) * 1024)
            S.op("sp", lambda e, st=st, k=k, sl=sl: e.dma_start(out=st.t[:, :], in_=src[:, k, sl]), writes=[(st, None)])
            cast_op(K, xT, xT.t[:, k, sl], (k, hf), st, st.t[:, :], None)


def load_w(K, wdst, dcol, wsrc_ap, c0, n, wst, wkey):
    S = K.S
    kc = wsrc_ap.shape[1]
    step = wst[0].t.shape[2]
    for o in range(0, n, step):
        m = min(step, n - o)
        st = wst[K.wst_rr % len(wst)]
        K.wst_rr += 1
        S.op("sp", lambda e, st=st, o=o, m=m: e.dma_start(out=st.t[:, 0:kc, 0:m], in_=wsrc_ap[:, :, c0 + o:c0 + o + m]),
             writes=[(st, None)])
        cast_op(K, wdst, wdst.t[:, 0:kc, dcol + o:dcol + o + m], (wkey, o), st, st.t[:, 0:kc, 0:m], None)


def phase_nsa(K, es_outer):
    nc, S, sb = K.nc, K.S, K.sb
    with ExitStack() as es:
        K.QT = sb(es, "QT", [128, 4, SEQ], BF16)
        K.KsT2 = sb(es, "KsT2", [128, 2, SEQ], BF16)
        K.KwT2 = sb(es, "KwT2", [128, 2, SEQ], BF16)
        K.KcT = sb(es, "KcT", [128, SEQ], BF16)
        K.VcT = sb(es, "VcT", [128, SEQ], BF16)
        K.VS = sb(es, "VS", [128, NT, 2, 65], BF16)
        K.VW = sb(es, "VW", [128, NT, 2, 65], BF16)
        K.GA = sb(es, "GA", [128, NT, 24], F32)
        K.ZA = sb(es, "ZA", [128, NT, 512], BF16)
        K.KCT2 = sb(es, "KCT2", [128, 2, 127], BF16)
        K.VCX = sb(es, "VCX", [128, 2, 97], BF16)
        phase_nsa_proj(K)
        phase_cmp(K)
        phase_attn(K)


def phase_nsa_proj(K):
    nc, S, sb, PS = K.nc, K.S, K.sb, K.PS
    w_in = K.w_in.ap().rearrange("(k p) c -> p k c", p=128)
    with ExitStack() as es:
        xT = sb(es, "xT_bf", [128, 8, SEQ], BF16)
        xst = [sb(es, f"xst{i}", [128, 1024], F32) for i in range(2)]
        wst = [sb(es, f"wst{i}", [128, 8, 256], F32) for i in range(2)]
        K.wst_rr = 0
        wq = sb(es, "wq", [128, 8, 512], BF16)
        wkd = sb(es, "wkd", [128, 8, 6 * 128], BF16)
        wtok = sb(es, "wtok", [128, 8, 280], BF16)
        wz = sb(es, "wz", [128, 8, 512], BF16)
        S.op("pool", lambda e: e.memset(K.VS.t[:, :, :, 64:65], 1.0), writes=[(K.VS, "ones")])
        S.op("pool", lambda e: e.memset(K.VW.t[:, :, :, 64:65], 1.0), writes=[(K.VW, "ones")])
        load_w(K, wq, 0, w_in, C_QA, 512, wst, "q")
        load_xT(K, xT, xst)
        gi = 0
        for base in (C_KS, C_KW):
            for g in range(2):
                for dup in range(2):
                    load_w(K, wkd, gi * 128 + dup * 64, w_in, base + g * 64, 64, wst, ("kd", gi, dup))
                gi += 1
        load_w(K, wkd, 4 * 128, w_in, C_KC, 128, wst, "kc")
        load_w(K, wkd, 5 * 128, w_in, C_VC, 128, wst, "vc")
        load_w(K, wtok, 0, w_in, C_VS, 128, wst, "vs")
        load_w(K, wtok, 128, w_in, C_VW, 128, wst, "vw")
        load_w(K, wtok, 256, w_in, C_GA, 24, wst, "ga")
        load_w(K, wz, 0, w_in, C_ZA, 512, wst, "za")

        psi = itertools.count()
        fm = []
        for j in range(4):
            fm.append((wq, j * 128, ("q", j)))
        for gi in range(6):
            fm.append((wkd, gi * 128, ("k", gi)))
        for (wb, c0, cons) in fm:
            for t in range(4):
                ps = PS[next(psi) % 8]
                tsl = slice(t * 512, (t + 1) * 512)
                for k in range(8):
                    S.op("pe", lambda e, ps=ps, wb=wb, c0=c0, k=k, tsl=tsl: e.matmul(
                        ps.t[:, :], wb.t[:, k, c0:c0 + 128], xT.t[:, k, tsl], start=(k == 0), stop=(k == 7)),
                        reads=[(wb, None), (xT, (k, tsl.start // 1024))], writes=[(ps, None)])
                if cons[0] == "q":
                    dst, dap, scale = K.QT, K.QT.t[:, cons[1], tsl], 0.125
                else:
                    gi = cons[1]
                    if gi < 2:
                        dst, dap = K.KsT2, K.KsT2.t[:, gi, tsl]
                    elif gi < 4:
                        dst, dap = K.KwT2, K.KwT2.t[:, gi - 2, tsl]
                    elif gi == 4:
                        dst, dap = K.KcT, K.KcT.t[:, tsl]
                    else:
                        dst, dap = K.VcT, K.VcT.t[:, tsl]
                    scale = None
                eng = "act" if (t % 2 == 0) else "dve"
                cast_op(K, dst, dap, (cons, t), ps, ps.t[:, :], None, eng=eng, scale=scale)
        for i in range(NT):
            isl = slice(i * 128, (i + 1) * 128)
            ps = PS[next(psi) % 8]
            for k in range(8):
                S.op("pe", lambda e, ps=ps, k=k, isl=isl: e.matmul(
                    ps.t[:, 0:280], xT.t[:, k, isl], wtok.t[:, k, 0:280], start=(k == 0), stop=(k == 7)),
                    reads=[(wtok, None), (xT, (k, isl.start // 1024))], writes=[(ps, None)])
            S.op("dve", lambda e, ps=ps, i=i: e.tensor_copy(
                K.VS.t[:, i, :, 0:64], ps.t[:, 0:128].rearrange("p (g d) -> p g d", g=2)),
                reads=[(ps, None)], writes=[(K.VS, i)])
            S.op("dve", lambda e, ps=ps, i=i: e.tensor_copy(
                K.VW.t[:, i, :, 0:64], ps.t[:, 128:256].rearrange("p (g d) -> p g d", g=2)),
                reads=[(ps, None)], writes=[(K.VW, i)])
            S.op("dve", lambda e, ps=ps, i=i: e.tensor_copy(K.GA.t[:, i, :], ps.t[:, 256:280]),
                 reads=[(ps, None)], writes=[(K.GA, i)])
            ps2 = PS[next(psi) % 8]
            for k in range(8):
                S.op("pe", lambda e, ps2=ps2, k=k, isl=isl: e.matmul(
                    ps2.t[:, :], xT.t[:, k, isl], wz.t[:, k, :], start=(k == 0), stop=(k == 7)),
                    reads=[(wz, None), (xT, (k, isl.start // 1024))], writes=[(ps2, None)])
            S.op("act", lambda e, ps2=ps2, i=i: e.activation(K.ZA.t[:, i, :], ps2.t[:, :], AF.Silu),
                 reads=[(ps2, None)], writes=[(K.ZA, i)])
        S.op("act", lambda e: e.activation(K.GA.t[:, :, :], K.GA.t[:, :, :], AF.Sigmoid), reads=[(K.GA, None)], writes=[(K.GA, None)])
        if "p1" in K.debug:
            with ExitStack() as es2:
                tmp = sb(es2, "dbgtmp", [128, SEQ], F32)
                for j in range(4):
                    dump(K, tmp, K.QT, K.QT.t[:, j, :], f"QT{j}")
                for g in range(2):
                    dump(K, tmp, K.KsT2, K.KsT2.t[:, g, :], f"KsT{g}")
                    dump(K, tmp, K.KwT2, K.KwT2.t[:, g, :], f"KwT{g}")
                dump(K, tmp, K.KcT, K.KcT.t[:, :], "KcT")
                dump(K, tmp, K.VcT, K.VcT.t[:, :], "VcT")
                dump(K, tmp, K.VS, K.VS.t[:, :, :, :].rearrange("p a b c -> p (a b c)"), "VS")
                dump(K, tmp, K.VW, K.VW.t[:, :, :, :].rearrange("p a b c -> p (a b c)"), "VW")
                dump(K, tmp, K.GA, K.GA.t[:, :, :].rearrange("p a b -> p (a b)"), "GA")
                dump(K, tmp, K.ZA, K.ZA.t[:, :, :].rearrange("p a b -> p (a b)"), "ZA")
                S.emit()
        else:
            S.emit()


def dump(K, tmp, buf, ap, name):
    S = K.S
    P, n = ap.shape[0], ap.shape[1]
    d = K.dbg(name, [P, n])
    W = tmp.t.shape[1]
    for o in range(0, n, W):
        m = min(W, n - o)
        S.op("dve", lambda e, o=o, m=m: e.tensor_copy(tmp.t[0:P, 0:m], ap[:, o:o + m]), reads=[(buf, None)], writes=[(tmp, None)])
        S.op("sp", lambda e, o=o, m=m: e.dma_start(out=d.ap()[:, o:o + m], in_=tmp.t[0:P, 0:m]), reads=[(tmp, None)])


def host_inputs(inputs, b):
    x = np.ascontiguousarray(inputs["x"][b])
    m = {
        "xT": np.ascontiguousarray(x.T),
        "x": x,
        "pT": np.ascontiguousarray(inputs["p"][0, b].T),
        "w_in": np.ascontiguousarray(inputs["w_in"][0]),
    }
    if "_consts" not in inputs:
        inputs["_consts"] = host_consts(inputs)
    m.update(inputs["_consts"])
    return m


_CACHE = {}


def kernel(**inputs):
    inputs = {k: np.asarray(v) for k, v in inputs.items()}
    if "nc" not in _CACHE:
        _CACHE["nc"] = build_program()[0]
    nc = _CACHE["nc"]
    in_maps = [host_inputs(inputs, b) for b in range(NCORES)]
    res = run_bass_kernel_spmd(nc, in_maps, core_ids=list(range(NCORES)))
    out = np.stack([np.asarray(r["out"]) for r in res.results], axis=0)
    return out.astype(np.float32)


def _bucket(dist):
    n = np.maximum(dist, 0)
    nf = np.maximum(n, 1).astype(np.float32)
    large = 16 + (np.log(nf / np.float32(16)) / np.float32(math.log(64)) * np.float32(16)).astype(np.int32)
    large = np.minimum(large, 31)
    return np.where(n < 16, n, large)


def host_consts(inputs):
    c = {}
    w1k = inputs["cmp_w1_k"][0].reshape(32, 64, 256).transpose(1, 0, 2)
    w1v = inputs["cmp_w1_v"][0].reshape(32, 64, 256).transpose(1, 0, 2)
    c["w1k"] = np.concatenate([w1k, w1k], 0)
    c["w1v"] = np.concatenate([w1v, w1v], 0)
    w2k = inputs["cmp_w2_k"][0]
    c["w2k"] = np.concatenate([w2k, w2k], 1).reshape(2, 128, 128).transpose(1, 0, 2)
    c["w2v"] = inputs["cmp_w2_v"][0].reshape(2, 128, 64).transpose(1, 0, 2)
    pk = inputs["cmp_pos_k"][0].T
    pv = inputs["cmp_pos_v"][0].T
    c["posk"] = np.concatenate([pk, pk], 0)
    c["posv"] = np.concatenate([pv, pv], 0)
    ci = np.arange(127)[:, None]
    sj = np.arange(32)[None, :]
    c["ovl"] = ((ci * 16 < (sj + 1) * 64) & (ci * 16 + 32 > sj * 64)).astype(np.float32)
    tab = inputs["rel_bias"].astype(np.float32)
    kp = np.arange(128)[:, None]

    def toep(width, win):
        n = np.arange(width)[None, :] - 384 - kp
        g = tab[_bucket(n)]
        bad = (n < 0) | ((n >= 512) if win else False)
        g = np.where(bad[:, :, None], np.float32(NEG), g)
        return np.ascontiguousarray(g.transpose(0, 2, 1))

    c["bsel"] = toep(1920, False)
    c["bwin"] = toep(1408, True)
    r = np.arange(127)[:, None]
    xx = np.arange(512)[None, :]
    n = xx - 16 * (r - 96) - 31
    g = np.where((n < 0)[:, :, None], np.float32(NEG), tab[_bucket(n)])
    g = np.concatenate([g.transpose(0, 2, 1), np.full((1, 8, 512), NEG, np.float32)], 0)
    c["bcmp"] = np.ascontiguousarray(g)
    shc = np.zeros((128, 4, 127), np.float32)
    for cc in range(4):
        for nn in range(127):
            p = nn + 96 - 32 * cc
            if nn < min(127, 32 * (cc + 1) - 1):
                shc[p, cc, nn] = 1.0
            else:
                shc[127, cc, nn] = 1.0
    c["shc"] = shc
    c["ident"] = np.eye(128, dtype=np.float32)
    e30 = np.zeros((32, 16, 128), np.float32)
    for kt in range(16):
        for p in range(128):
            e30[2 * kt + p // 64, kt, p] = 30000.0
    c["e30"] = e30
    q = np.arange(SEQ)
    cur = (q // 64)[:, None]
    blk = np.arange(32)[None, :]
    forced = (blk == 0) | (blk == cur) | (blk == cur - 1)
    fut = blk > cur
    keep = (~(forced | fut)).astype(np.float32)
    addm = np.where(forced, 1e6, np.where(fut, -1e6, 0.0)).astype(np.float32)
    c["keepm"] = np.ascontiguousarray(keep.reshape(16, 128, 32).transpose(1, 0, 2))
    c["addm"] = np.ascontiguousarray(addm.reshape(16, 128, 32).transpose(1, 0, 2))
    c["convw"] = inputs["dn_conv_w"][0].T.reshape(12, 128, 4).transpose(1, 0, 2)
    c["alog"] = np.broadcast_to(inputs["dn_a_log"][0][None, :], (128, 4))
    c["dtb"] = np.broadcast_to(inputs["dn_dt_bias"][0][None, :], (128, 4))
    c["normw"] = np.broadcast_to(inputs["dn_norm_w"][0][None, :], (128, 128))
    t = np.arange(128)
    same = (t[:, None] // 64) == (t[None, :] // 64)
    c["ubd"] = ((t[:, None] <= t[None, :]) & same).astype(np.float32)
    c["slbd"] = ((t[:, None] > t[None, :]) & same).astype(np.float32)
    c["ones"] = np.ones((128, 128), np.float32)
    c["w_ba"] = inputs["w_branch_a"][0]
    c["w_bb"] = inputs["w_branch_b"][0]
    c["w_out"] = inputs["w_out"][0]
    c["w_ple"] = inputs["w_ple"][0]
    c["w_pg"] = inputs["w_ple_gate"][0]
    c["lng"] = np.broadcast_to(inputs["ln_g"][0][None, :], (128, 1024))
    c["lnb"] = np.broadcast_to(inputs["ln_b"][0][None, :], (128, 1024))
    return {k: np.ascontiguousarray(v, dtype=np.float32) for k, v in c.items()}


def load_cast(K, dst, dst_ap, dkey, src_ap, stg, P, n, eng=None):
    S = K.S
    W = stg[0].t.shape[1]
    for o in range(0, n, W):
        m = min(W, n - o)
        st = stg[K.stg_rr % len(stg)]
        K.stg_rr += 1
        S.op("sp", lambda e, st=st, o=o, m=m: e.dma_start(out=st.t[0:P, 0:m], in_=src_ap[:, o:o + m]), writes=[(st, None)])
        cast_op(K, dst, dst_ap[:, o:o + m], (dkey, o), st, st.t[0:P, 0:m], None, eng=eng)


def phase_cmp(K):
    nc, S, sb, PS = K.nc, K.S, K.sb, K.PS
    with ExitStack() as es:
        stg = [sb(es, f"cstg{i}", [128, 2048], F32) for i in range(2)]
        K.stg_rr = 0
        w1 = [sb(es, "w1k_s", [128, 32 * 256], BF16), sb(es, "w1v_s", [128, 32 * 256], BF16)]
        w2k = sb(es, "w2k_s", [128, 256], BF16)
        w2v = sb(es, "w2v_s", [128, 128], BF16)
        pos = [sb(es, "posk_s", [128, 32], BF16), sb(es, "posv_s", [128, 32], BF16)]
        ovl = sb(es, "ovl_s", [128, 32], BF16)
        pb = sb(es, "pb", [128, 4], F32)
        gel = sb(es, "gel", [128, 8, 127], BF16)
        load_cast(K, w1[0], w1[0].t[:, :], "w", K.w1k.ap().rearrange("p i h -> p (i h)"), stg, 128, 8192)
        load_cast(K, w1[1], w1[1].t[:, :], "w", K.w1v.ap().rearrange("p i h -> p (i h)"), stg, 128, 8192)
        load_cast(K, w2k, w2k.t[:, :], "w", K.w2k.ap().rearrange("p a b -> p (a b)"), stg, 128, 256)
        load_cast(K, w2v, w2v.t[:, :], "w", K.w2v.ap().rearrange("p a b -> p (a b)"), stg, 128, 128)
        load_cast(K, pos[0], pos[0].t[:, :], "w", K.posk.ap(), stg, 128, 32)
        load_cast(K, pos[1], pos[1].t[:, :], "w", K.posv.ap(), stg, 128, 32)
        load_cast(K, ovl, ovl.t[0:127, :], "w", K.ovl.ap(), stg, 127, 32)
        psi = itertools.count()
        srcs = [K.KcT, K.VcT]
        for kv in range(2):
            for half in range(2):
                ps = PS[next(psi) % 8]
                hs = slice(half * 128, (half + 1) * 128)
                for i in range(32):
                    S.op("pe", lambda e, ps=ps, kv=kv, i=i, half=half: e.matmul(
                        ps.t[:, 0:1], w1[kv].t[0:64, i * 256 + half * 128:i * 256 + half * 128 + 128], pos[kv].t[0:64, i:i + 1],
                        start=(i == 0), stop=(i == 31)), reads=[(w1[kv], ("w", (i * 256) // 2048 * 2048)), (pos[kv], None)], writes=[(ps, None)])
                col = kv * 2 + half
                S.op("dve", lambda e, ps=ps, col=col: e.tensor_copy(pb.t[:, col:col + 1], ps.t[:, 0:1]),
                     reads=[(ps, None)], writes=[(pb, col)])
                for g in range(2):
                    ps = PS[next(psi) % 8]
                    gs = slice(g * 64, (g + 1) * 64)
                    for i in range(32):
                        S.op("pe", lambda e, ps=ps, kv=kv, i=i, half=half, gs=gs: e.matmul(
                            ps.t[:, 0:127], w1[kv].t[gs, i * 256 + half * 128:i * 256 + half * 128 + 128],
                            srcs[kv].t[gs, i:i + 2017:16], start=(i == 0), stop=(i == 31)),
                            reads=[(w1[kv], ("w", (i * 256) // 2048 * 2048)), (srcs[kv], None)], writes=[(ps, None)])
                    gi = (kv * 2 + half) * 2 + g
                    S.op("act", lambda e, ps=ps, gi=gi, col=col: e.activation(
                        gel.t[:, gi, :], ps.t[:, 0:127], AF.Gelu, bias=pb.t[:, col:col + 1]),
                        reads=[(ps, None), (pb, col)], writes=[(gel, gi)])
        for g in range(2):
            ps = PS[next(psi) % 8]
            for half in range(2):
                gi = (0 * 2 + half) * 2 + g
                S.op("pe", lambda e, ps=ps, half=half, gi=gi: e.matmul(
                    ps.t[:, 0:127], w2k.t[:, half * 128:(half + 1) * 128], gel.t[:, gi, :], start=(half == 0), stop=(half == 1)),
                    reads=[(w2k, None), (gel, gi)], writes=[(ps, None)])
            S.op("dve", lambda e, ps=ps, g=g: e.tensor_copy(K.KCT2.t[:, g, :], ps.t[:, 0:127]),
                 reads=[(ps, None)], writes=[(K.KCT2, g)])
            ps = PS[next(psi) % 8]
            for half in range(2):
                gi = (1 * 2 + half) * 2 + g
                S.op("pe", lambda e, ps=ps, half=half, gi=gi: e.matmul(
                    ps.t[0:127, 0:64], gel.t[:, gi, :], w2v.t[:, half * 64:(half + 1) * 64], start=(half == 0), stop=(half == 1)),
                    reads=[(w2v, None), (gel, gi)], writes=[(ps, None)])
            S.op("dve", lambda e, ps=ps, g=g: e.tensor_copy(K.VCX.t[0:127, g, 33:97], ps.t[0:127, 0:64]),
                 reads=[(ps, None)], writes=[(K.VCX, ("v", g))])
            S.op("pool", lambda e, g=g: e.tensor_copy(K.VCX.t[0:127, g, 0:32], ovl.t[0:127, :]),
                 reads=[(ovl, None)], writes=[(K.VCX, ("o", g))])
            S.op("pool", lambda e, g=g: e.memset(K.VCX.t[0:127, g, 32:33], 1.0), writes=[(K.VCX, ("1", g))])
        if "p2" in K.debug:
            with ExitStack() as es2:
                tmp = sb(es2, "dbgtmp2", [128, 512], F32)
                dump(K, tmp, K.KCT2, K.KCT2.t[:, :, :].rearrange("p a b -> p (a b)"), "KCT2")
                dump(K, tmp, K.VCX, K.VCX.t[0:127, :, :].rearrange("p a b -> p (a b)"), "VCX")
                S.emit()
        else:
            S.emit()


def phase_attn(K):
    nc, S, sb, PS = K.nc, K.S, K.sb, K.PS
    with ExitStack() as es:
        stg = [sb(es, f"astg{i}", [128, 2048], F32) for i in range(2)]
        K.stg_rr = 0
        ident = sb(es, "ident_s", [128, 128], BF16)
        BSel = sb(es, "BSel", [128, 8 * 1920], BF16)
        BWin = sb(es, "BWin", [128, 8 * 1408], BF16)
        BC = sb(es, "BC", [128, 8 * 512], BF16)
        SHC = sb(es, "SHC", [128, 4 * 127], BF16)
        E30 = sb(es, "E30", [32, 16 * 128], BF16)
        KEEP = sb(es, "KEEP", [128, 16 * 32], F32)
        ADDM = sb(es, "ADDM", [128, 16 * 32], F32)
        load_cast(K, ident, ident.t[:, :], "c", K.ident.ap(), stg, 128, 128)
        load_cast(K, BSel, BSel.t[:, :], "c", K.bsel.ap().rearrange("p a b -> p (a b)"), stg, 128, 8 * 1920)
        load_cast(K, BWin, BWin.t[:, :], "c", K.bwin.ap().rearrange("p a b -> p (a b)"), stg, 128, 8 * 1408)
        load_cast(K, BC, BC.t[:, :], "c", K.bcmp.ap().rearrange("p a b -> p (a b)"), stg, 128, 8 * 512)
        load_cast(K, SHC, SHC.t[:, :], "c", K.shc.ap().rearrange("p a b -> p (a b)"), stg, 128, 4 * 127)
        load_cast(K, E30, E30.t[:, :], "c", K.e30.ap().rearrange("p a b -> p (a b)"), stg, 32, 16 * 128)
        for (tb_, n_) in ((BSel, 8 * 1920), (BWin, 8 * 1408)):
            for o in range(0, n_, 2048):
                m_ = min(2048, n_ - o)
                S.op("act", lambda e, tb_=tb_, o=o, m_=m_: e.activation(tb_.t[:, o:o + m_], tb_.t[:, o:o + m_], AF.Exp),
                     reads=[(tb_, ("c", o))], writes=[(tb_, ("c", o))])
        S.op("sp", lambda e: e.dma_start(out=KEEP.t[:, :], in_=K.keepm.ap().rearrange("p a b -> p (a b)")), writes=[(KEEP, None)])
        S.op("sp", lambda e: e.dma_start(out=ADDM.t[:, :], in_=K.addm.ap().rearrange("p a b -> p (a b)")), writes=[(ADDM, None)])

        MTm1 = sb(es, "MTm1", [32, 2, 512], BF16)
        impacc = sb(es, "impacc", [128, 2, 4, 32], F32)
        imptmp = sb(es, "imptmp", [128, 4, 32], F32)
        imp2 = sb(es, "imp2", [128, 4, 32], F32)
        top8 = sb(es, "top8", [128, 8], F32)
        msk = sb(es, "msk", [128, 32], BF16)
        PT = [sb(es, f"PT{i}", [128, 512], BF16) for i in range(4)]
        oacc = sb(es, "oacc", [128, 4, 512], F32)
        otmp = sb(es, "otmp", [128, 4, 64], F32)
        rec = [sb(es, f"rec{i}", [128, 4], F32) for i in range(2)]
        coef = [sb(es, f"coef{i}", [128, 4], F32) for i in range(2)]
        oab = sb(es, "oab", [128, 512], BF16)
        PSs = [PS[0], PS[1], PS[2], PS[3]]
        PSo = [PS[4], PS[5], PS[6]]
        PSm = PS[7]
        si = itertools.count()
        oi = itertools.count()
        ri = itertools.count()
        ACT_EXP = AF.Exp

        def finish_branch(ps_o, W, dcol, imp_col, br, h, c, first):
            r = rec[next(ri) % 2]
            cf = coef[(next(ri)) % 2]
            pv = ps_o.t[:, 0:4 * W].rearrange("p (i w) -> p i w", w=W)
            S.op("dve", lambda e: e.tensor_scalar(r.t[:, :], pv[:, :, dcol:dcol + 1].rearrange("p i o -> p (i o)"), 1e-30, None, op0=ALU.max),
                 reads=[(ps_o, None)], writes=[(r, None)])
            S.op("dve", lambda e: e.reciprocal(r.t[:, :], r.t[:, :]), reads=[(r, None)], writes=[(r, None)])
            if imp_col is not None:
                g, hh = h // 4, h % 4
                rb = r.t[:, :].unsqueeze(2).to_broadcast([128, 4, 32])
                if hh == 0:
                    S.op("dve", lambda e: e.tensor_tensor(impacc.t[:, g, :, :], pv[:, :, 0:32], rb, ALU.mult),
                         reads=[(ps_o, None), (r, None)], writes=[(impacc, g)])
                else:
                    S.op("dve", lambda e: e.tensor_tensor(imptmp.t[:, :, :], pv[:, :, 0:32], rb, ALU.mult),
                         reads=[(ps_o, None), (r, None)], writes=[(imptmp, None)])
                    S.op("pool", lambda e: e.tensor_tensor(impacc.t[:, g, :, :], impacc.t[:, g, :, :], imptmp.t[:, :, :], ALU.add),
                         reads=[(impacc, g), (imptmp, None)], writes=[(impacc, g)])
            S.op("dve", lambda e: e.tensor_tensor(cf.t[:, :], r.t[:, :], K.GA.t[:, 4 * c:4 * c + 4, br * 8 + h], ALU.mult),
                 reads=[(r, None), (K.GA, None)], writes=[(cf, None)])
            cb = cf.t[:, :].unsqueeze(2).to_broadcast([128, 4, 64])
            ocol = 33 if imp_col is not None else 0
            dst = oacc.t[:, :, h * 64:(h + 1) * 64]
            if first:
                S.op("dve", lambda e: e.tensor_tensor(dst, pv[:, :, ocol:ocol + 64], cb, ALU.mult),
                     reads=[(ps_o, None), (cf, None)], writes=[(oacc, h)])
            else:
                S.op("dve", lambda e: e.tensor_tensor(otmp.t[:, :, :], pv[:, :, ocol:ocol + 64], cb, ALU.mult),
                     reads=[(ps_o, None), (cf, None)], writes=[(otmp, None)])
                S.op("pool", lambda e: e.tensor_tensor(dst, dst, otmp.t[:, :, :], ALU.add),
                     reads=[(oacc, h), (otmp, None)], writes=[(oacc, h)])

        DEPTH = 3

        def pipeline(jobs):
            n = len(jobs)
            for idx in range(n + DEPTH):
                if idx < n:
                    jobs[idx][0]()
                if idx - DEPTH >= 0:
                    jobs[idx - DEPTH][1]()

        for c in range(4):
            csl = slice(c * 512, (c + 1) * 512)
            NC = 127
            jobs = []
            for h in range(8):
                def mk(h=h, c=c, csl=csl, NC=NC):
                    g, hp = h // 4, (h % 2) * 64
                    st = {}

                    def scores():
                        k_ = next(si)
                        ps_s = PSs[k_ % 4]
                        pt = PT[k_ % 4]
                        st["pt"] = pt
                        S.op("pe", lambda e: e.matmul(ps_s.t[0:NC, :], K.KCT2.t[hp:hp + 64, g, 0:NC], K.QT.t[hp:hp + 64, h // 2, csl], start=True, stop=False),
                             reads=[(K.KCT2, None), (K.QT, None)], writes=[(ps_s, None)])
                        S.op("pe", lambda e: e.matmul(ps_s.t[0:NC, :], SHC.t[:, c * 127:c * 127 + NC], BC.t[:, h * 512:(h + 1) * 512], start=False, stop=True),
                             reads=[(SHC, None), (BC, None)], writes=[(ps_s, None)])
                        S.op("act", lambda e: e.activation(pt.t[0:NC, :], ps_s.t[0:NC, :], ACT_EXP), reads=[(ps_s, None)], writes=[(pt, None)])

                    def pv():
                        pt = st["pt"]
                        ps_o = PSo[next(oi) % 3]
                        for i in range(4):
                            S.op("pe", lambda e, i=i: e.matmul(ps_o.t[:, i * 97:(i + 1) * 97], pt.t[0:NC, i * 128:(i + 1) * 128], K.VCX.t[0:NC, g, :],
                                                              start=True, stop=True),
                                 reads=[(pt, None), (K.VCX, None)], writes=[(ps_o, None)])
                        finish_branch(ps_o, 97, 32, 0, 0, h, c, True)
                    return (scores, pv)
                jobs.append(mk())
            pipeline(jobs)
            if "p3x" in K.debug and c == 0:
                K.dtmp = sb(es, "dbgtmp4", [128, 1024], F32)
                dump(K, K.dtmp, oacc, oacc.t[:, :, :].rearrange("p a b -> p (a b)"), "oacc_cmp")
                dump(K, K.dtmp, impacc, impacc.t[:, :, :, :].rearrange("p a b c -> p (a b c)"), "impacc")
            for g in range(2):
                S.op("dve", lambda e, g=g, c=c: e.tensor_tensor(imp2.t[:, :, :], impacc.t[:, g, :, :],
                                                               KEEP.t[:, 4 * c * 32:(4 * c + 4) * 32].rearrange("p (i j) -> p i j", j=32), ALU.mult),
                     reads=[(impacc, g), (KEEP, None)], writes=[(imp2, None)])
                S.op("dve", lambda e, c=c: e.tensor_tensor(imp2.t[:, :, :], imp2.t[:, :, :],
                                                          ADDM.t[:, 4 * c * 32:(4 * c + 4) * 32].rearrange("p (i j) -> p i j", j=32), ALU.add),
                     reads=[(imp2, None), (ADDM, None)], writes=[(imp2, None)])
                for i in range(4):
                    S.op("dve", lambda e, i=i: e.max(top8.t[:, :], imp2.t[:, i, :]), reads=[(imp2, None)], writes=[(top8, None)])
                    S.op("dve", lambda e, i=i: e.tensor_scalar(msk.t[:, :], imp2.t[:, i, :], top8.t[:, 7:8], -1.0, op0=ALU.is_ge, op1=ALU.add),
                         reads=[(imp2, None), (top8, None)], writes=[(msk, None)])
                    psb = PSm.t.bitcast(BF16)
                    S.op("pe", lambda e, psb=psb: e.transpose(psb[0:32, 0:128], msk.t[:, :], ident.t[:, :]),
                         reads=[(msk, None), (ident, None)], writes=[(PSm, None)])
                    S.op("act", lambda e, psb=psb, g=g, i=i: e.copy(MTm1.t[0:32, g, i * 128:(i + 1) * 128], psb[0:32, 0:128]),
                         reads=[(PSm, None)], writes=[(MTm1, (g, i))])
            jobs = []
            for br in (2, 1):
                for h in range(8):
                    g, hp = h // 4, (h % 2) * 64
                    if br == 1:
                        kts = list(range(0, 4 * c + 4))
                    else:
                        kts = list(range(max(0, 4 * c - 4), 4 * c + 4))
                    shared = {"first": True}
                    for kt in kts:
                        def mk(br=br, h=h, g=g, hp=hp, kt=kt, c=c, csl=csl, shared=shared, kts=kts):
                            st = {}

                            def scores():
                                k_ = next(si)
                                ps_s = PSs[k_ % 4]
                                pt = PT[k_ % 4]
                                st["pt"] = pt
                                ksl = slice(kt * 128, (kt + 1) * 128)
                                KT = K.KsT2 if br == 1 else K.KwT2
                                S.op("pe", lambda e: e.matmul(ps_s.t[:, :], KT.t[hp:hp + 64, g, ksl], K.QT.t[hp:hp + 64, h // 2, csl], start=True, stop=(br == 2)),
                                     reads=[(KT, None), (K.QT, None)], writes=[(ps_s, None)])
                                d0 = 512 * c - 128 * kt
                                if br == 1:
                                    off = h * 1920 + min(d0, 1024) + 384
                                    EB = BSel
                                    S.op("pe", lambda e: e.matmul(ps_s.t[:, :], E30.t[0:32, kt * 128:(kt + 1) * 128], MTm1.t[0:32, g, :], start=False, stop=True),
                                         reads=[(E30, None), (MTm1, None)], writes=[(ps_s, None)])
                                else:
                                    off = h * 1408 + d0 + 384
                                    EB = BWin
                                S.op("act", lambda e: e.activation(pt.t[:, :], ps_s.t[:, :], ACT_EXP), reads=[(ps_s, None)], writes=[(pt, None)])
                                S.op("dve", lambda e: e.tensor_tensor(pt.t[:, :], pt.t[:, :], EB.t[:, off:off + 512], ALU.mult),
                                     reads=[(pt, None), (EB, None)], writes=[(pt, None)])

                            def pv():
                                pt = st["pt"]
                                if shared["first"]:
                                    shared["ps_o"] = PSo[next(oi) % 3]
                                ps_o = shared["ps_o"]
                                VV = K.VS if br == 1 else K.VW
                                for i in range(4):
                                    qi = 4 * c + i
                                    lo = 0 if br == 1 else max(0, qi - 4)
                                    if kt < lo or kt > qi:
                                        continue
                                    S.op("pe", lambda e, i=i, qi=qi, fst=shared["first"]: e.matmul(
                                        ps_o.t[:, i * 65:(i + 1) * 65], pt.t[:, i * 128:(i + 1) * 128], VV.t[:, kt, g, :],
                                        start=fst, stop=(kt == qi), skip_group_check=True),
                                        reads=[(pt, None), (VV, None)], writes=[(ps_o, None)])
                                    shared["first"] = False
                                if kt == kts[-1]:
                                    finish_branch(ps_o, 65, 64, None, br, h, c, False)
                            return (scores, pv)
                        jobs.append(mk())
            pipeline(jobs)
            for i in range(4):
                qi = 4 * c + i
                S.op("dve", lambda e, i=i, qi=qi: e.tensor_tensor(oab.t[:, :], oacc.t[:, i, :], K.ZA.t[:, qi, :], ALU.mult),
                     reads=[(oacc, None), (K.ZA, qi)], writes=[(oab, None)])
                psb = PSm.t.bitcast(BF16)
                for j in range(4):
                    S.op("pe", lambda e, psb=psb, j=j: e.transpose(psb[:, j * 128:(j + 1) * 128], oab.t[:, j * 128:(j + 1) * 128], ident.t[:, :]),
                         reads=[(oab, None), (ident, None)], writes=[(PSm, None)])
                S.op("act", lambda e, psb=psb, qi=qi: e.copy(K.oaT.t[:, :, qi * 128:(qi + 1) * 128],
                                                             psb[:, 0:512].rearrange("p (j t) -> p j t", t=128)),
                     reads=[(PSm, None)], writes=[(K.oaT, qi)])
        if "p3" in K.debug:
            with ExitStack() as es2:
                tmp = sb(es2, "dbgtmp3", [128, 1024], F32)
                for j in range(4):
                    dump(K, tmp, K.oaT, K.oaT.t[:, j, :], f"oaT{j}")
                S.emit()
        else:
            S.emit()


def phase_dn(K):
    nc, S, sb, PS = K.nc, K.S, K.sb, K.PS
    w_in = K.w_in.ap().rearrange("(k p) c -> p k c", p=128)
    for hp in range(2):
      K.h0 = NH * hp
      K.sfx = f"_{hp}"
      sfx = K.sfx
      with ExitStack() as es:
        qT = sb(es, "dqT" + sfx, [128, NH, SEQ], BF16)
        UO = sb(es, "dUO" + sfx, [128, NT, NH, 128], F32)
        WT = sb(es, "dWT" + sfx, [128, NH, SEQ], BF16)
        AQ = sb(es, "dAQ" + sfx, [128, NH, NT, 128], BF16)
        KD = sb(es, "dKD" + sfx, [128, NT, NH, 128], BF16)
        ZB = sb(es, "dZB" + sfx, [128, NT, NH * 128], BF16)
        EGC = sb(es, "dEGC" + sfx, [128, NT, NH], F32)
        EGT = sb(es, "dEGT" + sfx, [128, NT, 2 * NH], F32)
        ident = sb(es, "dident" + sfx, [128, 128], BF16)
        cst = sb(es, "dcst" + sfx, [128, 128], F32)
        S.op("sp", lambda e, cst=cst: e.dma_start(out=cst.t[:, :], in_=K.ident.ap()), writes=[(cst, None)])
        S.op("dve", lambda e, cst=cst, ident=ident: e.tensor_copy(ident.t[:, :], cst.t[:, :]), reads=[(cst, None)], writes=[(ident, None)])
        with ExitStack() as es1:
            kT = sb(es1, "dkT" + sfx, [128, NH, SEQ], BF16)
            Vt = sb(es1, "dVt" + sfx, [128, NT, NH, 128], BF16)
            Kt = sb(es1, "dKt" + sfx, [128, NT, NH, 128], BF16)
            G = sb(es1, "dG" + sfx, [128, NT, NH], F32)
            NB = sb(es1, "dNB" + sfx, [128, NT, NH], F32)
            Bt = sb(es1, "dB" + sfx, [128, NT, NH], F32)
            dn_proj(K, es1, w_in, qT, kT, Vt, Kt, G, NB, Bt, ZB, ident)
            if "stop_proj" not in K.debug:
                dn_prep(K, qT, kT, Vt, Kt, G, NB, Bt, UO, WT, AQ, KD, EGC, EGT, ident)
        if "stop_proj" not in K.debug and "stop_prep" not in K.debug:
            dn_scan(K, qT, UO, WT, AQ, KD, ZB, EGC, EGT, ident)


def dn_proj(K, es1, w_in, qT, kT, Vt, Kt, G, NB, Bt, ZB, ident):
    nc, S, sb, PS = K.nc, K.S, K.sb, K.PS
    with ExitStack() as es:
        xT = sb(es, "xT_bf2" + K.sfx, [128, 8, SEQ], BF16)
        xst = [sb(es, f"dxst{i}" + K.sfx, [128, 1024], F32) for i in range(2)]
        wst = [sb(es, f"dwst{i}" + K.sfx, [128, 8, 128], F32) for i in range(2)]
        K.wst_rr = 0
        wb = [sb(es, f"dwb{i}" + K.sfx, [128, 8, NH * 128], BF16) for i in range(3)]
        wsm = sb(es, "dwsm" + K.sfx, [128, 8, 8], BF16)
        h0 = K.h0
        cw = sb(es, "dcw" + K.sfx, [128, 12, 4], F32)
        alog = sb(es, "dalog" + K.sfx, [128, 4], F32)
        dtb = sb(es, "ddtb" + K.sfx, [128, 4], F32)
        ones = sb(es, "dones" + K.sfx, [128, 128], BF16)
        onesf = sb(es, "donesf" + K.sfx, [128, 128], F32)
        WP = 3
        stage_s = [sb(es, f"dstage{i}" + K.sfx, [128, 516], BF16) for i in range(WP)]
        identf = sb(es, "didentf" + K.sfx, [128, 128], F32)
        DW = sb(es, "dDW" + K.sfx, [128, 3 * NH * 4, 128], BF16)
        act_s = [sb(es, f"dact{i}" + K.sfx, [128, 512], F32) for i in range(WP)]
        sq_s = [sb(es, f"dsq{i}" + K.sfx, [128, 512], BF16) for i in range(WP)]
        rstd_s = [sb(es, f"drstd{i}" + K.sfx, [128, 512], F32) for i in range(WP)]
        vT_s = [sb(es, f"dvT{i}" + K.sfx, [128, 512], BF16) for i in range(WP)]
        sp_ = sb(es, "dsp" + K.sfx, [128, NT, NH], F32)
        RAW = sb(es, "draw" + K.sfx, [128, NT, 8], F32)
        S.op("sp", lambda e: e.dma_start(out=cw.t[:, :, :], in_=K.convw.ap()), writes=[(cw, None)])
        S.op("sp", lambda e: e.dma_start(out=alog.t[:, :], in_=K.alog.ap()), writes=[(alog, None)])
        S.op("sp", lambda e: e.dma_start(out=dtb.t[:, :], in_=K.dtb.ap()), writes=[(dtb, None)])
        S.op("sp", lambda e: e.dma_start(out=onesf.t[:, :], in_=K.ones.ap()), writes=[(onesf, None)])
        S.op("dve", lambda e: e.tensor_copy(ones.t[:, :], onesf.t[:, :]), reads=[(onesf, None)], writes=[(ones, None)])
        S.op("act", lambda e: e.activation(alog.t[:, :], alog.t[:, :], AF.Exp), reads=[(alog, None)], writes=[(alog, None)])
        S.op("sp", lambda e: e.dma_start(out=identf.t[:, :], in_=K.ident.ap()), writes=[(identf, None)])
        for cg_ in range(3):
            for hh_ in range(NH):
                for i_ in range(4):
                    j_ = cg_ * 4 + h0 + hh_
                    d_ = (cg_ * NH + hh_) * 4 + i_
                    S.op("dve", lambda e, j_=j_, i_=i_, d_=d_: e.tensor_scalar(DW.t[:, d_, :], identf.t[:, :], cw.t[:, j_, i_:i_ + 1], None, op0=ALU.mult),
                         reads=[(identf, None), (cw, None)], writes=[(DW, d_)])
        psi = itertools.count()
        load_w(K, wb[0], 0, w_in, C_QKVB + 0 * 512 + h0 * 128, NH * 128, wst, ("w", 0))
        load_xT(K, xT, xst)
        for cg in range(1, 3):
            load_w(K, wb[cg], 0, w_in, C_QKVB + cg * 512 + h0 * 128, NH * 128, wst, ("w", cg))
        prev_stage = [None]

        def conv_item(cg, hh, t):
            def gen(slot):
                w = wb[cg]
                j = cg * 4 + h0 + hh
                tsl = slice(t * 512, (t + 1) * 512)
                stage, act_, sq, rstd, vT = stage_s[slot], act_s[slot], sq_s[slot], rstd_s[slot], vT_s[slot]
                ps = PS[2 * slot]
                acc = ps
                for k in range(8):
                    S.op("pe", lambda e, k=k: e.matmul(ps.t[:, :], w.t[:, k, hh * 128:(hh + 1) * 128], xT.t[:, k, tsl], start=(k == 0), stop=(k == 7)),
                         reads=[(w, None), (xT, (k, tsl.start // 1024))], writes=[(ps, None)])
                yield
                if t == 0:
                    S.op("dve", lambda e: e.memset(stage.t[:, 0:3], 0.0), writes=[(stage, "c")])
                else:
                    pst_ = prev_stage[0]
                    S.op("dve", lambda e: e.tensor_copy(stage.t[:, 0:3], pst_.t[:, 512:515]), reads=[(pst_, None)], writes=[(stage, "c")])
                prev_stage[0] = stage
                S.op("act", lambda e: e.copy(stage.t[:, 3:515], ps.t[:, :]), reads=[(ps, None)], writes=[(stage, "m")])
                yield
                for i in range(4):
                    d_ = (cg * NH + hh) * 4 + i
                    S.op("pe", lambda e, i=i, d_=d_: e.matmul(ps.t[:, :], DW.t[:, d_, :], stage.t[:, i:i + 512], start=(i == 0), stop=(i == 3)),
                         reads=[(DW, d_), (stage, None)], writes=[(ps, None)])
                yield
                if cg == 2:
                    S.op("act", lambda e: e.activation(vT.t[:, :], acc.t[:, :], AF.Silu), reads=[(acc, None)], writes=[(vT, None)])
                    yield
                    psb = PS[2 * slot + 1]
                    pb = psb.t.bitcast(BF16)
                    for u in range(4):
                        S.op("pe", lambda e, u=u: e.transpose(pb[:, u * 128:(u + 1) * 128], vT.t[:, u * 128:(u + 1) * 128], ident.t[:, :]),
                             reads=[(vT, None), (ident, None)], writes=[(psb, None)])
                    yield
                    S.op("dve", lambda e: e.tensor_copy(Vt.t[:, 4 * t:4 * t + 4, hh, :], pb[:, 0:512].rearrange("p (u d) -> p u d", d=128)),
                         reads=[(psb, None)], writes=[(Vt, (t, hh))])
                else:
                    S.op("act", lambda e: e.activation(act_.t[:, :], acc.t[:, :], AF.Silu), reads=[(acc, None)], writes=[(act_, None)])
                    yield
                    S.op("dve", lambda e: e.tensor_tensor(sq.t[:, :], act_.t[:, :], act_.t[:, :], ALU.mult), reads=[(act_, None)], writes=[(sq, None)])
                    yield
                    ps2 = PS[2 * slot + 1]
                    S.op("pe", lambda e: e.matmul(ps2.t[:, :], ones.t[:, :], sq.t[:, :], start=True, stop=True),
                         reads=[(ones, None), (sq, None)], writes=[(ps2, None)])
                    yield
                    S.op("act", lambda e: e.activation(rstd.t[:, :], ps2.t[:, :], AF.Sqrt, bias=1e-6), reads=[(ps2, None)], writes=[(rstd, None)])
                    yield
                    S.op("dve", lambda e: e.reciprocal(rstd.t[:, :], rstd.t[:, :]), reads=[(rstd, None)], writes=[(rstd, None)])
                    dst = qT if cg == 0 else kT
                    sc = 128 ** -0.5 if cg == 0 else 1.0
                    S.op("dve", lambda e: e.scalar_tensor_tensor(dst.t[:, hh, tsl], act_.t[:, :], sc, rstd.t[:, :], op0=ALU.mult, op1=ALU.mult),
                         reads=[(act_, None), (rstd, None)], writes=[(dst, (hh, t))])
                    if cg == 1:
                        yield
                        psb = PS[2 * slot]
                        pb = psb.t.bitcast(BF16)
                        for u in range(4):
                            S.op("pe", lambda e, u=u: e.transpose(pb[:, u * 128:(u + 1) * 128], kT.t[:, hh, t * 512 + u * 128:t * 512 + (u + 1) * 128], ident.t[:, :]),
                                 reads=[(kT, (hh, t)), (ident, None)], writes=[(psb, None)])
                        yield
                        S.op("act", lambda e: e.copy(Kt.t[:, 4 * t:4 * t + 4, hh, :], pb[:, 0:512].rearrange("p (u d) -> p u d", d=128)),
                             reads=[(psb, None)], writes=[(Kt, (t, hh))])
            return gen

        run_interleaved([conv_item(cg, hh, t) for cg in range(3) for hh in range(NH) for t in range(4)], WP)
        psi = itertools.count()
        load_w(K, wsm, 0, w_in, C_BETA, 8, wst, "sm")
        wz = wb[1]
        load_w(K, wz, 0, w_in, C_ZB + h0 * 128, NH * 128, wst, "zb")
        for i in range(NT):
            isl = slice(i * 128, (i + 1) * 128)
            ps = PS[next(psi) % 4]
            for k in range(8):
                S.op("pe", lambda e, ps=ps, k=k, isl=isl: e.matmul(ps.t[:, 0:8], xT.t[:, k, isl], wsm.t[:, k, :], start=(k == 0), stop=(k == 7)),
                     reads=[(wsm, None), (xT, (k, isl.start // 1024))], writes=[(ps, None)])
            S.op("dve", lambda e, ps=ps, i=i: e.tensor_copy(RAW.t[:, i, :], ps.t[:, 0:8]), reads=[(ps, None)], writes=[(RAW, i)])
            ps2 = PS[next(psi) % 4]
            for k in range(8):
                S.op("pe", lambda e, ps2=ps2, k=k, isl=isl: e.matmul(ps2.t[:, 0:NH * 128], xT.t[:, k, isl], wz.t[:, k, 0:NH * 128], start=(k == 0), stop=(k == 7)),
                     reads=[(wz, None), (xT, (k, isl.start // 1024))], writes=[(ps2, None)])
            S.op("act", lambda e, ps2=ps2, i=i: e.activation(ZB.t[:, i, :], ps2.t[:, 0:NH * 128], AF.Silu), reads=[(ps2, None)], writes=[(ZB, i)])
        S.op("act", lambda e: e.activation(Bt.t[:, :, :], RAW.t[:, :, h0:h0 + NH], AF.Sigmoid), reads=[(RAW, None)], writes=[(Bt, None)])
        S.op("dve", lambda e: e.tensor_tensor(sp_.t[:, :, :], RAW.t[:, :, 4 + h0:4 + h0 + NH],
                                             dtb.t[:, h0:h0 + NH].unsqueeze(1).to_broadcast([128, NT, NH]), ALU.add),
             reads=[(RAW, None), (dtb, None)], writes=[(sp_, None)])
        S.op("act", lambda e: e.activation(sp_.t[:, :, :], sp_.t[:, :, :], AF.Exp), reads=[(sp_, None)], writes=[(sp_, None)])
        S.op("act", lambda e: e.activation(sp_.t[:, :, :], sp_.t[:, :, :], AF.Ln, bias=1.0), reads=[(sp_, None)], writes=[(sp_, None)])
        S.op("dve", lambda e: e.scalar_tensor_tensor(G.t[:, :, :], sp_.t[:, :, :], -1.0,
                                                    alog.t[:, h0:h0 + NH].unsqueeze(1).to_broadcast([128, NT, NH]), op0=ALU.mult, op1=ALU.mult),
             reads=[(sp_, None), (alog, None)], writes=[(G, None)])
        S.op("dve", lambda e: e.tensor_scalar_mul(NB.t[:, :, :], Bt.t[:, :, :], -1.0), reads=[(Bt, None)], writes=[(NB, None)])
        if "p4a" in K.debug:
            tmp = sb(es, "dbgtmp5" + K.sfx, [128, 2048], F32)
            for h in range(NH):
                dump(K, tmp, qT, qT.t[:, h, :], f"dqT{h0 + h}")
                dump(K, tmp, kT, kT.t[:, h, :], f"dkT{h0 + h}")
            dump(K, tmp, Vt, Vt.t[:, :, :, :].rearrange("p a b c -> p (a b c)"), "dVt" + K.sfx)
            dump(K, tmp, Kt, Kt.t[:, :, :, :].rearrange("p a b c -> p (a b c)"), "dKt" + K.sfx)
            dump(K, tmp, G, G.t[:, :, :].rearrange("p a b -> p (a b)"), "dG" + K.sfx)
            dump(K, tmp, Bt, Bt.t[:, :, :].rearrange("p a b -> p (a b)"), "dB" + K.sfx)
        S.emit()


def dn_prep(K, qT, kT, Vt, Kt, G, NB, Bt, UO, WT, AQ, KD, EGC, EGT, ident):
    nc, S, sb, PS = K.nc, K.S, K.sb, K.PS
    with ExitStack() as es:
        UBD = sb(es, "pUBD" + K.sfx, [128, 128], F32)
        SLBD = sb(es, "pSLBD" + K.sfx, [128, 128], F32)
        ONES = sb(es, "pONES" + K.sfx, [128, 128], F32)
        S.op("sp", lambda e: e.dma_start(out=UBD.t[:, :], in_=K.ubd.ap()), writes=[(UBD, None)])
        S.op("sp", lambda e: e.dma_start(out=SLBD.t[:, :], in_=K.slbd.ap()), writes=[(SLBD, None)])
        S.op("sp", lambda e: e.dma_start(out=ONES.t[:, :], in_=K.ones.ap()), writes=[(ONES, None)])
        UBDb = sb(es, "pUBDb" + K.sfx, [128, 128], BF16)
        SLBDb = sb(es, "pSLBDb" + K.sfx, [128, 128], BF16)
        ONESb = sb(es, "pONESb" + K.sfx, [128, 128], BF16)
        for (a_, b_) in ((UBD, UBDb), (SLBD, SLBDb), (ONES, ONESb)):
            S.op("dve", lambda e, a_=a_, b_=b_: e.tensor_copy(b_.t[:, :], a_.t[:, :]), reads=[(a_, None)], writes=[(b_, None)])
        NEGSL = sb(es, "pNEGSL" + K.sfx, [128, 128], BF16)
        NEGU = sb(es, "pNEGU" + K.sfx, [128, 128], BF16)
        for (a_, b_) in ((SLBD, NEGSL), (UBD, NEGU)):
            S.op("dve", lambda e, a_=a_, b_=b_: e.tensor_scalar(b_.t[:, :], a_.t[:, :], -1.0, -NEG, op0=ALU.add, op1=ALU.mult),
                 reads=[(a_, None)], writes=[(b_, None)])
        Gh = sb(es, "pGh" + K.sfx, [128, NT, NH], BF16)
        Gl = sb(es, "pGl" + K.sfx, [128, NT, NH], BF16)
        Gf = sb(es, "pGf" + K.sfx, [128, NT, NH], F32)
        S.op("dve", lambda e: e.tensor_copy(Gh.t[:, :, :], G.t[:, :, :]), reads=[(G, None)], writes=[(Gh, None)])
        S.op("dve", lambda e: e.tensor_copy(Gf.t[:, :, :], Gh.t[:, :, :]), reads=[(Gh, None)], writes=[(Gf, None)])
        S.op("dve", lambda e: e.tensor_tensor(Gf.t[:, :, :], G.t[:, :, :], Gf.t[:, :, :], ALU.subtract), reads=[(G, None), (Gf, None)], writes=[(Gf, None)])
        S.op("dve", lambda e: e.tensor_copy(Gl.t[:, :, :], Gf.t[:, :, :]), reads=[(Gf, None)], writes=[(Gl, None)])
        GSl2 = [sb(es, f"pGSl{i}" + K.sfx, [128, 2 * NH], BF16) for i in range(NT)]
        NBUF = 6
        gUl = [sb(es, f"pgUl{i}" + K.sfx, [128, 128], BF16) for i in range(NBUF)]
        gU = [sb(es, f"pgU{i}" + K.sfx, [128, 128], BF16) for i in range(NBUF)]
        Dm = [sb(es, f"pDm{i}" + K.sfx, [128, 128], F32) for i in range(NBUF)]
        DTm = [sb(es, f"pDTm{i}" + K.sfx, [128, 128], F32) for i in range(NBUF)]
        Nb = [[sb(es, f"pN{i}_{l}" + K.sfx, [128, 128], BF16) for l in range(2)] for i in range(NBUF)]
        NTb = [[sb(es, f"pNT{i}_{l}" + K.sfx, [128, 128], BF16) for l in range(2)] for i in range(NBUF)]
        X = [sb(es, f"pX{i}" + K.sfx, [128, 256], F32) for i in range(NBUF)]
        Xb = [sb(es, f"pXb{i}" + K.sfx, [128, 256], BF16) for i in range(NBUF)]
        GS2 = [sb(es, f"pGS{i}" + K.sfx, [128, 2 * NH], BF16) for i in range(NT)]
        GR2 = [sb(es, f"pGR{i}" + K.sfx, [128, NH], F32) for i in range(NT)]
        BK2 = [sb(es, f"pBK{i}" + K.sfx, [128, NH], F32) for i in range(NT)]
        wb_ = [sb(es, f"pwb{i}" + K.sfx, [128, 128], BF16) for i in range(NBUF)]
        psi = itertools.count()
        gens = []
        for m in range(NT):
            msl = slice(m * 128, (m + 1) * 128)
            GS, GSl, GR, BK = GS2[m], GSl2[m], GR2[m], BK2[m]
            ps = PS[next(psi) % 8]
            S.op("pe", lambda e, ps=ps, m=m: e.matmul(ps.t[:, 0:NH], UBDb.t[:, :], Gh.t[:, m, :], start=True, stop=False),
                 reads=[(UBDb, None), (Gh, None)], writes=[(ps, None)])
            S.op("pe", lambda e, ps=ps, m=m: e.matmul(ps.t[:, 0:NH], UBDb.t[:, :], Gl.t[:, m, :], start=False, stop=True),
                 reads=[(UBDb, None), (Gl, None)], writes=[(ps, None)])
            S.op("act", lambda e, ps=ps, m=m: e.activation(EGC.t[:, m, :], ps.t[:, 0:NH], AF.Exp), reads=[(ps, None)], writes=[(EGC, m)])
            ps = PS[next(psi) % 8]
            S.op("pe", lambda e, ps=ps, m=m: e.matmul(ps.t[:, 0:NH], SLBDb.t[:, :], Gh.t[:, m, :], start=True, stop=False),
                 reads=[(SLBDb, None), (Gh, None)], writes=[(ps, None)])
            S.op("pe", lambda e, ps=ps, m=m: e.matmul(ps.t[:, 0:NH], SLBDb.t[:, :], Gl.t[:, m, :], start=False, stop=True),
                 reads=[(SLBDb, None), (Gl, None)], writes=[(ps, None)])
            S.op("act", lambda e, ps=ps, GR=GR: e.activation(GR.t[:, :], ps.t[:, 0:NH], AF.Exp), reads=[(ps, None)], writes=[(GR, None)])
            ps = PS[next(psi) % 8]
            for (gs_, gsrc, st_) in ((GS, Gh, True), (GSl, Gl, False)):
                S.op("pool", lambda e, gs_=gs_: e.memset(gs_.t[:, :], 0.0), writes=[(gs_, None)])
                S.op("pool", lambda e, m=m, gs_=gs_, gsrc=gsrc: e.tensor_copy(gs_.t[0:64, 0:NH], gsrc.t[0:64, m, :]), reads=[(gsrc, None)], writes=[(gs_, None)])
                S.op("pool", lambda e, m=m, gs_=gs_, gsrc=gsrc: e.tensor_copy(gs_.t[64:128, NH:2 * NH], gsrc.t[64:128, m, :]), reads=[(gsrc, None)], writes=[(gs_, None)])
                S.op("pe", lambda e, ps=ps, gs_=gs_, st_=st_: e.matmul(ps.t[:, 0:2 * NH], ONESb.t[:, :], gs_.t[:, :], start=st_, stop=(not st_)),
                     reads=[(ONESb, None), (gs_, None)], writes=[(ps, None)])
            S.op("act", lambda e, ps=ps, m=m: e.activation(EGT.t[:, m, :], ps.t[:, 0:2 * NH], AF.Exp), reads=[(ps, None)], writes=[(EGT, m)])
            S.op("dve", lambda e, m=m, BK=BK: e.tensor_tensor(BK.t[:, :], Bt.t[:, m, :], EGC.t[:, m, :], ALU.mult),
                 reads=[(Bt, m), (EGC, m)], writes=[(BK, None)])
            for h in range(NH):
              def mk(m=m, h=h, msl=msl, GR=GR, BK=BK):
               def gen(b):
                    S.op("dve", lambda e, b=b, m=m, h=h: e.tensor_scalar(gU[b].t[:, :], UBD.t[:, :], Gh.t[:, m, h:h + 1], None, op0=ALU.mult),
                         reads=[(UBD, None), (Gh, None)], writes=[(gU[b], None)])
                    S.op("dve", lambda e, b=b, m=m, h=h: e.tensor_scalar(gUl[b].t[:, :], UBD.t[:, :], Gl.t[:, m, h:h + 1], None, op0=ALU.mult),
                         reads=[(UBD, None), (Gl, None)], writes=[(gUl[b], None)])
                    yield
                    psd = PS[next(psi) % 8]
                    S.op("pe", lambda e, psd=psd, b=b: e.matmul(psd.t[:, 0:128], gU[b].t[:, :], SLBDb.t[:, :], start=True, stop=False),
                         reads=[(gU[b], None), (SLBDb, None)], writes=[(psd, None)])
                    S.op("pe", lambda e, psd=psd, b=b: e.matmul(psd.t[:, 0:128], gUl[b].t[:, :], SLBDb.t[:, :], start=False, stop=False),
                         reads=[(gUl[b], None), (SLBDb, None)], writes=[(psd, None)])
                    S.op("pe", lambda e, psd=psd: e.matmul(psd.t[:, 0:128], ident.t[:, :], NEGSL.t[:, :], start=False, stop=True),
                         reads=[(ident, None), (NEGSL, None)], writes=[(psd, None)])
                    S.op("pe", lambda e, psd=psd, b=b: e.matmul(psd.t[:, 128:256], SLBDb.t[:, :], gU[b].t[:, :], start=False, stop=False, skip_group_check=True),
                         reads=[(gU[b], None), (SLBDb, None)], writes=[(psd, None)])
                    S.op("pe", lambda e, psd=psd, b=b: e.matmul(psd.t[:, 128:256], SLBDb.t[:, :], gUl[b].t[:, :], start=False, stop=False, skip_group_check=True),
                         reads=[(gUl[b], None), (SLBDb, None)], writes=[(psd, None)])
                    S.op("pe", lambda e, psd=psd: e.matmul(psd.t[:, 128:256], ident.t[:, :], NEGU.t[:, :], start=False, stop=True, skip_group_check=True),
                         reads=[(ident, None), (NEGU, None)], writes=[(psd, None)])
                    yield
                    S.op("act", lambda e, psd=psd, b=b: e.activation(Dm[b].t[:, :], psd.t[:, 0:128], AF.Exp), reads=[(psd, None)], writes=[(Dm[b], None)])
                    S.op("act", lambda e, psd=psd, b=b: e.activation(DTm[b].t[:, :], psd.t[:, 128:256], AF.Exp), reads=[(psd, None)], writes=[(DTm[b], None)])
                    yield
                    psk = PS[next(psi) % 8]
                    S.op("pe", lambda e, psk=psk, h=h, msl=msl: e.matmul(psk.t[:, 0:128], kT.t[:, h, msl], kT.t[:, h, msl], start=True, stop=True),
                         reads=[(kT, None)], writes=[(psk, None)])
                    S.op("pe", lambda e, psk=psk, h=h, msl=msl: e.matmul(psk.t[:, 128:256], kT.t[:, h, msl], qT.t[:, h, msl], start=True, stop=True),
                         reads=[(kT, None), (qT, None)], writes=[(psk, None)])
                    yield
                    S.op("dve", lambda e, psk=psk, b=b, m=m, h=h: e.scalar_tensor_tensor(
                        Nb[b][0].t[:, :], psk.t[:, 0:128], NB.t[:, m, h:h + 1], Dm[b].t[:, :], op0=ALU.mult, op1=ALU.mult),
                        reads=[(psk, None), (NB, m), (Dm[b], None)], writes=[(Nb[b][0], None)])
                    S.op("dve", lambda e, psk=psk, b=b, m=m, h=h: e.tensor_tensor(AQ.t[:, h, m, :], psk.t[:, 128:256], DTm[b].t[:, :], ALU.mult),
                         reads=[(psk, None), (DTm[b], None)], writes=[(AQ, (h, m))])
                    yield
                    pst = PS[next(psi) % 8]
                    ptb = pst.t.bitcast(BF16)
                    S.op("pe", lambda e, ptb=ptb, b=b: e.transpose(ptb[:, 0:128], Nb[b][0].t[:, :], ident.t[:, :]),
                         reads=[(Nb[b][0], None), (ident, None)], writes=[(pst, None)])
                    yield
                    S.op("act", lambda e, ptb=ptb, b=b: e.copy(NTb[b][0].t[:, :], ptb[:, 0:128]), reads=[(pst, None)], writes=[(NTb[b][0], None)])
                    yield
                    S.op("dve", lambda e, b=b, m=m, h=h: e.tensor_scalar(X[b].t[:, 0:128], Vt.t[:, m, h, :], Bt.t[:, m, h:h + 1], None, op0=ALU.mult),
                         reads=[(Vt, None), (Bt, m)], writes=[(X[b], None)])
                    S.op("dve", lambda e, b=b, m=m, h=h: e.tensor_scalar(X[b].t[:, 128:256], Kt.t[:, m, h, :], BK.t[:, h:h + 1], None, op0=ALU.mult),
                         reads=[(Kt, None), (BK, None)], writes=[(X[b], None)])
                    S.op("act", lambda e, b=b: e.copy(Xb[b].t[:, :], X[b].t[:, :]), reads=[(X[b], None)], writes=[(Xb[b], None)])
                    S.op("pool", lambda e, b=b, m=m, h=h: e.tensor_scalar(KD.t[:, m, h, :], Kt.t[:, m, h, :], GR.t[:, h:h + 1], None, op0=ALU.mult),
                         reads=[(Kt, None), (GR, None)], writes=[(KD, (m, h))])
                    for l in range(6):
                        yield
                        cur, nxt = l % 2, (l + 1) % 2
                        psx = PS[next(psi) % 8]
                        S.op("pe", lambda e, psx=psx, b=b, cur=cur: e.matmul(psx.t[:, 0:256], NTb[b][cur].t[:, :], Xb[b].t[:, :], start=True, stop=True),
                             reads=[(NTb[b][cur], None), (Xb[b], None)], writes=[(psx, None)])
                        yield
                        S.op("dve", lambda e, psx=psx, b=b: e.tensor_tensor(X[b].t[:, :], X[b].t[:, :], psx.t[:, 0:256], ALU.add),
                             reads=[(psx, None), (X[b], None)], writes=[(X[b], None)])
                        yield
                        if l < 5:
                            S.op("act", lambda e, b=b: e.copy(Xb[b].t[:, :], X[b].t[:, :]), reads=[(X[b], None)], writes=[(Xb[b], None)])
                            psq = PS[next(psi) % 8]
                            S.op("pe", lambda e, psq=psq, b=b, cur=cur: e.matmul(psq.t[:, 0:128], NTb[b][cur].t[:, :], Nb[b][cur].t[:, :], start=True, stop=True),
                                 reads=[(NTb[b][cur], None), (Nb[b][cur], None)], writes=[(psq, None)])
                            S.op("pe", lambda e, psq=psq, b=b, cur=cur: e.matmul(psq.t[:, 128:256], Nb[b][cur].t[:, :], NTb[b][cur].t[:, :], start=True, stop=True),
                                 reads=[(NTb[b][cur], None), (Nb[b][cur], None)], writes=[(psq, None)])
                            yield
                            S.op("act", lambda e, psq=psq, b=b, nxt=nxt: e.copy(Nb[b][nxt].t[:, :], psq.t[:, 0:128]),
                                 reads=[(psq, None)], writes=[(Nb[b][nxt], None)])
                            S.op("dve", lambda e, psq=psq, b=b, nxt=nxt: e.tensor_copy(NTb[b][nxt].t[:, :], psq.t[:, 128:256]),
                                 reads=[(psq, None)], writes=[(NTb[b][nxt], None)])
                    yield
                    S.op("pool", lambda e, b=b, m=m, h=h: e.tensor_copy(UO.t[:, m, h, :], X[b].t[:, 0:128]), reads=[(X[b], None)], writes=[(UO, (m, h))])
                    S.op("act", lambda e, b=b: e.copy(wb_[b].t[:, :], X[b].t[:, 128:256]), reads=[(X[b], None)], writes=[(wb_[b], None)])
                    yield
                    pst = PS[next(psi) % 8]
                    ptb = pst.t.bitcast(BF16)
                    S.op("pe", lambda e, ptb=ptb, b=b: e.transpose(ptb[:, 0:128], wb_[b].t[:, :], ident.t[:, :]),
                         reads=[(wb_[b], None), (ident, None)], writes=[(pst, None)])
                    yield
                    S.op("act", lambda e, ptb=ptb, h=h, msl=msl: e.copy(WT.t[:, h, msl], ptb[:, 0:128]), reads=[(pst, None)], writes=[(WT, (h, msl.start))])
               return gen
              gens.append(mk())
        run_interleaved(gens, NBUF)
        S.emit()


def dn_scan(K, qT, UO, WT, AQ, KD, ZB, EGC, EGT, ident):
    nc, S, sb, PS = K.nc, K.S, K.sb, K.PS
    with ExitStack() as es:
        St = sb(es, "sSt" + K.sfx, [128, NH, 128], F32)
        Sb = sb(es, "sSb" + K.sfx, [128, NH, 128], BF16)
        h0 = K.h0
        VN = [sb(es, f"sVN{h}" + K.sfx, [128, 128], BF16) for h in range(NH)]
        tq = [sb(es, f"stq{h}" + K.sfx, [128, 128], F32) for h in range(NH)]
        NW = sb(es, "sNW" + K.sfx, [128, 128], F32)
        sq = sb(es, "ssq" + K.sfx, [128, NH * 128], F32)
        ss = sb(es, "sss" + K.sfx, [128, NH], F32)
        on = sb(es, "son" + K.sfx, [128, NH * 128], F32)
        ob = sb(es, "sob" + K.sfx, [128, NH * 128], BF16)
        S.op("sp", lambda e: e.dma_start(out=NW.t[:, :], in_=K.normw.ap()), writes=[(NW, None)])
        S.op("pool", lambda e: e.memset(St.t[:, :, :], 0.0), writes=[(St, None)])
        S.op("pool", lambda e: e.memset(Sb.t[:, :, :], 0.0), writes=[(Sb, None)])
        for n in range(2 * NT):
            m, half = n // 2, n % 2
            rs = slice(half * 64, half * 64 + 64)
            msl = slice(m * 128, (m + 1) * 128)

            def head_gen(h, m=m, half=half, rs=rs, msl=msl):
                def gen(slot):
                    p1, p2 = PS[2 * h], PS[2 * h + 1]
                    S.op("pe", lambda e: e.matmul(p1.t[:, 0:128], WT.t[:, h, msl], Sb.t[:, h, :], start=True, stop=True),
                         reads=[(WT, None), (Sb, h)], writes=[(p1, "a")])
                    S.op("pe", lambda e: e.matmul(p1.t[:, 128:256], qT.t[:, h, msl], Sb.t[:, h, :], start=True, stop=True),
                         reads=[(qT, None), (Sb, h)], writes=[(p1, "b")])
                    yield
                    S.op("dve", lambda e: e.tensor_tensor(VN[h].t[rs, :], UO.t[rs, m, h, :], p1.t[rs, 0:128], ALU.subtract),
                         reads=[(UO, (m, h)), (p1, "a")], writes=[(VN[h], None)])
                    S.op("dve", lambda e: e.tensor_scalar(tq[h].t[rs, :], p1.t[rs, 128:256], EGC.t[rs, m, h:h + 1], None, op0=ALU.mult),
                         reads=[(p1, "b"), (EGC, None)], writes=[(tq[h], None)])
                    yield
                    S.op("pe", lambda e: e.matmul(p2.t[:, 128:256], KD.t[rs, m, h, :], VN[h].t[rs, :], start=True, stop=True),
                         reads=[(KD, None), (VN[h], None)], writes=[(p2, "b")])
                    S.op("pe", lambda e: e.matmul(p2.t[:, 0:128], AQ.t[rs, h, m, :], VN[h].t[rs, :], start=True, stop=True),
                         reads=[(AQ, None), (VN[h], None)], writes=[(p2, "a")])
                    yield
                    ecol = EGT.t[:, m, half * NH + h:half * NH + h + 1]
                    S.op("dve", lambda e: e.scalar_tensor_tensor(Sb.t[:, h, :], St.t[:, h, :], ecol, p2.t[:, 128:256], op0=ALU.mult, op1=ALU.add),
                         reads=[(St, h), (EGT, None), (p2, "b")], writes=[(Sb, h)])
                    S.op("dve", lambda e: e.scalar_tensor_tensor(St.t[:, h, :], St.t[:, h, :], ecol, p2.t[:, 128:256], op0=ALU.mult, op1=ALU.add),
                         reads=[(St, h), (EGT, None), (p2, "b")], writes=[(St, h)])
                    S.op("dve", lambda e: e.tensor_tensor(UO.t[rs, m, h, :], tq[h].t[rs, :], p2.t[rs, 0:128], ALU.add),
                         reads=[(tq[h], None), (p2, "a")], writes=[(UO, (m, h))])
                return gen

            run_interleaved([head_gen(h) for h in range(NH)], NH)
            if half == 1 and "nonorm" not in K.debug:
                uo = UO.t[:, m, :, :]
                S.op("pool", lambda e, uo=uo: e.tensor_tensor(sq.t[:, :].rearrange("p (h d) -> p h d", d=128), uo, uo, ALU.mult),
                     reads=[(UO, None)], writes=[(sq, None)])
                S.op("dve", lambda e: e.tensor_reduce(ss.t[:, :], sq.t[:, :].rearrange("p (h d) -> p h d", d=128), axis=AX.X, op=ALU.add),
                     reads=[(sq, None)], writes=[(ss, None)])
                S.op("act", lambda e: e.activation(ss.t[:, :], ss.t[:, :], AF.Sqrt, bias=1e-6, scale=1.0 / 128), reads=[(ss, None)], writes=[(ss, None)])
                S.op("dve", lambda e: e.reciprocal(ss.t[:, :], ss.t[:, :]), reads=[(ss, None)], writes=[(ss, None)])
                S.op("dve", lambda e, uo=uo: e.tensor_tensor(on.t[:, :].rearrange("p (h d) -> p h d", d=128), uo,
                                                            ss.t[:, :].unsqueeze(2).to_broadcast([128, NH, 128]), ALU.mult),
                     reads=[(UO, None), (ss, None)], writes=[(on, None)])
                S.op("pool", lambda e: e.tensor_tensor(on.t[:, :].rearrange("p (h d) -> p h d", d=128), on.t[:, :].rearrange("p (h d) -> p h d", d=128),
                                                      NW.t[:, :].unsqueeze(1).to_broadcast([128, NH, 128]), ALU.mult),
                     reads=[(on, None), (NW, None)], writes=[(on, None)])
                S.op("dve", lambda e, m=m: e.tensor_tensor(ob.t[:, :], on.t[:, :], ZB.t[:, m, :], ALU.mult),
                     reads=[(on, None), (ZB, m)], writes=[(ob, None)])
                pst = PS[7]
                ptb = pst.t.bitcast(BF16)
                for j in range(NH):
                    S.op("pe", lambda e, ptb=ptb, j=j: e.transpose(ptb[:, 512 + j * 128:512 + (j + 1) * 128], ob.t[:, j * 128:(j + 1) * 128], ident.t[:, :]),
                         reads=[(ob, None), (ident, None)], writes=[(pst, "t")])
                S.op("act", lambda e, ptb=ptb, m=m: e.copy(K.obT.t[:, h0:h0 + NH, m * 128:(m + 1) * 128], ptb[:, 512:512 + NH * 128].rearrange("p (j t) -> p j t", t=128)),
                     reads=[(pst, "t")], writes=[(K.obT, (m, h0))])
        if "p4" in K.debug:
            tmp = sb(es, "dbgtmp6" + K.sfx, [128, 2048], F32)
            for j in range(NH):
                dump(K, tmp, K.obT, K.obT.t[:, h0 + j, :], f"obT{h0 + j}")
        S.emit()


ALPHA = float((2 * 1) ** 0.25)


def phase_tail(K):
    nc, S, sb, PS = K.nc, K.S, K.sb, K.PS
    w_in = K.w_in.ap().rearrange("(k p) c -> p k c", p=128)
    with ExitStack() as es0:
        mT = sb(es0, "mT", [128, 8, SEQ], BF16)
        wst = [sb(es0, f"twst{i}", [128, 8, 256], F32) for i in range(2)]
        K.wst_rr = 0
        with ExitStack() as es:
            xT = sb(es, "xT_bf3", [128, 8, SEQ], BF16)
            xst = [sb(es, f"txst{i}", [128, 1024], F32) for i in range(2)]
            Wa = sb(es, "tWa", [128, 4, 1024], BF16)
            Wb = sb(es, "tWb", [128, 4, 1024], BF16)
            wgm = [sb(es, f"twgm{i}", [128, 8, 256], BF16) for i in range(2)]
            gma = sb(es, "tgma", [128, 512], F32)
            gmb = sb(es, "tgmb", [128, 512], F32)
            t1 = sb(es, "tt1", [128, 512], F32)
            t2 = sb(es, "tt2", [128, 512], F32)
            load_w(K, Wa, 0, K.w_ba.ap().rearrange("(k p) c -> p k c", p=128), 0, 1024, wst, "wa")
            load_w(K, Wb, 0, K.w_bb.ap().rearrange("(k p) c -> p k c", p=128), 0, 1024, wst, "wb")
            load_xT(K, xT, xst)
            for f in range(8):
                wg = wgm[f % 2]
                fs = slice(f * 128, (f + 1) * 128)
                load_w(K, wg, 0, w_in, C_GM + f * 128, 128, wst, ("ga", f))
                load_w(K, wg, 128, w_in, C_GM + 1024 + f * 128, 128, wst, ("gb", f))
                for t in range(4):
                    tsl = slice(t * 512, (t + 1) * 512)
                    pa, pb, pga, pgb = PS[0 + 4 * (t % 2)], PS[1 + 4 * (t % 2)], PS[2 + 4 * (t % 2)], PS[3 + 4 * (t % 2)]
                    for k in range(4):
                        S.op("pe", lambda e, pa=pa, k=k, fs=fs, tsl=tsl: e.matmul(pa.t[:, :], Wa.t[:, k, fs], K.oaT.t[:, k, tsl], start=(k == 0), stop=(k == 3)),
                             reads=[(Wa, None), (K.oaT, None)], writes=[(pa, None)])
                    for k in range(4):
                        S.op("pe", lambda e, pb=pb, k=k, fs=fs, tsl=tsl: e.matmul(pb.t[:, :], Wb.t[:, k, fs], K.obT.t[:, k, tsl], start=(k == 0), stop=(k == 3)),
                             reads=[(Wb, None), (K.obT, None)], writes=[(pb, None)])
                    for k in range(8):
                        S.op("pe", lambda e, pga=pga, k=k, wg=wg, tsl=tsl: e.matmul(pga.t[:, :], wg.t[:, k, 0:128], xT.t[:, k, tsl], start=(k == 0), stop=(k == 7)),
                             reads=[(wg, None), (xT, (k, tsl.start // 1024))], writes=[(pga, None)])
                    for k in range(8):
                        S.op("pe", lambda e, pgb=pgb, k=k, wg=wg, tsl=tsl: e.matmul(pgb.t[:, :], wg.t[:, k, 128:256], xT.t[:, k, tsl], start=(k == 0), stop=(k == 7)),
                             reads=[(wg, None), (xT, (k, tsl.start // 1024))], writes=[(pgb, None)])
                    S.op("act", lambda e, pga=pga: e.activation(gma.t[:, :], pga.t[:, :], AF.Sigmoid), reads=[(pga, None)], writes=[(gma, None)])
                    S.op("act", lambda e, pgb=pgb: e.activation(gmb.t[:, :], pgb.t[:, :], AF.Sigmoid), reads=[(pgb, None)], writes=[(gmb, None)])
                    S.op("dve", lambda e, pa=pa: e.tensor_tensor(t1.t[:, :], gma.t[:, :], pa.t[:, :], ALU.mult), reads=[(gma, None), (pa, None)], writes=[(t1, None)])
                    S.op("dve", lambda e, pb=pb: e.tensor_tensor(t2.t[:, :], gmb.t[:, :], pb.t[:, :], ALU.mult), reads=[(gmb, None), (pb, None)], writes=[(t2, None)])
                    S.op("pool", lambda e, f=f, tsl=tsl: e.tensor_tensor(mT.t[:, f, tsl], t1.t[:, :], t2.t[:, :], ALU.add),
                         reads=[(t1, None), (t2, None)], writes=[(mT, (f, t))])
            S.emit()
        with ExitStack() as es:
            Wo = sb(es, "tWo", [128, 8, 1024], BF16)
            Wg = sb(es, "tWg", [128, 8, 1024], BF16)
            Wp = sb(es, "tWp", [128, 2, 1024], BF16)
            pT = sb(es, "tpT", [128, 2, SEQ], BF16)
            LNG = sb(es, "tLNG", [128, 1024], F32)
            LNB = sb(es, "tLNB", [128, 1024], F32)
            ident = sb(es, "tident", [128, 128], BF16)
            cst = sb(es, "tcst", [128, 128], F32)
            xst = [sb(es, f"tpst{i}", [128, 1024], F32) for i in range(2)]
            S.op("sp", lambda e: e.dma_start(out=cst.t[:, :], in_=K.ident.ap()), writes=[(cst, None)])
            S.op("dve", lambda e: e.tensor_copy(ident.t[:, :], cst.t[:, :]), reads=[(cst, None)], writes=[(ident, None)])
            S.op("sp", lambda e: e.dma_start(out=LNG.t[:, :], in_=K.lng.ap()), writes=[(LNG, None)])
            S.op("sp", lambda e: e.dma_start(out=LNB.t[:, :], in_=K.lnb.ap()), writes=[(LNB, None)])
            load_w(K, Wo, 0, K.w_out.ap().rearrange("(k p) c -> p k c", p=128), 0, 1024, wst, "wo")
            load_w(K, Wg, 0, K.w_pg.ap().rearrange("(k p) c -> p k c", p=128), 0, 1024, wst, "wg")
            load_w(K, Wp, 0, K.w_ple.ap().rearrange("(k p) c -> p k c", p=128), 0, 1024, wst, "wp")
            psrc = K.pT.ap().rearrange("(k p) t -> p k t", p=128)
            for k in range(2):
                for hf in range(2):
                    st = xst[(2 * k + hf) % 2]
                    sl = slice(hf * 1024, (hf + 1) * 1024)
                    S.op("sp", lambda e, st=st, k=k, sl=sl: e.dma_start(out=st.t[:, :], in_=psrc[:, k, sl]), writes=[(st, None)])
                    cast_op(K, pT, pT.t[:, k, sl], (k, hf), st, st.t[:, :], None)
            W2 = 2
            xt = [sb(es, f"txt{i}", [128, 1024], F32) for i in range(W2)]
            hh_s = [sb(es, f"th{i}", [128, 1024], F32) for i in range(W2)]
            hb_s = [sb(es, f"thb{i}", [128, 1024], BF16) for i in range(W2)]
            hT_s = [sb(es, f"thT{i}", [128, 8, 128], BF16) for i in range(W2)]
            gate_s = [sb(es, f"tgate{i}", [128, 1024], F32) for i in range(W2)]
            tt_s = [sb(es, f"ttt{i}", [128, 1024], F32) for i in range(W2)]
            h2_s = [sb(es, f"th2{i}", [128, 1024], F32) for i in range(W2)]
            yy = [sb(es, f"tyy{i}", [128, 1024], F32) for i in range(W2)]
            st1_s = [sb(es, f"tst1{i}", [128, 4], F32) for i in range(W2)]
            xsrc = K.x.ap()

            def tile_gen(i):
                def gen(slot):
                    isl = slice(i * 128, (i + 1) * 128)
                    xb, yb, hh_, hb, hT, gate, tt, h2, st1 = (xt[slot], yy[slot], hh_s[slot], hb_s[slot], hT_s[slot], gate_s[slot],
                                                               tt_s[slot], h2_s[slot], st1_s[slot])
                    cen, sqv = h2, tt
                    bk = PS[4 * slot:4 * slot + 4]
                    S.op("sp", lambda e: e.dma_start(out=xb.t[:, :], in_=xsrc[isl, :]), writes=[(xb, None)])
                    for nh in range(2):
                        ns = slice(nh * 512, (nh + 1) * 512)
                        ps = bk[nh]
                        for k in range(8):
                            S.op("pe", lambda e, ps=ps, k=k, ns=ns: e.matmul(ps.t[:, :], mT.t[:, k, isl], Wo.t[:, k, ns], start=(k == 0), stop=(k == 7)),
                                 reads=[(mT, None), (Wo, None)], writes=[(ps, None)])
                    yield
                    for nh in range(2):
                        ns = slice(nh * 512, (nh + 1) * 512)
                        ps = bk[nh]
                        S.op("dve", lambda e, ps=ps, ns=ns: e.scalar_tensor_tensor(hb.t[:, ns], xb.t[:, ns], ALPHA, ps.t[:, :], op0=ALU.mult, op1=ALU.add),
                             reads=[(xb, None), (ps, None)], writes=[(hb, nh)])
                        S.op("dve", lambda e, ps=ps, ns=ns: e.scalar_tensor_tensor(hh_.t[:, ns], xb.t[:, ns], ALPHA, ps.t[:, :], op0=ALU.mult, op1=ALU.add),
                             reads=[(xb, None), (ps, None)], writes=[(hh_, nh)])
                    S.op("dve", lambda e: e.memset(st1.t[:, 0:4], 0.0), writes=[(st1, None)])
                    yield
                    pst = bk[2]
                    ptb = pst.t.bitcast(BF16)
                    for k in range(8):
                        S.op("pe", lambda e, k=k: e.transpose(ptb[:, k * 128:(k + 1) * 128], hb.t[:, k * 128:(k + 1) * 128], ident.t[:, :]),
                             reads=[(hb, None), (ident, None)], writes=[(pst, None)])
                    yield
                    S.op("act", lambda e: e.copy(hT.t[:, :, :], ptb[:, 0:1024].rearrange("p (k t) -> p k t", t=128)),
                         reads=[(pst, None)], writes=[(hT, None)])
                    yield
                    for nh in range(2):
                        ns = slice(nh * 512, (nh + 1) * 512)
                        pg, pp = bk[nh], bk[2 + nh]
                        for k in range(8):
                            S.op("pe", lambda e, pg=pg, k=k, ns=ns: e.matmul(pg.t[:, :], hT.t[:, k, :], Wg.t[:, k, ns], start=(k == 0), stop=(k == 7)),
                                 reads=[(hT, None), (Wg, None)], writes=[(pg, None)])
                        for k in range(2):
                            S.op("pe", lambda e, pp=pp, k=k, ns=ns: e.matmul(pp.t[:, :], pT.t[:, k, isl], Wp.t[:, k, ns], start=(k == 0), stop=(k == 1)),
                                 reads=[(pT, None), (Wp, None)], writes=[(pp, None)])
                    yield
                    for nh in range(2):
                        ns = slice(nh * 512, (nh + 1) * 512)
                        S.op("act", lambda e, pg=bk[nh], ns=ns: e.activation(gate.t[:, ns], pg.t[:, :], AF.Sigmoid), reads=[(bk[nh], None)], writes=[(gate, nh)])
                    yield
                    for nh in range(2):
                        ns = slice(nh * 512, (nh + 1) * 512)
                        S.op("dve", lambda e, pp=bk[2 + nh], ns=ns: e.tensor_tensor(tt.t[:, ns], gate.t[:, ns], pp.t[:, :], ALU.mult),
                             reads=[(gate, nh), (bk[2 + nh], None)], writes=[(tt, nh)])
                    yield
                    S.op("dve", lambda e: e.tensor_tensor(h2.t[:, :], hh_.t[:, :], tt.t[:, :], ALU.add), reads=[(hh_, None), (tt, None)], writes=[(h2, None)])
                    yield
                    S.op("act", lambda e: e.activation(sqv.t[:, :], h2.t[:, :], AF.Identity, scale=-1.0 / 1024, accum_out=st1.t[:, 1:2]),
                         reads=[(h2, None), (st1, None)], writes=[(sqv, None), (st1, 1)])
                    yield
                    S.op("act", lambda e: e.activation(sqv.t[:, :], h2.t[:, :], AF.Square, bias=st1.t[:, 1:2], accum_out=st1.t[:, 2:3]),
                         reads=[(h2, None), (st1, 1)], writes=[(sqv, None), (st1, 2)])
                    yield
                    S.op("act", lambda e: e.activation(st1.t[:, 3:4], st1.t[:, 2:3], AF.Sqrt, bias=1e-5, scale=1.0 / 1024), reads=[(st1, 2)], writes=[(st1, 3)])
                    yield
                    S.op("dve", lambda e: e.reciprocal(st1.t[:, 3:4], st1.t[:, 3:4]), reads=[(st1, 3)], writes=[(st1, 3)])
                    yield
                    S.op("dve", lambda e: e.tensor_scalar(yb.t[:, :], h2.t[:, :], st1.t[:, 1:2], st1.t[:, 3:4], op0=ALU.add, op1=ALU.mult),
                         reads=[(h2, None), (st1, None)], writes=[(yb, None)])
                    yield
                    S.op("dve", lambda e: e.tensor_tensor(yb.t[:, :], yb.t[:, :], LNG.t[:, :], ALU.mult), reads=[(yb, None), (LNG, None)], writes=[(yb, None)])
                    yield
                    S.op("dve", lambda e: e.tensor_tensor(yb.t[:, :], yb.t[:, :], LNB.t[:, :], ALU.add), reads=[(yb, None), (LNB, None)], writes=[(yb, None)])
                    yield
                    S.op("sp", lambda e: e.dma_start(out=K.out.ap()[isl, :], in_=yb.t[:, :]), reads=[(yb, None)])
                return gen

            run_interleaved([tile_gen(i) for i in range(NT)], W2)
            S.emit()
```
